# Optimizing a Trainium2 kernel written in Bass

```python
import math
import jax
import jax.numpy as jnp
from jax import lax
import numpy as np

D_MODEL = 1024
BATCH = 8
SEQ = 2048
DEPTH = 4
DEC_BATCH = 128
DEC_SEQ = 4
PAST_LEN = 16384
PAGE_SIZE = 128

N_META = 16
N_AB = (DEPTH + 1) // 2
N_C = DEPTH // 2

GLA_HEADS = 4
GLA_DK = D_MODEL // 4
GLA_DV = D_MODEL // 2
GLA_HK = GLA_DK // GLA_HEADS
GLA_HV = GLA_DV // GLA_HEADS
GLA_GATE_RANK = 16
GLA_GATE_NORM = 16.0
GLA_CHUNK = 16
GLA_COLS = 2 * GLA_DK + 2 * GLA_DV + GLA_GATE_RANK

RWKV_DIM = D_MODEL // 2
RWKV_HEAD = 64
RWKV_HEADS = RWKV_DIM // RWKV_HEAD
RWKV_DECAY_RANK = 64
RWKV_A_RANK = 64
RWKV_GATE_RANK = 128
RWKV_COLS = 3 * RWKV_DIM + RWKV_DECAY_RANK + RWKV_A_RANK + RWKV_GATE_RANK
RWKV_GN_EPS = 64e-5

AB_COLS = GLA_COLS + RWKV_COLS
AB_OUT = GLA_DV + RWKV_DIM

GDN_HEADS = 8
GDN_HEAD = 128
GDN_DIM = GDN_HEADS * GDN_HEAD
GDN_CONV = 4
GDN_CHUNK = 64
C_COLS = 4 * GDN_DIM + 2 * GDN_HEADS

D_FF = 2816
FFN_CONV = 3

LN_EPS = 1e-5
NORM_EPS = 1e-6
DEEPNORM_ALPHA = (2.0 * DEPTH) ** 0.25
DEEPNORM_BETA = (8.0 * DEPTH) ** -0.25

kernel_name = 'hybrid_gla_rwkv7_gdn_convffn_step'


def _split(t, sizes):
    offs, acc = [], 0
    for s in sizes[:-1]:
        acc += s
        offs.append(acc)
    return jnp.split(t, offs, axis=-1)


def _layer_norm(x, g, b):
    xf = x.astype(jnp.float32)
    mu = jnp.mean(xf, axis=-1, keepdims=True)
    var = jnp.mean(jnp.square(xf - mu), axis=-1, keepdims=True)
    y = (xf - mu) * lax.rsqrt(var + LN_EPS) * g.astype(jnp.float32) + b.astype(jnp.float32)
    return y.astype(x.dtype)


def _rms_norm(x, g):
    xf = x.astype(jnp.float32)
    return xf * lax.rsqrt(jnp.mean(jnp.square(xf), axis=-1, keepdims=True) + NORM_EPS) * g.astype(jnp.float32)


def _l2_normalize(x):
    xf = x.astype(jnp.float32)
    return xf * lax.rsqrt(jnp.sum(xf * xf, axis=-1, keepdims=True) + NORM_EPS)


def _causal_dwconv(u, buf, w):
    width = w.shape[0]
    T = u.shape[1]
    full = jnp.concatenate([buf.astype(u.dtype), u], axis=1)
    out = sum(full[:, j:j + T] * w[j] for j in range(width))
    return out, full[:, T:]


def _to_chunks(t, chunk):
    B, T, H, d = t.shape
    return t.reshape(B, T // chunk, chunk, H, d).transpose(1, 0, 3, 2, 4)


def _from_chunks(t):
    n, B, H, L, d = t.shape
    return t.transpose(1, 0, 3, 2, 4).reshape(B, n * L, H, d)


def _gla_chunked(q, k, v, log_a, s0, chunk):
    f32 = jnp.float32
    qc, kc, vc, gc = (_to_chunks(t.astype(f32), chunk) for t in (q, k, v, log_a))
    causal = jnp.tril(jnp.ones((chunk, chunk), bool))

    def step(s, inp):
        qi, ki, vi, gi = inp
        b = jnp.cumsum(gi, axis=-2)
        qd = qi * jnp.exp(b)
        kd = ki * jnp.exp(-b)
        att = jnp.where(causal, jnp.einsum('bhld,bhmd->bhlm', qd, kd), 0.0)
        o = jnp.einsum('bhld,bhdv->bhlv', qd, s) + jnp.einsum('bhlm,bhmv->bhlv', att, vi)
        b_end = b[..., -1:, :]
        s = s * jnp.exp(b_end)[..., 0, :, None] + jnp.einsum('bhld,bhlv->bhdv', ki * jnp.exp(b_end - b), vi)
        return s, o

    s, o = lax.scan(step, s0.astype(f32), (qc, kc, vc, gc))
    return _from_chunks(o), s


def _gated_delta_chunked(q, k, v, beta, g, s0, chunk):
    f32 = jnp.float32
    dv = v.shape[-1]
    qc, kc, vc = (_to_chunks(t.astype(f32), chunk) for t in (q, k, v))
    bc = _to_chunks(beta.astype(f32)[..., None], chunk)[..., 0]
    gc = _to_chunks(g.astype(f32)[..., None], chunk)[..., 0]
    incl = jnp.tril(jnp.ones((chunk, chunk), bool))
    strict = jnp.tril(jnp.ones((chunk, chunk), bool), -1)
    eye = jnp.eye(chunk, dtype=f32)

    def step(s, inp):
        qi, ki, vi, bi, gi = inp
        gam = jnp.cumsum(gi, axis=-1)
        dec = jnp.exp(jnp.where(incl, gam[..., :, None] - gam[..., None, :], -jnp.inf))
        a_mat = jnp.where(strict, jnp.einsum('bhlk,bhmk->bhlm', ki, ki) * dec, 0.0) * bi[..., :, None] + eye
        rhs = jnp.concatenate([vi * bi[..., None], ki * (bi * jnp.exp(gam))[..., None]], axis=-1)
        sol = lax.linalg.triangular_solve(a_mat, rhs, left_side=True, lower=True, unit_diagonal=True)
        u, w = sol[..., :dv], sol[..., dv:]
        delta = u - jnp.einsum('bhlk,bhkv->bhlv', w, s)
        qk = jnp.einsum('bhlk,bhmk->bhlm', qi, ki) * dec
        o = jnp.einsum('bhlk,bhkv->bhlv', qi * jnp.exp(gam)[..., None], s) + jnp.einsum('bhlm,bhmv->bhlv', qk, delta)
        g_end = gam[..., -1:]
        s = s * jnp.exp(g_end)[..., None] + jnp.einsum('bhlk,bhlv->bhkv', ki * jnp.exp(g_end - gam)[..., None], delta)
        return s, o

    s, o = lax.scan(step, s0.astype(f32), (qc, kc, vc, bc, gc))
    return _from_chunks(o), s


def _rwkv7_scan(r, decay, k, v, a, b, s0):
    def step(s, inp):
        rt, wt, kt, vt, at, bt = inp
        sa = jnp.einsum('bhvk,bhk->bhv', s, at)
        s = s * wt[:, :, None, :] + sa[..., None] * bt[:, :, None, :] + vt[..., None] * kt[:, :, None, :]
        return s, jnp.einsum('bhvk,bhk->bhv', s, rt)

    xs = tuple(jnp.moveaxis(t, 1, 0) for t in (r, decay, k, v, a, b))
    s, y = lax.scan(step, s0.astype(jnp.float32), xs)
    return jnp.moveaxis(y, 0, 1), s


def _ab_mixer(x, x_last, s_gla, s_rwkv, p, i, gla_chunk):
    f32 = jnp.float32
    B, T, _ = x.shape
    w_in = p['w_in_ab'][i]
    z = x @ w_in
    z_gla, z_rwkv = z[..., :GLA_COLS], z[..., GLA_COLS:]
    q, k, v, g_lo, og = _split(z_gla, (GLA_DK, GLA_DK, GLA_DV, GLA_GATE_RANK, GLA_DV))
    log_a = jax.nn.log_sigmoid((g_lo @ p['gla_gate_w2'][i] + p['gla_gate_b'][i]).astype(f32)) / GLA_GATE_NORM
    heads_k = lambda t: t.astype(f32).reshape(B, T, GLA_HEADS, GLA_HK)
    o_gla, s_gla_new = _gla_chunked(heads_k(q) * GLA_HK ** -0.5, heads_k(k),
                                    v.astype(f32).reshape(B, T, GLA_HEADS, GLA_HV), heads_k(log_a),
                                    s_gla, gla_chunk)
    o_gla = _rms_norm(o_gla, p['gla_norm_g'][i].reshape(GLA_HEADS, GLA_HV)).reshape(B, T, GLA_DV) * jax.nn.silu(og.astype(f32))
    prev_first = x_last @ w_in[:, GLA_COLS:]
    z_prev = jnp.concatenate([prev_first[:, None].astype(z.dtype), z_rwkv[:, :-1]], axis=1)
    z_rwkv = z_rwkv + p['rwkv_mu'][i] * (z_prev - z_rwkv)
    r, kr, vr, w_lo, a_lo, g_lo2 = _split(z_rwkv.astype(f32), (RWKV_DIM, RWKV_DIM, RWKV_DIM, RWKV_DECAY_RANK, RWKV_A_RANK, RWKV_GATE_RANK))
    w_raw = p['rwkv_w0'][i] + jnp.tanh(w_lo) @ p['rwkv_w2'][i]
    decay = jnp.exp(-jnp.exp(-jax.nn.softplus(-w_raw) - 0.5))
    a_lr = jax.nn.sigmoid(p['rwkv_a0'][i] + a_lo @ p['rwkv_a2'][i])
    gate = jax.nn.sigmoid(g_lo2) @ p['rwkv_g2'][i]
    heads = lambda t: t.reshape(B, T, RWKV_HEADS, RWKV_HEAD)
    kk = _l2_normalize(heads(kr * p['rwkv_k_k'][i]))
    k_mod = kr * (1.0 + (a_lr - 1.0) * p['rwkv_k_a'][i])
    y, s_rwkv_new = _rwkv7_scan(heads(r), heads(decay), heads(k_mod), heads(vr), -kk, kk * heads(a_lr), s_rwkv)
    mu = jnp.mean(y, axis=-1, keepdims=True)
    var = jnp.mean(jnp.square(y - mu), axis=-1, keepdims=True)
    y = ((y - mu) * lax.rsqrt(var + RWKV_GN_EPS)).reshape(B, T, RWKV_DIM) * p['rwkv_ln_g'][i] + p['rwkv_ln_b'][i]
    bonus = jnp.sum(heads(r) * heads(k_mod) * p['rwkv_r_k'][i], axis=-1, keepdims=True) * heads(vr)
    y = (y + bonus.reshape(B, T, RWKV_DIM)) * gate
    mix = jnp.concatenate([o_gla, y], axis=-1).astype(x.dtype) @ p['w_out_ab'][i]
    return mix, s_gla_new, s_rwkv_new, x[:, -1]


def _c_mixer(x, conv_buf, s, p, i, segments):
    f32 = jnp.float32
    B, T, _ = x.shape
    z = x @ p['w_in_c'][i]
    qkv, zg, b_lo, a_lo = _split(z, (3 * GDN_DIM, GDN_DIM, GDN_HEADS, GDN_HEADS))
    qkv, new_buf = _causal_dwconv(qkv, conv_buf, p['gdn_conv_w'][i])
    qkv = jax.nn.silu(qkv.astype(f32))
    q, k, v = (t.reshape(B, T, GDN_HEADS, GDN_HEAD) for t in jnp.split(qkv, 3, axis=-1))
    q = _l2_normalize(q) * GDN_HEAD ** -0.5
    k = _l2_normalize(k)
    beta = jax.nn.sigmoid(b_lo.astype(f32))
    g = -jnp.exp(p['gdn_A_log'][i].astype(f32)) * jax.nn.softplus(a_lo.astype(f32) + p['gdn_dt_bias'][i])
    outs = []
    for start, end, chunk in segments:
        o, s = _gated_delta_chunked(q[:, start:end], k[:, start:end], v[:, start:end],
                                    beta[:, start:end], g[:, start:end], s, chunk)
        outs.append(o)
    o = jnp.concatenate(outs, axis=1)
    o = _rms_norm(o, p['gdn_norm_g'][i]).reshape(B, T, GDN_DIM) * jax.nn.silu(zg.astype(f32))
    return o.astype(x.dtype) @ p['w_out_c'][i], new_buf, s


def _conv_ffn(x, buf, p, l):
    h = x @ p['w_up'][l]
    gate, up = jnp.split(h, [D_FF], axis=-1)
    gate, new_buf = _causal_dwconv(gate, buf, p['ffn_conv_w'][l])
    return (jax.nn.silu(gate + p['ffn_conv_b'][l]) * up) @ p['w_down'][l], new_buf


def _trunk(x, st_gla, st_rwkv, st_shift, st_gdn, st_gdn_conv, st_ffn_conv, p, n_lead):
    T = x.shape[1]
    if n_lead > 0:
        segments = ((0, n_lead, n_lead), (n_lead, T, math.gcd(T - n_lead, GDN_CHUNK)))
    else:
        segments = ((0, T, math.gcd(T, GDN_CHUNK)),)
    gla_chunk = math.gcd(T, GLA_CHUNK)
    new_gla, new_rwkv, new_shift, new_gdn, new_gdn_conv, new_ffn_conv = [], [], [], [], [], []
    for l in range(DEPTH):
        i = l // 2
        if l % 2 == 0:
            mix, s_g, s_r, s_sh = _ab_mixer(x, st_shift[i], st_gla[i], st_rwkv[i], p, i, gla_chunk)
            new_gla.append(s_g.astype(st_gla.dtype))
            new_rwkv.append(s_r.astype(st_rwkv.dtype))
            new_shift.append(s_sh.astype(st_shift.dtype))
        else:
            mix, c_buf, s_d = _c_mixer(x, st_gdn_conv[i], st_gdn[i], p, i, segments)
            new_gdn_conv.append(c_buf.astype(st_gdn_conv.dtype))
            new_gdn.append(s_d.astype(st_gdn.dtype))
        x = _layer_norm(DEEPNORM_ALPHA * x + mix, p['ln_mix_g'][l], p['ln_mix_b'][l])
        f, f_buf = _conv_ffn(x, st_ffn_conv[l], p, l)
        new_ffn_conv.append(f_buf.astype(st_ffn_conv.dtype))
        x = _layer_norm(DEEPNORM_ALPHA * x + f, p['ln_ffn_g'][l], p['ln_ffn_b'][l])
    return (x, jnp.stack(new_gla), jnp.stack(new_rwkv), jnp.stack(new_shift),
            jnp.stack(new_gdn), jnp.stack(new_gdn_conv), jnp.stack(new_ffn_conv))


def setup_inputs(seed: int = 0) -> dict:
    key = jax.random.key(seed)
    ks = iter(jax.random.split(key, 64))
    f32 = jnp.float32
    nrm = lambda shape, std: std * jax.random.normal(next(ks), shape, f32)
    ones_n = lambda shape: 1.0 + nrm(shape, 0.02)
    beta = DEEPNORM_BETA
    ratio = jnp.linspace(0.0, 1.0, RWKV_DIM, dtype=f32)
    w0 = (-6.0 + 5.0 * ratio ** 1.5 + 0.5)[None] + nrm((N_AB, RWKV_DIM), 0.1)
    dt = jnp.exp(jax.random.uniform(next(ks), (N_C, GDN_HEADS), f32, math.log(1e-3), math.log(1e-1)))
    dt_bias = dt + jnp.log(-jnp.expm1(-dt))
    a_log = jnp.log(jax.random.uniform(next(ks), (N_C, GDN_HEADS), f32, 1.0, 16.0))
    return {
        'x_prompt': nrm((BATCH, SEQ, D_MODEL), 1.0),
        'x_sample': nrm((DEC_BATCH, DEC_SEQ, D_MODEL), 1.0),
        'state_gla': nrm((N_AB, DEC_BATCH, GLA_HEADS, GLA_HK, GLA_HV), 0.5),
        'state_rwkv': nrm((N_AB, DEC_BATCH, RWKV_HEADS, RWKV_HEAD, RWKV_HEAD), 0.5),
        'state_rwkv_shift': nrm((N_AB, DEC_BATCH, D_MODEL), 1.0),
        'state_gdn': nrm((N_C, DEC_BATCH, GDN_HEADS, GDN_HEAD, GDN_HEAD), 0.1),
        'state_gdn_conv': nrm((N_C, DEC_BATCH, GDN_CONV - 1, 3 * GDN_DIM), 1.0),
        'state_ffn_conv': nrm((DEPTH, DEC_BATCH, FFN_CONV - 1, D_FF), 1.0),
        'meta_tokens': nrm((N_META, D_MODEL), 1.0),
        'w_in_ab': nrm((N_AB, D_MODEL, AB_COLS), D_MODEL ** -0.5),
        'gla_gate_w2': nrm((N_AB, GLA_GATE_RANK, GLA_DK), GLA_GATE_RANK ** -0.5),
        'gla_gate_b': nrm((N_AB, GLA_DK), 0.1),
        'gla_norm_g': ones_n((N_AB, GLA_DV)),
        'rwkv_mu': jax.random.uniform(next(ks), (N_AB, RWKV_COLS), f32, 0.0, 1.0),
        'rwkv_w0': w0,
        'rwkv_w2': nrm((N_AB, RWKV_DECAY_RANK, RWKV_DIM), 0.1 * RWKV_DECAY_RANK ** -0.5),
        'rwkv_a0': nrm((N_AB, RWKV_DIM), 0.1),
        'rwkv_a2': nrm((N_AB, RWKV_A_RANK, RWKV_DIM), RWKV_A_RANK ** -0.5),
        'rwkv_g2': nrm((N_AB, RWKV_GATE_RANK, RWKV_DIM), RWKV_GATE_RANK ** -0.5),
        'rwkv_k_k': 0.85 + nrm((N_AB, RWKV_DIM), 0.02),
        'rwkv_k_a': ones_n((N_AB, RWKV_DIM)),
        'rwkv_r_k': nrm((N_AB, RWKV_HEADS, RWKV_HEAD), 0.1),
        'rwkv_ln_g': ones_n((N_AB, RWKV_DIM)),
        'rwkv_ln_b': nrm((N_AB, RWKV_DIM), 0.02),
        'w_out_ab': nrm((N_AB, AB_OUT, D_MODEL), AB_OUT ** -0.5 * beta),
        'w_in_c': nrm((N_C, D_MODEL, C_COLS), D_MODEL ** -0.5),
        'gdn_conv_w': nrm((N_C, GDN_CONV, 3 * GDN_DIM), GDN_CONV ** -0.5),
        'gdn_A_log': a_log,
        'gdn_dt_bias': dt_bias,
        'gdn_norm_g': ones_n((N_C, GDN_HEAD)),
        'w_out_c': nrm((N_C, GDN_DIM, D_MODEL), GDN_DIM ** -0.5 * beta),
        'w_up': nrm((DEPTH, D_MODEL, 2 * D_FF), D_MODEL ** -0.5),
        'ffn_conv_w': nrm((DEPTH, FFN_CONV, D_FF), FFN_CONV ** -0.5),
        'ffn_conv_b': nrm((DEPTH, D_FF), 0.02),
        'w_down': nrm((DEPTH, D_FF, D_MODEL), D_FF ** -0.5 * beta),
        'ln_mix_g': ones_n((DEPTH, D_MODEL)),
        'ln_mix_b': nrm((DEPTH, D_MODEL), 0.02),
        'ln_ffn_g': ones_n((DEPTH, D_MODEL)),
        'ln_ffn_b': nrm((DEPTH, D_MODEL), 0.02),
    }


def reference(x_prompt, x_sample, state_gla, state_rwkv, state_rwkv_shift, state_gdn, state_gdn_conv,
              state_ffn_conv, meta_tokens, w_in_ab, gla_gate_w2, gla_gate_b, gla_norm_g, rwkv_mu, rwkv_w0,
              rwkv_w2, rwkv_a0, rwkv_a2, rwkv_g2, rwkv_k_k, rwkv_k_a, rwkv_r_k, rwkv_ln_g, rwkv_ln_b,
              w_out_ab, w_in_c, gdn_conv_w, gdn_A_log, gdn_dt_bias, gdn_norm_g, w_out_c, w_up,
              ffn_conv_w, ffn_conv_b, w_down, ln_mix_g, ln_mix_b, ln_ffn_g, ln_ffn_b):
    p = {
        'w_in_ab': w_in_ab, 'gla_gate_w2': gla_gate_w2, 'gla_gate_b': gla_gate_b, 'gla_norm_g': gla_norm_g,
        'rwkv_mu': rwkv_mu, 'rwkv_w0': rwkv_w0, 'rwkv_w2': rwkv_w2, 'rwkv_a0': rwkv_a0, 'rwkv_a2': rwkv_a2,
        'rwkv_g2': rwkv_g2, 'rwkv_k_k': rwkv_k_k, 'rwkv_k_a': rwkv_k_a, 'rwkv_r_k': rwkv_r_k,
        'rwkv_ln_g': rwkv_ln_g, 'rwkv_ln_b': rwkv_ln_b, 'w_out_ab': w_out_ab,
        'w_in_c': w_in_c, 'gdn_conv_w': gdn_conv_w, 'gdn_A_log': gdn_A_log, 'gdn_dt_bias': gdn_dt_bias,
        'gdn_norm_g': gdn_norm_g, 'w_out_c': w_out_c,
        'w_up': w_up, 'ffn_conv_w': ffn_conv_w, 'ffn_conv_b': ffn_conv_b, 'w_down': w_down,
        'ln_mix_g': ln_mix_g, 'ln_mix_b': ln_mix_b, 'ln_ffn_g': ln_ffn_g, 'ln_ffn_b': ln_ffn_b,
    }
    B = x_prompt.shape[0]
    dt = x_prompt.dtype
    meta = jnp.broadcast_to(meta_tokens.astype(dt)[None], (B, N_META, D_MODEL))
    xp = jnp.concatenate([meta, x_prompt], axis=1)
    zeros_like_state = lambda s: jnp.zeros((s.shape[0], B) + s.shape[2:], dt)
    (hp, p_gla, p_rwkv, p_shift, p_gdn, p_gdn_conv, p_ffn_conv) = _trunk(
        xp, zeros_like_state(state_gla), zeros_like_state(state_rwkv), zeros_like_state(state_rwkv_shift),
        zeros_like_state(state_gdn), zeros_like_state(state_gdn_conv), zeros_like_state(state_ffn_conv),
        p, N_META)
    y_prompt = hp[:, N_META:]
    (y_sample, s_gla, s_rwkv, s_shift, s_gdn, s_gdn_conv, s_ffn_conv) = _trunk(
        x_sample, state_gla, state_rwkv, state_rwkv_shift, state_gdn, state_gdn_conv, state_ffn_conv, p, 0)
    return (y_prompt, y_sample, p_gla, p_rwkv, p_shift, p_gdn, p_gdn_conv, p_ffn_conv,
            s_gla, s_rwkv, s_shift, s_gdn, s_gdn_conv, s_ffn_conv)
```

```python
import math
from contextlib import ExitStack

import numpy as np
import concourse.bass as bass
import concourse.mybir as mybir
from concourse.bass_utils import run_bass_kernel_spmd

F32 = mybir.dt.float32
AF = mybir.ActivationFunctionType
ALU = mybir.AluOpType

D = 1024
DEPTH_FULL = 4
N_META = 16
GLA_COLS = 1552
AB_COLS = 3344
C_COLS = 4112
D_FF = 2816
ALPHA = (2.0 * DEPTH_FULL) ** 0.25
LN_EPS = 1e-5
NORM_EPS = 1e-6
GN_EPS = 64e-5
C0 = math.exp(-0.5)


class Buf:
    __slots__ = ("name", "t", "w", "r", "fresh")

    def __init__(self, name, t):
        self.name = name
        self.t = t
        self.w = None
        self.r = {}
        self.fresh = True

    def __getitem__(self, idx):
        return self.t[idx]


class K:
    ENG = ("pe", "act", "dve", "pool")

    def __init__(self, nc, es, n_dma_sems=8):
        self.nc = nc
        self.es = es
        self.engobj = {"pe": nc.tensor, "act": nc.scalar, "dve": nc.vector, "pool": nc.gpsimd, "sp": nc.sync}
        self.sem = {}
        self.cnt = {}
        for e in self.ENG:
            self.sem[e] = es.enter_context(nc.semaphore("sem_" + e))
            self.cnt[e] = 0
        self.dq = {}
        for q in ("sp", "pool"):
            keys = []
            for j in range(n_dma_sems):
                k = "d_%s_%d" % (q, j)
                self.sem[k] = es.enter_context(nc.semaphore(k))
                self.cnt[k] = 0
                keys.append(k)
            self.dq[q] = [keys, 0]
        self.known = {e: {} for e in ("pe", "act", "dve", "pool", "sp")}
        self.n_inst = 0
        self.n_wait = 0
        self._psum = []
        self._psum_i = 0
        self._uid = 0
        self._rot = {}

    def sbuf(self, name, shape, dtype=F32):
        self._uid += 1
        t = self.es.enter_context(self.nc.sbuf_tensor("%s_%d" % (name, self._uid), list(shape), dtype))
        return Buf(name, t)

    def alias(self, buf, n):
        return [Buf("%s_%d" % (buf.name, i), buf.t) for i in range(n)]

    def init_psum(self, n=8):
        for i in range(n):
            t = self.es.enter_context(self.nc.psum_tensor("psb%d" % i, [128, 512], F32))
            self._psum.append(Buf("ps%d" % i, t))

    def psum(self):
        b = self._psum[self._psum_i % len(self._psum)]
        self._psum_i += 1
        b.fresh = True
        return b

    def rot(self, name, shape, n, dtype=F32):
        if name not in self._rot:
            self._rot[name] = [[self.sbuf(name, shape, dtype) for _ in range(n)], 0]
        lst = self._rot[name]
        b = lst[0][lst[1] % len(lst[0])]
        lst[1] += 1
        return b

    def _deps(self, eng, reads, writes, is_dma=False):
        deps = {}

        def add(k, v, war):
            if k == eng and not is_dma:
                if eng == "pe":
                    return
            if deps.get(k, 0) < v:
                deps[k] = v

        for b in reads:
            if b.w is not None:
                add(b.w[0], b.w[1], False)
        for b in writes:
            if b.w is not None:
                add(b.w[0], b.w[1], False)
            for k, v in b.r.items():
                add(k, v, True)
        return deps

    def _wait(self, eng, deps):
        kn = self.known[eng]
        eo = self.engobj[eng]
        for k, v in deps.items():
            if kn.get(k, 0) < v:
                eo.wait_ge(self.sem[k], v)
                kn[k] = v
                self.n_wait += 1

    def op(self, eng, fn, reads=(), writes=()):
        deps = self._deps(eng, reads, writes)
        self._wait(eng, deps)
        inst = fn(self.engobj[eng])
        self.cnt[eng] += 1
        c = self.cnt[eng]
        inst.then_inc(self.sem[eng], 1)
        self.n_inst += 1
        for b in reads:
            if b.r.get(eng, 0) < c:
                b.r[eng] = c
        for b in writes:
            b.w = (eng, c)
            b.r = {}
        return inst

    def dma(self, q, out, in_, reads=(), writes=(), **kw):
        deps = self._deps(q, reads, writes, True)
        keys, i = self.dq[q]
        k = keys[i % len(keys)]
        self.dq[q][1] = i + 1
        if self.cnt[k] > 0:
            deps[k] = max(deps.get(k, 0), self.cnt[k])
        self._wait(q, deps)
        inst = self.engobj[q].dma_start(out=out, in_=in_, **kw)
        self.cnt[k] += 16
        c = self.cnt[k]
        inst.then_inc(self.sem[k], 16)
        self.n_inst += 1
        for b in reads:
            if b.r.get(k, 0) < c:
                b.r[k] = c
        for b in writes:
            b.w = (k, c)
            b.r = {}
        return inst

    def finish(self):
        deps = {}
        for q in self.dq:
            for k in self.dq[q][0]:
                if self.cnt[k] > 0:
                    deps[k] = self.cnt[k]
        for e in self.ENG:
            if self.cnt[e] > 0:
                deps[e] = self.cnt[e]
        self._wait("sp", deps)

    def mm(self, bank, out, lhsT, rhs, last, reads):
        st = bank.fresh
        bank.fresh = False
        return self.op("pe", lambda e: e.matmul(out, lhsT=lhsT, rhs=rhs, start=st, stop=last), reads, [bank])

    def tr(self, bank, out, in_, ident, reads):
        bank.fresh = False
        return self.op("pe", lambda e: e.transpose(out, in_, ident), reads, [bank])

    def act(self, out, in_, func, reads, writes, bias=None, scale=None):
        kw = {}
        if bias is not None:
            kw["bias"] = bias
        if scale is not None:
            kw["scale"] = scale
        return self.op("act", lambda e: e.activation(out=out, in_=in_, func=func, **kw), reads, writes)

    def tt(self, out, in0, in1, op, reads, writes, eng="dve"):
        return self.op(eng, lambda e: e.tensor_tensor(out=out, in0=in0, in1=in1, op=op), reads, writes)

    def ts(self, out, in0, s1, s2, op0, op1, reads, writes, eng="dve"):
        if op1 is None:
            return self.op(eng, lambda e: e.tensor_scalar(out=out, in0=in0, scalar1=s1, scalar2=None, op0=op0), reads, writes)
        return self.op(eng, lambda e: e.tensor_scalar(out=out, in0=in0, scalar1=s1, scalar2=s2, op0=op0, op1=op1), reads, writes)

    def stt(self, out, in0, scalar, in1, op0, op1, reads, writes):
        return self.op("dve", lambda e: e.scalar_tensor_tensor(out=out, in0=in0, scalar=scalar, in1=in1, op0=op0, op1=op1), reads, writes)

    def copy(self, out, in_, reads, writes, eng="dve"):
        if eng == "act":
            return self.op("act", lambda e: e.copy(out=out, in_=in_), reads, writes)
        return self.op(eng, lambda e: e.tensor_copy(out=out, in_=in_), reads, writes)

    def memset(self, out, val, writes, eng="pool"):
        return self.op(eng, lambda e: e.memset(out, val), (), writes)

    def recip(self, out, in_, reads, writes):
        return self.op("dve", lambda e: e.reciprocal(out=out, in_=in_), reads, writes)

    def scan(self, out, data0, data1, reads, writes):
        return self.op("dve", lambda e: e.tensor_tensor_scan(out=out, data0=data0, data1=data1, initial=0.0,
                                                             op0=ALU.mult, op1=ALU.add), reads, writes)


C_ID, C_ONES, C_BO, C_IU, C_SU, C_SL, C_NSL, C_NSU = [i * 128 for i in range(8)]
C_RST = 8 * 128
C_SEL = 9 * 128
C_W = C_SEL


def make_consts():
    c = np.zeros((128, C_W), np.float32)
    i = np.arange(128)
    c[:, C_ID:C_ID + 128] = np.eye(128)
    c[:, C_ONES:C_ONES + 128] = 1.0
    bo = (i[:, None] // 64 == i[None, :] // 64).astype(np.float32)
    c[:, C_BO:C_BO + 128] = bo
    c[:, C_IU:C_IU + 128] = (i[:, None] <= i[None, :])
    c[:, C_SU:C_SU + 128] = (i[:, None] < i[None, :])
    c[:, C_SL:C_SL + 128] = (i[:, None] > i[None, :])
    c[:, C_NSL:C_NSL + 128] = -(i[:, None] > i[None, :]).astype(np.float32)
    c[:, C_NSU:C_NSU + 128] = -(i[:, None] < i[None, :]).astype(np.float32)
    c[:, C_RST:C_RST + 128] = (i[None, :] % 4 != 0)
    return c


def build(SEQ, NS, DEPTH):
    N_AB = (DEPTH + 1) // 2
    N_C = DEPTH // 2
    TP = N_META + SEQ
    nc = bass.Bass("TRN2", target_bir_lowering=False)

    def din(name, shape):
        return nc.dram_tensor(name, list(shape), F32, kind="ExternalInput").ap()

    def dout(name, shape):
        return nc.dram_tensor(name, list(shape), F32, kind="ExternalOutput").ap()

    xp = din("xp", [TP, D])
    xs = din("xs", [NS * 4, D])
    st_gla = din("st_gla", [N_AB, NS, 4, 64, 128])
    st_rwkv = din("st_rwkv", [N_AB, NS, 8, 64, 64])
    st_shift = din("st_shift", [N_AB, NS, D])
    st_gdn = din("st_gdn", [max(N_C, 1), NS, 8, 128, 128])
    st_gconv = din("st_gconv", [max(N_C, 1), NS, 3, 3072])
    st_fconv = din("st_fconv", [DEPTH, NS, 2, D_FF])
    consts = din("consts", [128, C_W])
    w_in_ab = din("w_in_ab", [N_AB, D, AB_COLS])
    gla_gate_w2 = din("gla_gate_w2", [N_AB, 16, 256])
    gla_gate_b = din("gla_gate_b", [N_AB, 256])
    gla_norm_g = din("gla_norm_g", [N_AB, 512])
    rwkv_mu = din("rwkv_mu", [N_AB, 1792])
    rwkv_w0 = din("rwkv_w0", [N_AB, 512])
    rwkv_w2 = din("rwkv_w2", [N_AB, 64, 512])
    rwkv_a0 = din("rwkv_a0", [N_AB, 512])
    rwkv_a2 = din("rwkv_a2", [N_AB, 64, 512])
    rwkv_g2 = din("rwkv_g2", [N_AB, 128, 512])
    rwkv_k_k = din("rwkv_k_k", [N_AB, 512])
    rwkv_k_a = din("rwkv_k_a", [N_AB, 512])
    rwkv_r_k = din("rwkv_r_k", [N_AB, 512])
    rwkv_ln_g = din("rwkv_ln_g", [N_AB, 512])
    rwkv_ln_b = din("rwkv_ln_b", [N_AB, 512])
    w_out_ab = din("w_out_ab", [N_AB, D, D])
    w_in_c = din("w_in_c", [max(N_C, 1), D, C_COLS])
    gdn_conv_w = din("gdn_conv_w", [max(N_C, 1), 4, 3072])
    gdn_A_log = din("gdn_A_log", [max(N_C, 1), 8])
    gdn_dt_bias = din("gdn_dt_bias", [max(N_C, 1), 8])
    gdn_norm_g = din("gdn_norm_g", [max(N_C, 1), 128])
    w_out_c = din("w_out_c", [max(N_C, 1), D, D])
    w_up = din("w_up", [DEPTH, D, 2 * D_FF])
    ffn_conv_w = din("ffn_conv_w", [DEPTH, 3, D_FF])
    ffn_conv_b = din("ffn_conv_b", [DEPTH, D_FF])
    w_down = din("w_down", [DEPTH, D_FF, D])
    ln_mix_g = din("ln_mix_g", [DEPTH, D])
    ln_mix_b = din("ln_mix_b", [DEPTH, D])
    ln_ffn_g = din("ln_ffn_g", [DEPTH, D])
    ln_ffn_b = din("ln_ffn_b", [DEPTH, D])

    y_p = dout("y_p", [SEQ, D])
    y_s = dout("y_s", [NS * 4, D])
    o_gla = [dout("p_gla", [N_AB, 1, 4, 64, 128]), dout("s_gla", [N_AB, NS, 4, 64, 128])]
    o_rwkv = [dout("p_rwkv", [N_AB, 1, 8, 64, 64]), dout("s_rwkv", [N_AB, NS, 8, 64, 64])]
    o_shift = [dout("p_shift", [N_AB, 1, D]), dout("s_shift", [N_AB, NS, D])]
    o_gdn = [dout("p_gdn", [max(N_C, 1), 1, 8, 128, 128]), dout("s_gdn", [max(N_C, 1), NS, 8, 128, 128])]
    o_gconv = [dout("p_gconv", [max(N_C, 1), 1, 3, 3072]), dout("s_gconv", [max(N_C, 1), NS, 3, 3072])]
    o_fconv = [dout("p_fconv", [DEPTH, 1, 2, D_FF]), dout("s_fconv", [DEPTH, NS, 2, D_FF])]

    es = ExitStack()
    k = K(nc, es)
    k.init_psum()
    NMAX = 128

    cst = k.sbuf("cst", [128, C_W])
    k.dma("sp", cst[:, :], consts[:, :], [], [cst])
    RC = [cst]

    def cm(off, r, c):
        return cst[0:r, off:off + c]

    ident = cst[:, C_ID:C_ID + 128]
    ones = cst[:, C_ONES:C_ONES + 128]
    bo64 = cst[:, C_BO:C_BO + 128]

    def load_fm(name, src, nch):
        b = k.sbuf(name, [128, nch])
        k.dma("pool", b[:, :], src.rearrange("(c p) -> p c", p=128), [], [b], allow_slow_non_contiguous=True)
        return b

    PAB = []
    for i in range(N_AB):
        p = {}
        p["gb"] = load_fm("gb", gla_gate_b[i], 2)
        p["gng"] = load_fm("gng", gla_norm_g[i], 4)
        p["mu"] = load_fm("mu", rwkv_mu[i], 14)
        for nm, src in (("w0", rwkv_w0), ("a0", rwkv_a0), ("kk", rwkv_k_k), ("ka", rwkv_k_a), ("rk", rwkv_r_k),
                        ("lg", rwkv_ln_g), ("lb", rwkv_ln_b)):
            p[nm] = load_fm(nm, src[i], 4)
        p["gw2"] = k.sbuf("gw2", [16, 256])
        k.dma("pool", p["gw2"][:, :], gla_gate_w2[i], [], [p["gw2"]])
        p["w2a2"] = k.sbuf("w2a2", [128, 512])
        k.dma("pool", p["w2a2"][0:64, :], rwkv_w2[i], [], [p["w2a2"]])
        k.dma("pool", p["w2a2"][64:128, :], rwkv_a2[i], [], [p["w2a2"]])
        p["g2"] = k.sbuf("g2", [128, 512])
        k.dma("pool", p["g2"][:, :], rwkv_g2[i], [], [p["g2"]])
        p["ngb"] = k.sbuf("ngb", [128, 2])
        k.ts(p["ngb"][:, :], p["gb"][:, :], -1.0, None, ALU.mult, None, [p["gb"]], [p["ngb"]])
        p["omka"] = k.sbuf("omka", [128, 4])
        k.ts(p["omka"][:, :], p["ka"][:, :], -1.0, 1.0, ALU.mult, ALU.add, [p["ka"]], [p["omka"]])
        PAB.append(p)
    PC = []
    for i in range(N_C):
        p = {}
        p["cw"] = k.sbuf("gcw", [128, 4, 24])
        for j in range(4):
            k.dma("pool", p["cw"][:, j, :], gdn_conv_w[i, j].rearrange("(c p) -> p c", p=128), [], [p["cw"]],
                  allow_slow_non_contiguous=True)
        p["alog"] = k.sbuf("alog", [8, 1])
        k.dma("pool", p["alog"][:, :], gdn_A_log[i].rearrange("(p o) -> p o", o=1), [], [p["alog"]])
        p["dtb"] = k.sbuf("dtb", [8, 1])
        k.dma("pool", p["dtb"][:, :], gdn_dt_bias[i].rearrange("(p o) -> p o", o=1), [], [p["dtb"]])
        p["nexpA"] = k.sbuf("nexpA", [8, 1])
        k.act(p["nexpA"][:, :], p["alog"][:, :], AF.Exp, [p["alog"]], [p["nexpA"]])
        k.ts(p["nexpA"][:, :], p["nexpA"][:, :], -1.0, None, ALU.mult, None, [p["nexpA"]], [p["nexpA"]])
        p["ng"] = k.sbuf("gdng", [128, 1])
        k.dma("pool", p["ng"][:, :], gdn_norm_g[i].rearrange("(p o) -> p o", o=1), [], [p["ng"]])
        p["wba"] = k.sbuf("wba", [128, 8, 16])
        k.dma("pool", p["wba"][:, :, :], w_in_c[i][:, 4096:4112].rearrange("(k p) c -> p k c", p=128), [], [p["wba"]])
        PC.append(p)
    PL = []
    for l in range(DEPTH):
        p = {}
        p["fcw"] = k.sbuf("fcw", [128, 3, 22])
        for j in range(3):
            k.dma("pool", p["fcw"][:, j, :], ffn_conv_w[l, j].rearrange("(c p) -> p c", p=128), [], [p["fcw"]],
                  allow_slow_non_contiguous=True)
        p["fcb"] = load_fm("fcb", ffn_conv_b[l], 22)
        p["lmg"] = load_fm("lmg", ln_mix_g[l], 8)
        p["lmb"] = load_fm("lmb", ln_mix_b[l], 8)
        p["lfg"] = load_fm("lfg", ln_ffn_g[l], 8)
        p["lfb"] = load_fm("lfb", ln_ffn_b[l], 8)
        PL.append(p)

    def zeros(name, shape):
        b = k.sbuf(name, shape)
        k.memset(b.t[:], 0.0, [b])
        return b

    P_gla = [zeros("pgla", [128, 2, 128]) for _ in range(N_AB)]
    P_rwkv = [zeros("prwkv", [128, 4, 128]) for _ in range(N_AB)]
    P_gdn = [zeros("pgdn", [128, 8, 128]) for _ in range(N_C)]
    C_xlast = [zeros("cxl", [128, 8, 1]) for _ in range(N_AB)]
    C_gconv = [zeros("cgc", [128, 24, 3]) for _ in range(N_C)]
    C_fconv = [zeros("cfc", [128, 22, 2]) for _ in range(DEPTH)]
    RAW = [zeros("srw_raw", [128, 4, 128]) for _ in range(1)]
    upad = zeros("upad", [128, 8, 128])
    vpad = zeros("vpad", [128, 8, 128])

    WT_ELEMS = 2816
    GL = k.sbuf("gla", [128, 16, NMAX])
    GL_T = k.alias(GL, 16)
    ARENA = k.sbuf("arena", [128, 48, NMAX])
    ARENA_T = k.alias(ARENA, 48)

    def proj(W, kc, c0, ncols, rhs_fn, rbufs, N, consume, gw):
        g0 = c0
        while g0 < c0 + ncols:
            w = min(gw, c0 + ncols - g0)
            wt = k.rot("wt", [128, WT_ELEMS], 2)
            wv = wt.t[:, 0:kc * w].rearrange("p (k c) -> p k c", c=w)
            k.dma("sp", wv, W[:, g0:g0 + w].rearrange("(k p) c -> p k c", p=128), [], [wt])
            j0 = 0
            while j0 < w:
                cw = min(128, w - j0)
                ps = k.psum()
                for kk in range(kc):
                    k.mm(ps, ps[0:cw, 0:N], wv[:, kk, j0:j0 + cw], rhs_fn(kk), kk == kc - 1, [wt] + rbufs)
                consume(g0 + j0, cw, ps)
                j0 += cw
            g0 += w

    def store_rows(src_fn, sbufs, nch, ncols, dst, view=None):
        stg = k.rot("stg", [128, 3072], 1)
        for c4 in range(0, nch, 4):
            ps = k.psum()
            n4 = min(4, nch - c4)
            for j in range(n4):
                if view is not None:
                    sl_ = 8 + j % 2
                    k.copy(GL[:, sl_, 0:ncols].rearrange("p (a b) -> p a b", b=view[1]), src_fn(c4 + j), sbufs, [GL_T[sl_]], eng="pool")
                    k.tr(ps, ps[0:ncols, j * 128:(j + 1) * 128], GL[:, sl_, 0:ncols], ident, [GL_T[sl_]] + RC)
                    continue
                k.tr(ps, ps[0:ncols, j * 128:(j + 1) * 128], src_fn(c4 + j), ident, sbufs + RC)
            k.copy(stg[0:ncols, c4 * 128:(c4 + n4) * 128], ps[0:ncols, 0:n4 * 128], [ps], [stg],
                   eng="act" if (c4 // 4) % 2 else "dve")
        k.dma("pool", dst, stg[0:ncols, 0:nch * 128], [stg], [])

    def layer_norm(tb, N, g, b, xo):
        sq = k.rot("mixT", [128, 8, NMAX], 1)
        k.act(sq[:, :, 0:N], tb[:, :, 0:N], AF.Square, [tb], [sq])
        ps1 = k.psum()
        for m in range(8):
            k.mm(ps1, ps1[:, 0:N], ones, tb[:, m, 0:N], m == 7, [tb] + RC)
        ps2 = k.psum()
        for m in range(8):
            k.mm(ps2, ps2[:, 0:N], ones, sq[:, m, 0:N], m == 7, [sq] + RC)
        st = k.rot("pc_st", [128, 4, NMAX], 1)
        mean, var, rstd = st[:, 0, 0:N], st[:, 1, 0:N], st[:, 2, 0:N]
        k.ts(mean, ps1[:, 0:N], 1.0 / D, None, ALU.mult, None, [ps1], [st])
        k.tt(st[:, 3, 0:N], mean, mean, ALU.mult, [st], [st])
        k.stt(var, ps2[:, 0:N], 1.0 / D, st[:, 3, 0:N], ALU.mult, ALU.subtract, [ps2, st], [st])
        k.ts(var, var, LN_EPS, None, ALU.add, None, [st], [st])
        k.act(var, var, AF.Sqrt, [st], [st])
        k.recip(rstd, var, [st], [st])
        for m in range(8):
            k.tt(sq[:, m, 0:N], tb[:, m, 0:N], mean, ALU.subtract, [tb, st], [sq])
            k.tt(sq[:, m, 0:N], sq[:, m, 0:N], rstd, ALU.mult, [sq, st], [sq])
            k.ts(xo[:, m, 0:N], sq[:, m, 0:N], g[:, m:m + 1], b[:, m:m + 1], ALU.mult, ALU.add, [sq, g, b], [xo])

    def tri_solve(Pp, Pm, pbuf, R0, rbuf, L, w, Ls_nil, final_out, final_bufs):
        nlev = 0
        while (1 << nlev) < Ls_nil:
            nlev += 1
        cur_p, cur_m, cur_r, cur_rb = Pp, Pm, R0, rbuf
        cur_pb = pbuf
        if nlev == 0:
            k.copy(final_out, cur_r, [cur_rb], final_bufs)
            return
        for j in range(nlev):
            ps = k.psum()
            k.mm(ps, ps[0:L, 0:w], cur_p, cur_r, True, [cur_pb, cur_rb])
            if j == nlev - 1:
                k.tt(final_out, ps[0:L, 0:w], cur_r, ALU.add, [ps, cur_rb], final_bufs)
                break
            nr = k.rot("solr", [128, 128], 2)
            k.tt(nr[0:L, 0:w], ps[0:L, 0:w], cur_r, ALU.add, [ps, cur_rb], [nr])
            psq = k.psum()
            k.mm(psq, psq[0:L, 0:L], cur_m, cur_p, False, [cur_pb])
            k.mm(psq, psq[0:L, 128:128 + L], cur_p, cur_m, True, [cur_pb])
            npb = k.rot("solp", [128, 256], 2)
            k.copy(npb[0:L, 0:L], psq[0:L, 0:L], [psq], [npb], eng="act")
            k.copy(npb[0:L, 128:128 + L], psq[0:L, 128:128 + L], [psq], [npb], eng="act")
            cur_p, cur_m, cur_pb = npb[0:L, 0:L], npb[0:L, 128:128 + L], npb
            cur_r, cur_rb = nr[0:L, 0:w], nr

    def ab_mixer(i, tile, xT, mixT):
        kind, nseq, Ls, N = tile["kind"], tile["nseq"], tile["Ls"], tile["N"]
        P = PAB[i]
        Ne = nseq * (Ls + 1)
        xe = k.rot("xe", [128, 8, NMAX + 16], 1)
        xev = xe.t[:, :, 0:Ne].rearrange("p c (s t) -> p c s t", t=Ls + 1)
        if kind == "p":
            k.copy(xev[:, :, 0, 0:1], C_xlast[i][:, :, :], [C_xlast[i]], [xe], eng="pool")
        else:
            xl = k.rot("stg", [128, 3072], 1)
            k.dma("pool", xl[0:nseq, 0:D], st_shift[i], [], [xl])
            for c4 in (0, 4):
                ps = k.psum()
                for j in range(4):
                    k.tr(ps, ps[:, j * nseq:(j + 1) * nseq], xl[0:nseq, (c4 + j) * 128:(c4 + j + 1) * 128],
                         ident[0:nseq, 0:nseq], [xl] + RC)
                k.copy(xev[:, c4:c4 + 4, :, 0], ps[:, 0:4 * nseq].rearrange("p (c s) -> p c s", s=nseq), [ps], [xe])
        k.copy(xev[:, :, :, 1:Ls + 1], xT[:, :, 0:N].rearrange("p c (s t) -> p c s t", t=Ls), [xT], [xe], eng="pool")
        if kind == "p":
            k.copy(C_xlast[i][:, :, :], xT[:, :, N - 1:N], [xT, xe], [C_xlast[i]], eng="pool")
            if tile["last"]:
                store_rows(lambda c: C_xlast[i][:, c, :], [C_xlast[i]], 8, 1, o_shift[0][i])
        else:
            xlv = xT[:, :, 0:N].rearrange("p c (s t) -> p c s t", t=Ls)
            store_rows(lambda c: xlv[:, c, :, Ls - 1], [xT], 8, nseq, o_shift[1][i])

        z = k.rot("z", [128, 33, NMAX + 4], 1)
        zc = tile["zc"]

        def cons(c_abs, cw, ps):
            if c_abs < 1024:
                ci = c_abs // 128
            elif c_abs == 1024:
                ci = 8
            else:
                ci = 9 + (c_abs - 1040) // 128
            k.copy(z[0:cw, ci, 0:Ne], ps[0:cw, 0:Ne], [ps], [zc[ci]], eng="act")

        W = w_in_ab[i]
        proj(W, 8, 0, 1024, lambda kk: xe[:, kk, 0:Ne], [xe], Ne, cons, 256)
        proj(W, 8, 1024, 16, lambda kk: xe[:, kk, 0:Ne], [xe], Ne, cons, 256)
        proj(W, 8, 1040, AB_COLS - 1040, lambda kk: xe[:, kk, 0:Ne], [xe], Ne, cons, 256)

        def zv(ci, rows=slice(0, 128)):
            return z.t[rows, ci, 0:Ne].rearrange("p (s t) -> p s t", t=Ls + 1)[:, :, 1:Ls + 1]

        def zp(ci, rows=slice(0, 128)):
            return z.t[rows, ci, 0:Ne].rearrange("p (s t) -> p s t", t=Ls + 1)[:, :, 0:Ls]

        def v3(ap):
            return ap.rearrange("p (s t) -> p s t", t=Ls)

        g = GL
        gA = GL_T
        for c in range(2):
            ps = k.psum()
            k.mm(ps, ps[:, 0:Ne], P["gw2"][0:16, c * 128:(c + 1) * 128], z[0:16, 8, 0:Ne], True, [P["gw2"], zc[8]])
            pv = ps[:, 0:Ne].rearrange("p (s t) -> p s t", t=Ls + 1)[:, :, 1:Ls + 1]
            k.act(v3(g[:, 10 + c, 0:N]), pv, AF.Exp, [ps, P["ngb"]], [gA[10 + c]], bias=P["ngb"][:, c:c + 1], scale=-1.0)
            k.act(g[:, 10 + c, 0:N], g[:, 10 + c, 0:N], AF.Ln, [gA[10 + c]], [gA[10 + c]], bias=1.0)
            rst = ones[:, 0:N] if kind == "p" else cst[:, C_RST:C_RST + N]
            k.scan(g[:, c, 0:N], rst, g[:, 10 + c, 0:N], [gA[10 + c]] + RC, [gA[c]])
            k.act(g[:, 2 + c, 0:N], g[:, c, 0:N], AF.Exp, [gA[c]], [gA[2 + c]], scale=-1.0 / 16)
            k.act(g[:, 4 + c, 0:N], g[:, c, 0:N], AF.Exp, [gA[c]], [gA[4 + c]], scale=1.0 / 16)
            k.stt(v3(g[:, 6 + c, 0:N]), zv(c), 0.125, v3(g[:, 2 + c, 0:N]), ALU.mult, ALU.mult, [zc[c], gA[2 + c]], [gA[6 + c]])
            k.tt(v3(g[:, 8 + c, 0:N]), zv(2 + c), v3(g[:, 4 + c, 0:N]), ALU.mult, [zc[2 + c], gA[4 + c]], [gA[8 + c]])
        for h in range(4):
            k.act(v3(g[:, 12 + h, 0:N]), zv(9 + h), AF.Silu, [zc[9 + h]], [gA[12 + h]])

        r = ARENA
        rA = ARENA_T
        zm = k.rot("zm", [128, 14, NMAX], 1)
        zmA = k.alias(zm, 14)
        for c in range(14):
            ci = 13 + c
            k.tt(v3(zm[:, c, 0:N]), zp(ci), zv(ci), ALU.subtract, [zc[ci]], [zmA[c]], eng="pool")
            k.stt(v3(zm[:, c, 0:N]), v3(zm[:, c, 0:N]), P["mu"][:, c:c + 1], zv(ci), ALU.mult, ALU.add,
                  [zmA[c], P["mu"], zc[ci]], [zmA[c]])
        k.act(r[0:64, 12, 0:N], zm[0:64, 12, 0:N], AF.Tanh, [zmA[12]], [rA[12]])
        k.copy(r[64:128, 12, 0:N], zm[64:128, 12, 0:N], [zmA[12]], [rA[12]])
        k.act(r[:, 13, 0:N], zm[:, 13, 0:N], AF.Sigmoid, [zmA[13]], [rA[13]])
        rst = ones[:, 0:N] if kind == "p" else cst[:, C_RST:C_RST + N]
        for c in range(4):
            cs = slice(c * 128, (c + 1) * 128)
            ps = k.psum()
            k.mm(ps, ps[:, 0:N], P["w2a2"][0:64, cs], r[0:64, 12, 0:N], True, [P["w2a2"], rA[12]])
            k.act(r[:, 14 + c, 0:N], ps[:, 0:N], AF.Sigmoid, [ps, P["w0"]], [rA[14 + c]], bias=P["w0"][:, c:c + 1])
            ps = k.psum()
            k.mm(ps, ps[:, 0:N], P["w2a2"][64:128, cs], r[64:128, 12, 0:N], True, [P["w2a2"], rA[12]])
            k.act(r[:, 18 + c, 0:N], ps[:, 0:N], AF.Sigmoid, [ps, P["a0"]], [rA[18 + c]], bias=P["a0"][:, c:c + 1])
            ps = k.psum()
            k.mm(ps, ps[:, 0:N], P["g2"][:, cs], r[:, 13, 0:N], True, [P["g2"], rA[13]])
            k.copy(r[:, 22 + c, 0:N], ps[:, 0:N], [ps], [rA[22 + c]], eng="act")
            k.ts(r[:, 26 + c, 0:N], zm[:, 4 + c, 0:N], P["kk"][:, c:c + 1], None, ALU.mult, None, [zmA[4 + c], P["kk"]], [rA[26 + c]])
            k.act(r[:, 38 + c, 0:N], r[:, 26 + c, 0:N], AF.Square, [rA[26 + c]], [rA[38 + c]])
            ps = k.psum()
            k.mm(ps, ps[:, 0:N], bo64, r[:, 38 + c, 0:N], True, [rA[38 + c]] + RC)
            k.act(r[:, 38 + c, 0:N], ps[:, 0:N], AF.Sqrt, [ps], [rA[38 + c]], bias=NORM_EPS)
            k.recip(r[:, 38 + c, 0:N], r[:, 38 + c, 0:N], [rA[38 + c]], [rA[38 + c]])
            k.tt(r[:, 26 + c, 0:N], r[:, 26 + c, 0:N], r[:, 38 + c, 0:N], ALU.mult, [rA[26 + c], rA[38 + c]], [rA[26 + c]])
            k.ts(r[:, 30 + c, 0:N], r[:, 18 + c, 0:N], P["ka"][:, c:c + 1], P["omka"][:, c:c + 1], ALU.mult, ALU.add,
                 [rA[18 + c], P["ka"], P["omka"]], [rA[30 + c]])
            k.tt(r[:, 30 + c, 0:N], r[:, 30 + c, 0:N], zm[:, 4 + c, 0:N], ALU.mult, [rA[30 + c], zmA[4 + c]], [rA[30 + c]])
            k.tt(r[:, 42 + c, 0:N], r[:, 26 + c, 0:N], r[:, 18 + c, 0:N], ALU.mult, [rA[26 + c], rA[18 + c]], [rA[42 + c]])
            k.scan(r[:, 34 + c, 0:N], rst, r[:, 14 + c, 0:N], [rA[14 + c]] + RC, [rA[34 + c]])
        ar = k.rot("ar", [128, 4, 2, NMAX], 1)
        bk = k.rot("bk", [128, 4, 2, NMAX], 1)
        arA = k.alias(ar, 4)
        bkA = k.alias(bk, 4)
        for c in range(4):
            k.act(r[:, 38 + c, 0:N], r[:, 34 + c, 0:N], AF.Exp, [rA[34 + c]], [rA[38 + c]], scale=-C0)
            k.tt(ar[:, c, 1, 0:N], zm[:, c, 0:N], r[:, 38 + c, 0:N], ALU.mult, [zmA[c], rA[38 + c]], [arA[c]])
            k.tt(r[:, 38 + c, 0:N], r[:, 34 + c, 0:N], r[:, 14 + c, 0:N], ALU.subtract, [rA[34 + c], rA[14 + c]], [rA[38 + c]])
            k.act(r[:, 38 + c, 0:N], r[:, 38 + c, 0:N], AF.Exp, [rA[38 + c]], [rA[38 + c]], scale=-C0)
            k.stt(ar[:, c, 0, 0:N], r[:, 26 + c, 0:N], -1.0, r[:, 38 + c, 0:N], ALU.mult, ALU.mult, [rA[26 + c], rA[38 + c]], [arA[c]])
            k.act(r[:, 38 + c, 0:N], r[:, 34 + c, 0:N], AF.Exp, [rA[34 + c]], [rA[38 + c]], scale=C0)
            k.tt(bk[:, c, 0, 0:N], r[:, 42 + c, 0:N], r[:, 38 + c, 0:N], ALU.mult, [rA[42 + c], rA[38 + c]], [bkA[c]])
            k.tt(bk[:, c, 1, 0:N], r[:, 30 + c, 0:N], r[:, 38 + c, 0:N], ALU.mult, [rA[30 + c], rA[38 + c]], [bkA[c]])

        oT = k.rot("oT", [128, 8, NMAX], 1)
        oA = k.alias(oT, 8)

        for s in range(nseq):
            c0 = s * Ls
            L = Ls
            cols = slice(c0, c0 + L)
            last = c0 + L - 1
            if kind == "p":
                Mg, Mr = P_gla[i], P_rwkv[i]
            else:
                Mg = k.rot("sgla", [128, 2, 128], 2)
                k.dma("sp", Mg[:, :, :], st_gla[i, s].rearrange("(c two) kk v -> (two kk) c v", two=2), [], [Mg])
                raw = RAW[0]
                for two in range(2):
                    k.dma("sp", raw[two * 64:(two + 1) * 64, :, two * 64:(two + 1) * 64],
                          st_rwkv[i, s].rearrange("(c two) v kk -> two v c kk", two=2)[two], [], [raw])
                Mr = k.rot("srw", [128, 4, 128], 1)
                ps = k.psum()
                for c in range(4):
                    k.tr(ps, ps[:, c * 128:(c + 1) * 128], raw[:, c, :], ident, [raw] + RC)
                k.copy(Mr[:, :, :], ps[:, 0:512].rearrange("p (c v) -> p c v", v=128), [ps], [Mr], eng="act")

            vt = k.rot("gvt", [128, 4, 128], 1)
            ps = k.psum()
            for h in range(4):
                k.tr(ps, ps[0:L, h * 128:(h + 1) * 128], z.t[:, 4 + h, s * (Ls + 1) + 1:s * (Ls + 1) + 1 + L], ident, [zc[4 + h]] + RC)
            k.copy(vt[0:L, :, :], ps[0:L, 0:512].rearrange("p (h v) -> p h v", v=128), [ps], [vt], eng="act")
            kt = k.rot("gkt", [128, 2, 128], 1)
            ps = k.psum()
            for c in range(2):
                k.ts(g[:, 10 + c, cols], g[:, c, cols], g[:, c, last:last + 1], None, ALU.subtract, None, [gA[c]], [gA[10 + c]])
                k.act(g[:, 10 + c, cols], g[:, 10 + c, cols], AF.Exp, [gA[10 + c]], [gA[10 + c]], scale=1.0 / 16)
                k.tt(g[:, 10 + c, cols], g[:, 10 + c, cols],
                     z.t[:, 2 + c, s * (Ls + 1) + 1:s * (Ls + 1) + 1 + L], ALU.mult, [gA[10 + c], zc[2 + c]], [gA[10 + c]])
                k.tr(ps, ps[0:L, c * 128:(c + 1) * 128], g[:, 10 + c, cols], ident, [gA[10 + c]] + RC)
            k.copy(kt[0:L, :, :], ps[0:L, 0:256].rearrange("p (c v) -> p c v", v=128), [ps], [kt])
            for h in range(4):
                c, two = h // 2, h % 2
                rows = slice(two * 64, two * 64 + 64)
                ps = k.psum()
                k.mm(ps, ps[0:L, 0:L], g[rows, 8 + c, cols], g[rows, 6 + c, cols], True, [gA[8 + c], gA[6 + c]])
                at = k.rot("gat", [128, 128], 2)
                k.tt(at[0:L, 0:L], ps[0:L, 0:L], cm(C_IU, L, L), ALU.mult, [ps] + RC, [at])
                po = k.psum()
                k.mm(po, po[:, 0:L], Mg[rows, c, :], g[rows, 6 + c, cols], False, [Mg, gA[6 + c]])
                k.mm(po, po[:, 0:L], vt[0:L, h, :], at[0:L, 0:L], True, [vt, at])
                k.copy(oT[:, h, cols], po[:, 0:L], [po], [oA[h]], eng="act")
                pu = k.psum()
                k.mm(pu, pu[:, 0:128], kt[0:L, c, :], vt[0:L, h, :], True, [kt, vt])
                k.stt(Mg[rows, c, :], Mg[rows, c, :], g[rows, 2 + c, last:last + 1], pu[rows, 0:128], ALU.mult, ALU.add,
                      [Mg, gA[2 + c], pu], [Mg])

            vtr = vpad
            ps = k.psum()
            for c in range(4):
                k.tr(ps, ps[0:L, c * 128:(c + 1) * 128], zm[:, 8 + c, cols], ident, [zmA[8 + c]] + RC)
            for c in range(4):
                for two in range(2):
                    h = 2 * c + two
                    k.copy(vpad[0:L, h, two * 64:two * 64 + 64], ps[0:L, c * 128 + two * 64:c * 128 + two * 64 + 64], [ps], [vpad],
                           eng="act" if two else "dve")
            ket = k.rot("rket", [128, 4, 2, 128], 1)
            for c in range(4):
                k.ts(r[:, 38 + c, cols], r[:, 34 + c, cols], r[:, 34 + c, last:last + 1], None, ALU.subtract, None, [rA[34 + c]], [rA[38 + c]])
                k.act(r[:, 38 + c, cols], r[:, 38 + c, cols], AF.Exp, [rA[38 + c]], [rA[38 + c]], scale=C0)
                tmp = k.rot("rtmp", [128, 2, 128], 1)
                k.tt(tmp[:, 0, 0:L], r[:, 30 + c, cols], r[:, 38 + c, cols], ALU.mult, [rA[30 + c], rA[38 + c]], [tmp])
                k.tt(tmp[:, 1, 0:L], r[:, 42 + c, cols], r[:, 38 + c, cols], ALU.mult, [rA[42 + c], rA[38 + c]], [tmp])
                ps = k.psum()
                k.tr(ps, ps[0:L, 0:128], tmp[:, 0, 0:L], ident, [tmp] + RC)
                k.tr(ps, ps[0:L, 128:256], tmp[:, 1, 0:L], ident, [tmp] + RC)
                k.copy(ket[0:L, c, :, :], ps[0:L, 0:256].rearrange("p (a v) -> p a v", v=128), [ps], [ket], eng="act")
            for c in range(4):
                NK = []
                for two in range(2):
                    rows = slice(two * 64, two * 64 + 64)
                    pb = k.psum()
                    k.mm(pb, pb[0:L, 0:L], bk[rows, c, 0, cols], ar[rows, c, 0, cols], False, [bkA[c], arA[c]])
                    k.mm(pb, pb[0:L, L:2 * L], bk[rows, c, 0, cols], ar[rows, c, 1, cols], True, [bkA[c], arA[c]])
                    pk = k.psum()
                    k.mm(pk, pk[0:L, 0:L], bk[rows, c, 1, cols], ar[rows, c, 0, cols], False, [bkA[c], arA[c]])
                    k.mm(pk, pk[0:L, L:2 * L], bk[rows, c, 1, cols], ar[rows, c, 1, cols], True, [bkA[c], arA[c]])
                    pn = k.psum()
                    k.mm(pn, pn[0:L, 0:L], ar[rows, c, 0, cols], bk[rows, c, 0, cols], True, [bkA[c], arA[c]])
                    nb = k.rot("rnb", [128, 5, 128], 2)
                    k.tt(nb[0:L, 0, 0:L], pb[0:L, 0:L], cm(C_SU, L, L), ALU.mult, [pb] + RC, [nb])
                    k.tt(nb[0:L, 1, 0:L], pb[0:L, L:2 * L], cm(C_IU, L, L), ALU.mult, [pb] + RC, [nb])
                    k.tt(nb[0:L, 2, 0:L], pk[0:L, 0:L], cm(C_SU, L, L), ALU.mult, [pk] + RC, [nb])
                    k.tt(nb[0:L, 3, 0:L], pk[0:L, L:2 * L], cm(C_IU, L, L), ALU.mult, [pk] + RC, [nb])
                    k.tt(nb[0:L, 4, 0:L], pn[0:L, 0:L], cm(C_SL, L, L), ALU.mult, [pn] + RC, [nb])
                    NK.append(nb)
                pr = k.psum()
                k.mm(pr, pr[0:L, 0:128], ar[:, c, 0, cols], Mr[:, c, :], False, [arA[c], Mr])
                for two in range(2):
                    k.mm(pr, pr[0:L, 0:128], NK[two][0:L, 2, 0:L], vpad[0:L, 2 * c + two, :], two == 1, [NK[two], vpad])
                r0 = k.rot("rr0", [128, 128], 1)
                k.copy(r0[0:L, :], pr[0:L, 0:128], [pr], [r0], eng="act")
                for two in range(2):
                    h = 2 * c + two
                    tri_solve(NK[two][0:L, 0, 0:L], NK[two][0:L, 4, 0:L], NK[two], r0[0:L, two * 64:two * 64 + 64], r0, L, 64, L,
                              upad[0:L, h, two * 64:two * 64 + 64], [upad])
                py = k.psum()
                k.mm(py, py[:, 0:L], Mr[:, c, :], ar[:, c, 1, cols], False, [Mr, arA[c]])
                for two in range(2):
                    h = 2 * c + two
                    k.mm(py, py[:, 0:L], upad[0:L, h, :], NK[two][0:L, 1, 0:L], False, [upad, NK[two]])
                    k.mm(py, py[:, 0:L], vpad[0:L, h, :], NK[two][0:L, 3, 0:L], two == 1, [vpad, NK[two]])
                k.copy(oT[:, 4 + c, cols], py[:, 0:L], [py], [oA[4 + c]], eng="act")
                pu = k.psum()
                k.mm(pu, pu[:, 0:128], ket[0:L, c, 0, :], vpad[0:L, 2 * c, :], False, [ket, vpad])
                k.mm(pu, pu[:, 0:128], ket[0:L, c, 0, :], vpad[0:L, 2 * c + 1, :], False, [ket, vpad])
                k.mm(pu, pu[:, 0:128], ket[0:L, c, 1, :], upad[0:L, 2 * c, :], False, [ket, upad])
                k.mm(pu, pu[:, 0:128], ket[0:L, c, 1, :], upad[0:L, 2 * c + 1, :], True, [ket, upad])
                ut = k.rot("rut", [128, 128], 1)
                k.tt(ut[:, :], pu[:, 0:128], bo64, ALU.mult, [pu] + RC, [ut])
                de = k.rot("rde", [128, 1], 2)
                k.act(de[:, :], r[:, 34 + c, last:last + 1], AF.Exp, [rA[34 + c]], [de], scale=-C0)
                k.stt(Mr[:, c, :], Mr[:, c, :], de[:, 0:1], ut[:, :], ALU.mult, ALU.add, [Mr, de, ut], [Mr])

            if kind == "s" or tile["last"]:
                gi = 0 if kind == "p" else 1
                si = 0 if kind == "p" else s
                k.dma("pool", o_gla[gi][i, si].rearrange("(c two) kk v -> (two kk) c v", two=2), Mg[:, :, :], [Mg], [])
                ps = k.psum()
                for c in range(4):
                    k.tr(ps, ps[:, c * 128:(c + 1) * 128], Mr[:, c, :], ident, [Mr] + RC)
                so = k.rot("srw_out", [128, 4, 128], 1)
                k.copy(so[:, :, :], ps[:, 0:512].rearrange("p (c v) -> p c v", v=128), [ps], [so])
                for two in range(2):
                    k.dma("pool", o_rwkv[gi][i, si].rearrange("(c two) v kk -> two v c kk", two=2)[two],
                          so[two * 64:(two + 1) * 64, :, two * 64:(two + 1) * 64], [so], [])

        for h in range(4):
            sq = k.rot("pc_sq", [128, NMAX], 2)
            k.act(sq[:, 0:N], oT[:, h, 0:N], AF.Square, [oA[h]], [sq])
            ps = k.psum()
            k.mm(ps, ps[:, 0:N], ones, sq[:, 0:N], True, [sq] + RC)
            k.act(sq[:, 0:N], ps[:, 0:N], AF.Sqrt, [ps], [sq], bias=NORM_EPS, scale=1.0 / 128)
            k.recip(sq[:, 0:N], sq[:, 0:N], [sq], [sq])
            k.tt(sq[:, 0:N], sq[:, 0:N], oT[:, h, 0:N], ALU.mult, [sq, oA[h]], [sq])
            k.stt(mixT[:, h, 0:N], sq[:, 0:N], P["gng"][:, h:h + 1], g[:, 12 + h, 0:N], ALU.mult, ALU.mult,
                  [sq, P["gng"], gA[12 + h]], [mixT])
        for c in range(4):
            sq = k.rot("pc_sq", [128, NMAX], 2)
            st = k.rot("pc_st", [128, 4, NMAX], 1)
            k.act(sq[:, 0:N], oT[:, 4 + c, 0:N], AF.Square, [oA[4 + c]], [sq])
            p1 = k.psum()
            k.mm(p1, p1[:, 0:N], bo64, oT[:, 4 + c, 0:N], True, [oA[4 + c]] + RC)
            p2 = k.psum()
            k.mm(p2, p2[:, 0:N], bo64, sq[:, 0:N], True, [sq] + RC)
            k.ts(st[:, 0, 0:N], p1[:, 0:N], 1.0 / 64, None, ALU.mult, None, [p1], [st])
            k.tt(st[:, 1, 0:N], st[:, 0, 0:N], st[:, 0, 0:N], ALU.mult, [st], [st])
            k.stt(st[:, 1, 0:N], p2[:, 0:N], 1.0 / 64, st[:, 1, 0:N], ALU.mult, ALU.subtract, [p2, st], [st])
            k.act(st[:, 1, 0:N], st[:, 1, 0:N], AF.Sqrt, [st], [st], bias=GN_EPS)
            k.recip(st[:, 1, 0:N], st[:, 1, 0:N], [st], [st])
            k.tt(st[:, 2, 0:N], oT[:, 4 + c, 0:N], st[:, 0, 0:N], ALU.subtract, [oA[4 + c], st], [st])
            k.tt(st[:, 2, 0:N], st[:, 2, 0:N], st[:, 1, 0:N], ALU.mult, [st], [st])
            k.ts(st[:, 2, 0:N], st[:, 2, 0:N], P["lg"][:, c:c + 1], P["lb"][:, c:c + 1], ALU.mult, ALU.add, [st, P["lg"], P["lb"]], [st])
            k.stt(sq[:, 0:N], zm[:, c, 0:N], P["rk"][:, c:c + 1], r[:, 30 + c, 0:N], ALU.mult, ALU.mult,
                  [zmA[c], P["rk"], rA[30 + c]], [sq])
            p3 = k.psum()
            k.mm(p3, p3[:, 0:N], bo64, sq[:, 0:N], True, [sq] + RC)
            k.tt(st[:, 3, 0:N], p3[:, 0:N], zm[:, 8 + c, 0:N], ALU.mult, [p3, zmA[8 + c]], [st])
            k.tt(st[:, 2, 0:N], st[:, 2, 0:N], st[:, 3, 0:N], ALU.add, [st], [st])
            k.tt(mixT[:, 4 + c, 0:N], st[:, 2, 0:N], r[:, 22 + c, 0:N], ALU.mult, [st, rA[22 + c]], [mixT])

    def c_mixer(i, tile, xT, mixT):
        kind, nseq, Ls, N = tile["kind"], tile["nseq"], tile["Ls"], tile["N"]
        P = PC[i]
        Ne = nseq * (Ls + 3)
        z = k.rot("z", [128, 33, NMAX + 4], 1)
        zc = tile["zc"]
        ze = lambda ci: z.t[:, ci, 0:Ne].rearrange("p (s t) -> p s t", t=Ls + 3)
        if kind == "p":
            k.copy(z.t[:, 0:24, 0:3], C_gconv[i][:, :, :], [C_gconv[i]], zc[0:24], eng="pool")
        else:
            cb = k.rot("stg", [128, 3072], 1)
            k.dma("pool", cb[0:nseq * 3, :], st_gconv[i].rearrange("s r c -> (s r) c"), [], [cb])
            for c4 in range(0, 24, 4):
                ps = k.psum()
                for j in range(4):
                    k.tr(ps, ps[:, j * 48:j * 48 + nseq * 3], cb[0:nseq * 3, (c4 + j) * 128:(c4 + j + 1) * 128],
                         ident[0:nseq * 3, 0:nseq * 3], [cb] + RC)
                for j in range(4):
                    k.copy(ze(c4 + j)[:, :, 0:3], ps[:, j * 48:j * 48 + nseq * 3].rearrange("p (s r) -> p s r", r=3), [ps], [zc[c4 + j]])

        def v3(ap):
            return ap.rearrange("p (s t) -> p s t", t=Ls)

        def cons(c_abs, cw, ps):
            ci = c_abs // 128
            if ci < 24:
                k.copy(ze(ci)[:, :, 3:Ls + 3], v3(ps[:, 0:N]), [ps], [zc[ci]], eng="act")
            else:
                k.act(z[:, ci, 0:N], ps[:, 0:N], AF.Silu, [ps], [zc[ci]])

        proj(w_in_c[i], 8, 0, 4096, lambda kk: xT[:, kk, 0:N], [xT], N, cons, 256)
        if kind == "p":
            k.copy(C_gconv[i][:, :, :], z.t[:, 0:24, Ls:Ls + 3], zc[0:24], [C_gconv[i]], eng="pool")
            if tile["last"]:
                store_rows(lambda c: C_gconv[i][:, c, :], [C_gconv[i]], 24, 3, o_gconv[0][i, 0])
        else:
            store_rows(lambda c: ze(c)[:, :, Ls:Ls + 3], zc[0:24], 24, nseq * 3, o_gconv[1][i].rearrange("s r c -> (s r) c"), view=(nseq, 3))
        q = ARENA
        qA = ARENA_T
        for ci in range(24):
            e = ze(ci)
            tmp = k.rot("ctmp", [128, NMAX], 2)
            k.ts(v3(tmp[:, 0:N]), e[:, :, 0:Ls], P["cw"][:, 0, ci:ci + 1], None, ALU.mult, None, [zc[ci], P["cw"]], [tmp])
            for j in (1, 2, 3):
                k.stt(v3(tmp[:, 0:N]), e[:, :, j:j + Ls], P["cw"][:, j, ci:ci + 1], v3(tmp[:, 0:N]), ALU.mult, ALU.add,
                      [zc[ci], P["cw"], tmp], [tmp])
            k.act(q[:, ci, 0:N], tmp[:, 0:N], AF.Silu, [tmp], [qA[ci]])
        for ci in range(16):
            sq = k.rot("pc_sq", [128, NMAX], 2)
            k.act(sq[:, 0:N], q[:, ci, 0:N], AF.Square, [qA[ci]], [sq])
            ps = k.psum()
            k.mm(ps, ps[:, 0:N], ones, sq[:, 0:N], True, [sq] + RC)
            k.act(sq[:, 0:N], ps[:, 0:N], AF.Sqrt, [ps], [sq], bias=NORM_EPS)
            k.recip(sq[:, 0:N], sq[:, 0:N], [sq], [sq])
            if ci < 8:
                k.stt(q[:, ci, 0:N], q[:, ci, 0:N], 128.0 ** -0.5, sq[:, 0:N], ALU.mult, ALU.mult, [qA[ci], sq], [qA[ci]])
            else:
                k.tt(q[:, ci, 0:N], q[:, ci, 0:N], sq[:, 0:N], ALU.mult, [qA[ci], sq], [qA[ci]])
        gb = k.rot("gb8", [8, 6, NMAX], 1)
        pb_ = k.psum()
        for kk in range(8):
            k.mm(pb_, pb_[0:8, 0:N], P["wba"][:, kk, 0:8], xT[:, kk, 0:N], kk == 7, [P["wba"], xT])
        k.act(gb[:, 1, 0:N], pb_[0:8, 0:N], AF.Sigmoid, [pb_], [gb])
        pa_ = k.psum()
        for kk in range(8):
            k.mm(pa_, pa_[0:8, 0:N], P["wba"][:, kk, 8:16], xT[:, kk, 0:N], kk == 7, [P["wba"], xT])
        k.act(gb[:, 5, 0:N], pa_[0:8, 0:N], AF.Exp, [pa_, P["dtb"]], [gb], bias=P["dtb"][:, 0:1])
        k.act(gb[:, 5, 0:N], gb[:, 5, 0:N], AF.Ln, [gb], [gb], bias=1.0)
        k.ts(gb[:, 4, 0:N], gb[:, 5, 0:N], P["nexpA"][:, 0:1], None, ALU.mult, None, [gb, P["nexpA"]], [gb])
        rst = ones[0:8, 0:N] if kind == "p" else cst[0:8, C_RST:C_RST + N]
        k.scan(gb[:, 0, 0:N], rst, gb[:, 4, 0:N], [gb] + RC, [gb])
        k.act(gb[:, 3, 0:N], gb[:, 0, 0:N], AF.Exp, [gb], [gb])
        k.ts(gb[:, 3, 0:N], gb[:, 3, 0:N], -1.0, None, ALU.mult, None, [gb], [gb])

        oT = k.rot("oT", [128, 8, NMAX], 1)
        oA = k.alias(oT, 8)
        for s in range(nseq):
            c0 = s * Ls
            L = Ls
            cols = slice(c0, c0 + L)
            last = c0 + L - 1
            if kind == "p":
                Ms = P_gdn[i]
                MsT = [Ms]
            else:
                Ms = GL
                MsT = GL_T[0:8]
                k.dma("sp", Ms[:, 0:8, :], st_gdn[i, s].rearrange("h kk v -> kk h v"), [], MsT)
            k.ts(gb[:, 2, cols], gb[:, 0, cols], gb[:, 0, last:last + 1], None, ALU.subtract, None, [gb], [gb])
            k.act(gb[:, 2, cols], gb[:, 2, cols], AF.Exp, [gb], [gb], scale=-1.0)
            ps = k.psum()
            for j in range(4):
                k.tr(ps, ps[0:L, j * 8:(j + 1) * 8], gb[0:8, j, cols], ident[0:8, 0:8], [gb] + RC)
            tm = k.rot("gtm", [128, 4, 8], 2)
            k.copy(tm[0:L, :, :], ps[0:L, 0:32].rearrange("p (a h) -> p a h", h=8), [ps], [tm])
            for h in range(8):
                gm = k.rot("ggm", [8, 2, NMAX], 1)
                k.ts(gm[0:8, :, 0:L], gb[0:8, 0:2, cols], ident[0:8, h:h + 1], None, ALU.mult, None, [gb] + RC, [gm])
                pg = k.psum()
                k.mm(pg, pg[:, 0:L], ones[0:8, :], gm[0:8, 0, 0:L], False, [gm] + RC)
                k.mm(pg, pg[:, 128:128 + L], ones[0:8, :], gm[0:8, 1, 0:L], True, [gm] + RC)
                pq = k.psum()
                k.mm(pq, pq[0:L, 0:L], q[:, 8 + h, cols], q[:, h, cols], False, [qA[8 + h], qA[h]])
                k.mm(pq, pq[0:L, 128:128 + L], q[:, 8 + h, cols], q[:, 8 + h, cols], True, [qA[8 + h]])
                w = k.rot("gw", [128, 6, 128], 1)
                gam_p = tm[0:L, 0, h:h + 1]
                k.ts(w[0:L, 0, 0:L], pg[0:L, 0:L], gam_p, 0.0, ALU.subtract, ALU.min, [pg, tm], [w])
                k.act(w[0:L, 0, 0:L], w[0:L, 0, 0:L], AF.Exp, [w], [w])
                k.ts(w[0:L, 1, 0:L], pg[0:L, 0:L], gam_p, 0.0, ALU.subtract, ALU.max, [pg, tm], [w])
                k.act(w[0:L, 1, 0:L], w[0:L, 1, 0:L], AF.Exp, [w], [w], scale=-1.0)
                k.tt(w[0:L, 4, 0:L], w[0:L, 0, 0:L], cm(C_IU, L, L), ALU.mult, [w] + RC, [w])
                k.tt(w[0:L, 4, 0:L], w[0:L, 4, 0:L], pq[0:L, 0:L], ALU.mult, [w, pq], [w])
                k.tt(w[0:L, 2, 0:L], w[0:L, 0, 0:L], pg[0:L, 128:128 + L], ALU.mult, [w, pg], [w])
                k.tt(w[0:L, 2, 0:L], w[0:L, 2, 0:L], cm(C_NSU, L, L), ALU.mult, [w] + RC, [w])
                k.tt(w[0:L, 2, 0:L], w[0:L, 2, 0:L], pq[0:L, 128:128 + L], ALU.mult, [w, pq], [w])
                k.stt(w[0:L, 3, 0:L], w[0:L, 1, 0:L], tm[0:L, 1, h:h + 1], cm(C_NSL, L, L), ALU.mult, ALU.mult, [w, tm] + RC, [w])
                k.tt(w[0:L, 3, 0:L], w[0:L, 3, 0:L], pq[0:L, 128:128 + L], ALU.mult, [w, pq], [w])
                k.act(w[:, 5, 0:L], pg[:, 0:L], AF.Exp, [pg], [w])
                qg = k.rot("gqg", [128, 128], 2)
                k.tt(qg[:, 0:L], q[:, h, cols], w[:, 5, 0:L], ALU.mult, [qA[h], w], [qg])
                pt = k.psum()
                k.tr(pt, pt[0:L, 0:128], q[:, 16 + h, cols], ident, [qA[16 + h]] + RC)
                k.tr(pt, pt[0:L, 128:256], q[:, 8 + h, cols], ident, [qA[8 + h]] + RC)
                vk = k.rot("gvk", [128, 2, 128], 2)
                k.copy(vk[0:L, 0, :], pt[0:L, 0:128], [pt], [vk], eng="act")
                k.ts(vk[0:L, 1, :], pt[0:L, 128:256], tm[0:L, 2, h:h + 1], None, ALU.mult, None, [pt, tm], [vk])
                pk = k.psum()
                k.mm(pk, pk[0:L, 0:128], q[:, 8 + h, cols], Ms[:, h, :], True, [qA[8 + h]] + MsT)
                rr = k.rot("grr", [128, 128], 2)
                k.stt(rr[0:L, :], pk[0:L, 0:128], tm[0:L, 3, h:h + 1], vk[0:L, 0, :], ALU.mult, ALU.add, [pk, tm, vk], [rr])
                k.ts(rr[0:L, :], rr[0:L, :], tm[0:L, 1, h:h + 1], None, ALU.mult, None, [rr, tm], [rr])
                dl = k.rot("gdl", [128, 128], 2)
                tri_solve(w[0:L, 2, 0:L], w[0:L, 3, 0:L], w, rr[0:L, :], rr, L, 128, L, dl[0:L, :], [dl])
                po = k.psum()
                k.mm(po, po[:, 0:L], Ms[:, h, :], qg[:, 0:L], False, MsT + [qg])
                k.mm(po, po[:, 0:L], dl[0:L, :], w[0:L, 4, 0:L], True, [dl, w])
                k.copy(oT[:, h, cols], po[:, 0:L], [po], [oA[h]], eng="act")
                pu = k.psum()
                k.mm(pu, pu[:, 0:128], vk[0:L, 1, :], dl[0:L, :], True, [vk, dl])
                k.stt(Ms[:, h, :], Ms[:, h, :], w[:, 5, L - 1:L], pu[:, 0:128], ALU.mult, ALU.add, MsT + [w, pu], MsT)
            if kind == "s" or tile["last"]:
                gi = 0 if kind == "p" else 1
                si = 0 if kind == "p" else s
                k.dma("pool", o_gdn[gi][i, si].rearrange("h kk v -> kk h v"), Ms[:, 0:8, :], MsT, [])
        for h in range(8):
            sq = k.rot("pc_sq", [128, NMAX], 2)
            k.act(sq[:, 0:N], oT[:, h, 0:N], AF.Square, [oA[h]], [sq])
            ps = k.psum()
            k.mm(ps, ps[:, 0:N], ones, sq[:, 0:N], True, [sq] + RC)
            k.act(sq[:, 0:N], ps[:, 0:N], AF.Sqrt, [ps], [sq], bias=NORM_EPS, scale=1.0 / 128)
            k.recip(sq[:, 0:N], sq[:, 0:N], [sq], [sq])
            k.tt(sq[:, 0:N], sq[:, 0:N], oT[:, h, 0:N], ALU.mult, [sq, oA[h]], [sq])
            k.stt(mixT[:, h, 0:N], sq[:, 0:N], P["ng"][:, 0:1], z[:, 24 + h, 0:N], ALU.mult, ALU.mult, [sq, P["ng"], zc[24 + h]], [mixT])

    def ffn(l, tile, xT, tb):
        kind, nseq, Ls, N = tile["kind"], tile["nseq"], tile["Ls"], tile["N"]
        P = PL[l]
        Ne = nseq * (Ls + 2)
        ge = k.rot("z", [128, 33, NMAX + 4], 1)
        zc = tile["zc"]
        gev = lambda ci: ge.t[:, ci, 0:Ne].rearrange("p (s t) -> p s t", t=Ls + 2)
        if kind == "p":
            k.copy(ge.t[:, 0:22, 0:2], C_fconv[l][:, :, :], [C_fconv[l]], zc[0:22], eng="pool")
        else:
            cb = k.rot("stg", [128, 3072], 1)
            k.dma("pool", cb[0:nseq * 2, 0:D_FF], st_fconv[l].rearrange("s r c -> (s r) c"), [], [cb])
            for c4 in range(0, 22, 4):
                n4 = min(4, 22 - c4)
                ps = k.psum()
                for j in range(n4):
                    k.tr(ps, ps[:, j * 32:j * 32 + nseq * 2], cb[0:nseq * 2, (c4 + j) * 128:(c4 + j + 1) * 128],
                         ident[0:nseq * 2, 0:nseq * 2], [cb] + RC)
                for j in range(n4):
                    k.copy(gev(c4 + j)[:, :, 0:2], ps[:, j * 32:j * 32 + nseq * 2].rearrange("p (s r) -> p s r", r=2), [ps], [zc[c4 + j]])
        aT = ARENA
        aA = ARENA_T[24:46]

        def v3(ap):
            return ap.rearrange("p (s t) -> p s t", t=Ls)

        def cons(c_abs, cw, ps):
            ci = c_abs // 128
            if ci < 22:
                k.copy(gev(ci)[:, :, 2:Ls + 2], v3(ps[:, 0:N]), [ps], [zc[ci]], eng="act")
                e = gev(ci)
                tmp = k.rot("ctmp", [128, NMAX], 2)
                k.ts(v3(tmp[:, 0:N]), e[:, :, 0:Ls], P["fcw"][:, 0, ci:ci + 1], None, ALU.mult, None, [zc[ci], P["fcw"]], [tmp])
                for j in (1, 2):
                    k.stt(v3(tmp[:, 0:N]), e[:, :, j:j + Ls], P["fcw"][:, j, ci:ci + 1], v3(tmp[:, 0:N]), ALU.mult, ALU.add,
                          [zc[ci], P["fcw"], tmp], [tmp])
                k.act(ARENA[:, 24 + ci, 0:N], tmp[:, 0:N], AF.Silu, [tmp, P["fcb"]], [aA[ci]], bias=P["fcb"][:, ci:ci + 1])
            else:
                j = ci - 22
                k.tt(ARENA[:, 24 + j, 0:N], ARENA[:, 24 + j, 0:N], ps[:, 0:N], ALU.mult, [aA[j], ps], [aA[j]])

        proj(w_up[l], 8, 0, 2 * D_FF, lambda kk: xT[:, kk, 0:N], [xT], N, cons, 256)
        if kind == "p":
            k.copy(C_fconv[l][:, :, :], ge.t[:, 0:22, Ls:Ls + 2], zc[0:22], [C_fconv[l]], eng="pool")
            if tile["last"]:
                store_rows(lambda c: C_fconv[l][:, c, :], [C_fconv[l]], 22, 2, o_fconv[0][l, 0])
        else:
            store_rows(lambda c: gev(c)[:, :, Ls:Ls + 2], zc[0:22], 22, nseq * 2, o_fconv[1][l].rearrange("s r c -> (s r) c"), view=(nseq, 2))

        def cons2(c_abs, cw, ps):
            m = c_abs // 128
            k.stt(tb[:, m, 0:N], xT[:, m, 0:N], ALPHA, ps[:, 0:N], ALU.mult, ALU.add, [xT, ps], [tb])

        proj(w_down[l], 22, 0, D, lambda kk: ARENA[:, 24 + kk, 0:N], aA, N, cons2, 128)

    tiles = [dict(kind="p", nseq=1, Ls=N_META, N=N_META, r0=0)]
    r0 = N_META
    while r0 < TP:
        tiles.append(dict(kind="p", nseq=1, Ls=128, N=128, r0=r0))
        r0 += 128
    for t in tiles:
        t["last"] = False
    tiles[-1]["last"] = True
    tiles.append(dict(kind="s", nseq=NS, Ls=4, N=NS * 4, r0=0, last=True))
    zbuf0 = None
    for tile in tiles:
        N = tile["N"]
        tile["first_rw"] = False
        src = xp if tile["kind"] == "p" else xs
        xtm = k.rot("stg", [128, 3072], 1)
        k.dma("sp", xtm[0:N, 0:D], src[tile["r0"]:tile["r0"] + N, :], [], [xtm])
        xT = k.rot("xT", [128, 8, NMAX], 2)
        for c4 in (0, 4):
            ps = k.psum()
            for j in range(4):
                k.tr(ps, ps[:, j * 128:j * 128 + N], xtm[0:N, (c4 + j) * 128:(c4 + j + 1) * 128], ident[0:N, 0:N], [xtm] + RC)
            k.copy(xT[:, c4:c4 + 4, 0:N], ps[:, 0:512].rearrange("p (c t) -> p c t", t=128)[:, :, 0:N], [ps], [xT])
        zb = k.rot("z", [128, 33, NMAX + 4], 1)
        if zbuf0 is None:
            zbuf0 = k.alias(zb, 33)
        tile["zc"] = zbuf0
        for l in range(DEPTH):
            i = l // 2
            mixT = k.rot("mixT", [128, 8, NMAX], 1)
            if l % 2 == 0:
                ab_mixer(i, tile, xT, mixT)
                Wo = w_out_ab[i]
            else:
                c_mixer(i, tile, xT, mixT)
                Wo = w_out_c[i]
            tb = k.rot("tb", [128, 8, NMAX], 1)

            def cons_o(c_abs, cw, ps, xT=xT, tb=tb):
                m = c_abs // 128
                k.stt(tb[:, m, 0:N], xT[:, m, 0:N], ALPHA, ps[:, 0:N], ALU.mult, ALU.add, [xT, ps], [tb])

            proj(Wo, 8, 0, D, lambda kk: mixT[:, kk, 0:N], [mixT], N, cons_o, 256)
            x2 = k.rot("xT", [128, 8, NMAX], 2)
            layer_norm(tb, N, PL[l]["lmg"], PL[l]["lmb"], x2)
            tb2 = k.rot("tb", [128, 8, NMAX], 1)
            ffn(l, tile, x2, tb2)
            x3 = k.rot("xT", [128, 8, NMAX], 2)
            layer_norm(tb2, N, PL[l]["lfg"], PL[l]["lfb"], x3)
            xT = x3
        if tile["kind"] == "s":
            store_rows(lambda c: xT[:, c, 0:N], [xT], 8, N, y_s[:, :])
        elif tile["r0"] >= N_META:
            store_rows(lambda c: xT[:, c, 0:N], [xT], 8, N, y_p[tile["r0"] - N_META:tile["r0"] - N_META + N, :])
    k.finish()
    stats = (k.n_inst, k.n_wait)
    es.close()
    return nc, stats


_W_NAMES = ["w_in_ab", "gla_gate_w2", "gla_gate_b", "gla_norm_g", "rwkv_mu", "rwkv_w0", "rwkv_w2", "rwkv_a0", "rwkv_a2",
            "rwkv_g2", "rwkv_k_k", "rwkv_k_a", "rwkv_r_k", "rwkv_ln_g", "rwkv_ln_b", "w_out_ab", "w_in_c", "gdn_conv_w",
            "gdn_A_log", "gdn_dt_bias", "gdn_norm_g", "w_out_c", "w_up", "ffn_conv_w", "ffn_conv_b", "w_down",
            "ln_mix_g", "ln_mix_b", "ln_ffn_g", "ln_ffn_b"]


def make_in_maps(inputs, n_cores, depth):
    f = lambda a: np.ascontiguousarray(np.asarray(a, dtype=np.float32))
    xprompt = f(inputs["x_prompt"])
    B, SEQ, _ = xprompt.shape
    xsample = f(inputs["x_sample"])
    NS = xsample.shape[0] // n_cores
    meta = f(inputs["meta_tokens"])
    consts = make_consts()
    shared = {"consts": consts}
    for nm in _W_NAMES:
        a = f(inputs[nm])
        if nm == "rwkv_r_k":
            a = a.reshape(a.shape[0], 512)
        shared[nm] = a
    maps = []
    for c in range(n_cores):
        m = dict(shared)
        m["xp"] = np.ascontiguousarray(np.concatenate([meta, xprompt[c]], axis=0))
        sl = slice(c * NS, (c + 1) * NS)
        m["xs"] = np.ascontiguousarray(xsample[sl].reshape(NS * 4, D))
        m["st_gla"] = f(inputs["state_gla"][:, sl])
        m["st_rwkv"] = f(inputs["state_rwkv"][:, sl])
        m["st_shift"] = f(inputs["state_rwkv_shift"][:, sl])
        m["st_gdn"] = f(inputs["state_gdn"][:, sl])
        m["st_gconv"] = f(inputs["state_gdn_conv"][:, sl])
        m["st_fconv"] = f(inputs["state_ffn_conv"][:, sl])
        maps.append(m)
    return maps, SEQ, NS


def gather(results, n_cores, NS):
    cat = lambda nm, ax: np.concatenate([np.asarray(r[nm]) for r in results], axis=ax)
    y_p = np.stack([np.asarray(r["y_p"]) for r in results], axis=0)
    y_s = cat("y_s", 0).reshape(n_cores * NS, 4, D)
    outs = [y_p, y_s]
    for g in ("p", "s"):
        for nm in ("gla", "rwkv", "shift", "gdn", "gconv", "fconv"):
            outs.append(cat("%s_%s" % (g, nm), 1))
    return tuple(np.ascontiguousarray(o, dtype=np.float32) for o in outs)


def kernel(**inputs):
    n_cores = 8
    maps, SEQ, NS = make_in_maps(inputs, n_cores, DEPTH_FULL)
    nc, _ = build(SEQ, NS, DEPTH_FULL)
    res = run_bass_kernel_spmd(nc, maps, core_ids=list(range(n_cores)))
    return gather(res.results, n_cores, NS)
```

```python
import math
from contextlib import ExitStack

import numpy as np
import concourse.bass as bass
import concourse.mybir as mybir
from concourse.bass_utils import run_bass_kernel_spmd

F32 = mybir.dt.float32
F32R = mybir.dt.float32r
AF = mybir.ActivationFunctionType
ALU = mybir.AluOpType

D = 1024
DEPTH_FULL = 4
N_META = 16
GLA_COLS = 1552
AB_COLS = 3344
C_COLS = 4112
D_FF = 2816
ALPHA = (2.0 * DEPTH_FULL) ** 0.25
LN_EPS = 1e-5
NORM_EPS = 1e-6
GN_EPS = 64e-5
C0 = math.exp(-0.5)


class Buf:
    __slots__ = ("name", "t", "w", "r", "fresh")

    def __init__(self, name, t):
        self.name = name
        self.t = t
        self.w = None
        self.r = {}
        self.fresh = True

    def __getitem__(self, idx):
        return self.t[idx]


class K:
    ENG = ("pe", "act", "dve", "pool")

    def __init__(self, nc, es, n_dma_sems=8):
        self.nc = nc
        self.es = es
        self.engobj = {"pe": nc.tensor, "act": nc.scalar, "dve": nc.vector, "pool": nc.gpsimd, "sp": nc.sync}
        self.sem = {}
        self.cnt = {}
        for e in self.ENG:
            self.sem[e] = es.enter_context(nc.semaphore("sem_" + e))
            self.cnt[e] = 0
        self.dq = {}
        for q in ("sp", "pool"):
            keys = []
            for j in range(n_dma_sems):
                k = "d_%s_%d" % (q, j)
                self.sem[k] = es.enter_context(nc.semaphore(k))
                self.cnt[k] = 0
                keys.append(k)
            self.dq[q] = [keys, 0]
        self.known = {e: {} for e in ("pe", "act", "dve", "pool", "sp")}
        self.n_inst = 0
        self.n_wait = 0
        self._psum = []
        self._psum_i = 0
        self._uid = 0
        self._rot = {}

    def sbuf(self, name, shape, dtype=F32):
        self._uid += 1
        t = self.es.enter_context(self.nc.sbuf_tensor("%s_%d" % (name, self._uid), list(shape), dtype))
        return Buf(name, t)

    def alias(self, buf, n):
        return [Buf("%s_%d" % (buf.name, i), buf.t) for i in range(n)]

    def init_psum(self, n=8):
        for i in range(n):
            t = self.es.enter_context(self.nc.psum_tensor("psb%d" % i, [128, 512], F32))
            self._psum.append(Buf("ps%d" % i, t))

    def psum(self):
        b = self._psum[self._psum_i % len(self._psum)]
        self._psum_i += 1
        b.fresh = True
        return b

    def rot(self, name, shape, n, dtype=F32):
        if name not in self._rot:
            self._rot[name] = [[self.sbuf(name, shape, dtype) for _ in range(n)], 0]
        lst = self._rot[name]
        b = lst[0][lst[1] % len(lst[0])]
        lst[1] += 1
        return b

    def _deps(self, eng, reads, writes, is_dma=False):
        deps = {}

        def add(k, v, war):
            if k == eng and not is_dma:
                if eng == "pe":
                    return
            if deps.get(k, 0) < v:
                deps[k] = v

        for b in reads:
            if b.w is not None:
                add(b.w[0], b.w[1], False)
        for b in writes:
            if b.w is not None:
                add(b.w[0], b.w[1], False)
            for k, v in b.r.items():
                add(k, v, True)
        return deps

    def _wait(self, eng, deps):
        kn = self.known[eng]
        eo = self.engobj[eng]
        for k, v in deps.items():
            if kn.get(k, 0) < v:
                eo.wait_ge(self.sem[k], v)
                kn[k] = v
                self.n_wait += 1

    def op(self, eng, fn, reads=(), writes=()):
        deps = self._deps(eng, reads, writes)
        self._wait(eng, deps)
        inst = fn(self.engobj[eng])
        self.cnt[eng] += 1
        c = self.cnt[eng]
        inst.then_inc(self.sem[eng], 1)
        self.n_inst += 1
        for b in reads:
            if b.r.get(eng, 0) < c:
                b.r[eng] = c
        for b in writes:
            b.w = (eng, c)
            b.r = {}
        return inst

    def dma(self, q, out, in_, reads=(), writes=(), **kw):
        deps = self._deps(q, reads, writes, True)
        keys, i = self.dq[q]
        k = keys[i % len(keys)]
        self.dq[q][1] = i + 1
        if self.cnt[k] > 0:
            deps[k] = max(deps.get(k, 0), self.cnt[k])
        self._wait(q, deps)
        inst = self.engobj[q].dma_start(out=out, in_=in_, **kw)
        self.cnt[k] += 16
        c = self.cnt[k]
        inst.then_inc(self.sem[k], 16)
        self.n_inst += 1
        for b in reads:
            if b.r.get(k, 0) < c:
                b.r[k] = c
        for b in writes:
            b.w = (k, c)
            b.r = {}
        return inst

    def finish(self):
        deps = {}
        for q in self.dq:
            for k in self.dq[q][0]:
                if self.cnt[k] > 0:
                    deps[k] = self.cnt[k]
        for e in self.ENG:
            if self.cnt[e] > 0:
                deps[e] = self.cnt[e]
        self._wait("sp", deps)

    def mm(self, bank, out, lhsT, rhs, last, reads):
        st = bank.fresh
        bank.fresh = False
        return self.op("pe", lambda e: e.matmul(out, lhsT=lhsT, rhs=rhs, start=st, stop=last), reads, [bank])

    def tr(self, bank, out, in_, ident, reads):
        bank.fresh = False
        return self.op("pe", lambda e: e.transpose(out, in_, ident), reads, [bank])

    def act(self, out, in_, func, reads, writes, bias=None, scale=None):
        kw = {}
        if bias is not None:
            kw["bias"] = bias
        if scale is not None:
            kw["scale"] = scale
        return self.op("act", lambda e: e.activation(out=out, in_=in_, func=func, **kw), reads, writes)

    def tt(self, out, in0, in1, op, reads, writes, eng="dve"):
        return self.op(eng, lambda e: e.tensor_tensor(out=out, in0=in0, in1=in1, op=op), reads, writes)

    def ts(self, out, in0, s1, s2, op0, op1, reads, writes, eng="dve"):
        if op1 is None:
            return self.op(eng, lambda e: e.tensor_scalar(out=out, in0=in0, scalar1=s1, scalar2=None, op0=op0), reads, writes)
        return self.op(eng, lambda e: e.tensor_scalar(out=out, in0=in0, scalar1=s1, scalar2=s2, op0=op0, op1=op1), reads, writes)

    def stt(self, out, in0, scalar, in1, op0, op1, reads, writes):
        return self.op("dve", lambda e: e.scalar_tensor_tensor(out=out, in0=in0, scalar=scalar, in1=in1, op0=op0, op1=op1), reads, writes)

    def copy(self, out, in_, reads, writes, eng="dve"):
        if eng == "act":
            return self.op("act", lambda e: e.copy(out=out, in_=in_), reads, writes)
        return self.op(eng, lambda e: e.tensor_copy(out=out, in_=in_), reads, writes)

    def memset(self, out, val, writes, eng="pool"):
        return self.op(eng, lambda e: e.memset(out, val), (), writes)

    def recip(self, out, in_, reads, writes):
        return self.op("dve", lambda e: e.reciprocal(out=out, in_=in_), reads, writes)

    def scan(self, out, data0, data1, reads, writes):
        return self.op("dve", lambda e: e.tensor_tensor_scan(out=out, data0=data0, data1=data1, initial=0.0,
                                                             op0=ALU.mult, op1=ALU.add), reads, writes)


C_ID, C_ONES, C_BO, C_IU, C_SU, C_SL, C_NSL, C_NSU = [i * 128 for i in range(8)]
C_RST = 8 * 128
C_SEL = 9 * 128
C_W = C_SEL


def make_consts():
    c = np.zeros((128, C_W), np.float32)
    i = np.arange(128)
    c[:, C_ID:C_ID + 128] = np.eye(128)
    c[:, C_ONES:C_ONES + 128] = 1.0
    bo = (i[:, None] // 64 == i[None, :] // 64).astype(np.float32)
    c[:, C_BO:C_BO + 128] = bo
    c[:, C_IU:C_IU + 128] = (i[:, None] <= i[None, :])
    c[:, C_SU:C_SU + 128] = (i[:, None] < i[None, :])
    c[:, C_SL:C_SL + 128] = (i[:, None] > i[None, :])
    c[:, C_NSL:C_NSL + 128] = -(i[:, None] > i[None, :]).astype(np.float32)
    c[:, C_NSU:C_NSU + 128] = -(i[:, None] < i[None, :]).astype(np.float32)
    c[:, C_RST:C_RST + 128] = (i[None, :] % 4 != 0)
    return c


def build(SEQ, NS, DEPTH):
    N_AB = (DEPTH + 1) // 2
    N_C = DEPTH // 2
    TP = N_META + SEQ
    nc = bass.Bass("TRN2", target_bir_lowering=False)
    nc.dge_precook = False

    def din(name, shape, dt=F32):
        return nc.dram_tensor(name, list(shape), dt, kind="ExternalInput").ap()

    def dout(name, shape):
        return nc.dram_tensor(name, list(shape), F32, kind="ExternalOutput").ap()

    xp = din("xp", [TP, D])
    xs = din("xs", [NS * 4, D])
    st_gla = din("st_gla", [N_AB, NS, 4, 64, 128])
    st_rwkv = din("st_rwkv", [N_AB, NS, 8, 64, 64])
    st_shift = din("st_shift", [N_AB, NS, D])
    st_gdn = din("st_gdn", [max(N_C, 1), NS, 8, 128, 128])
    st_gconv = din("st_gconv", [max(N_C, 1), NS, 3, 3072])
    st_fconv = din("st_fconv", [DEPTH, NS, 2, D_FF])
    consts = din("consts", [128, C_W])
    w_in_ab = din("w_in_ab", [N_AB, D, AB_COLS], F32R)
    gla_gate_w2 = din("gla_gate_w2", [N_AB, 16, 256])
    gla_gate_b = din("gla_gate_b", [N_AB, 256])
    gla_norm_g = din("gla_norm_g", [N_AB, 512])
    rwkv_mu = din("rwkv_mu", [N_AB, 1792])
    rwkv_w0 = din("rwkv_w0", [N_AB, 512])
    rwkv_w2 = din("rwkv_w2", [N_AB, 64, 512])
    rwkv_a0 = din("rwkv_a0", [N_AB, 512])
    rwkv_a2 = din("rwkv_a2", [N_AB, 64, 512])
    rwkv_g2 = din("rwkv_g2", [N_AB, 128, 512])
    rwkv_k_k = din("rwkv_k_k", [N_AB, 512])
    rwkv_k_a = din("rwkv_k_a", [N_AB, 512])
    rwkv_r_k = din("rwkv_r_k", [N_AB, 512])
    rwkv_ln_g = din("rwkv_ln_g", [N_AB, 512])
    rwkv_ln_b = din("rwkv_ln_b", [N_AB, 512])
    w_out_ab = din("w_out_ab", [N_AB, D, D], F32R)
    w_in_c = din("w_in_c", [max(N_C, 1), D, C_COLS], F32R)
    gdn_conv_w = din("gdn_conv_w", [max(N_C, 1), 4, 3072])
    gdn_A_log = din("gdn_A_log", [max(N_C, 1), 8])
    gdn_dt_bias = din("gdn_dt_bias", [max(N_C, 1), 8])
    gdn_norm_g = din("gdn_norm_g", [max(N_C, 1), 128])
    w_out_c = din("w_out_c", [max(N_C, 1), D, D], F32R)
    w_up = din("w_up", [DEPTH, D, 2 * D_FF], F32R)
    ffn_conv_w = din("ffn_conv_w", [DEPTH, 3, D_FF])
    ffn_conv_b = din("ffn_conv_b", [DEPTH, D_FF])
    w_down = din("w_down", [DEPTH, D_FF, D], F32R)
    ln_mix_g = din("ln_mix_g", [DEPTH, D])
    ln_mix_b = din("ln_mix_b", [DEPTH, D])
    ln_ffn_g = din("ln_ffn_g", [DEPTH, D])
    ln_ffn_b = din("ln_ffn_b", [DEPTH, D])

    y_p = dout("y_p", [SEQ, D])
    y_s = dout("y_s", [NS * 4, D])
    o_gla = [dout("p_gla", [N_AB, 1, 4, 64, 128]), dout("s_gla", [N_AB, NS, 4, 64, 128])]
    o_rwkv = [dout("p_rwkv", [N_AB, 1, 8, 64, 64]), dout("s_rwkv", [N_AB, NS, 8, 64, 64])]
    o_shift = [dout("p_shift", [N_AB, 1, D]), dout("s_shift", [N_AB, NS, D])]
    o_gdn = [dout("p_gdn", [max(N_C, 1), 1, 8, 128, 128]), dout("s_gdn", [max(N_C, 1), NS, 8, 128, 128])]
    o_gconv = [dout("p_gconv", [max(N_C, 1), 1, 3, 3072]), dout("s_gconv", [max(N_C, 1), NS, 3, 3072])]
    o_fconv = [dout("p_fconv", [DEPTH, 1, 2, D_FF]), dout("s_fconv", [DEPTH, NS, 2, D_FF])]

    es = ExitStack()
    k = K(nc, es)
    k.init_psum()
    NMAX = 128

    cst = k.sbuf("cst", [128, C_W])
    k.dma("sp", cst[:, :], consts[:, :], [], [cst])
    RC = [cst]

    def cm(off, r, c):
        return cst[0:r, off:off + c]

    ident = cst[:, C_ID:C_ID + 128]
    ones = cst[:, C_ONES:C_ONES + 128]
    bo64 = cst[:, C_BO:C_BO + 128]

    def load_fm(name, src, nch):
        b = k.sbuf(name, [128, nch])
        k.dma("pool", b[:, :], src.rearrange("(c p) -> p c", p=128), [], [b], allow_slow_non_contiguous=True)
        return b

    PAB = []
    for i in range(N_AB):
        p = {}
        p["gb"] = load_fm("gb", gla_gate_b[i], 2)
        p["gng"] = load_fm("gng", gla_norm_g[i], 4)
        p["mu"] = load_fm("mu", rwkv_mu[i], 14)
        for nm, src in (("w0", rwkv_w0), ("a0", rwkv_a0), ("kk", rwkv_k_k), ("ka", rwkv_k_a), ("rk", rwkv_r_k),
                        ("lg", rwkv_ln_g), ("lb", rwkv_ln_b)):
            p[nm] = load_fm(nm, src[i], 4)
        p["gw2"] = k.sbuf("gw2", [16, 256])
        k.dma("pool", p["gw2"][:, :], gla_gate_w2[i], [], [p["gw2"]])
        p["w2a2"] = k.sbuf("w2a2", [128, 512])
        k.dma("pool", p["w2a2"][0:64, :], rwkv_w2[i], [], [p["w2a2"]])
        k.dma("pool", p["w2a2"][64:128, :], rwkv_a2[i], [], [p["w2a2"]])
        p["g2"] = k.sbuf("g2", [128, 512])
        k.dma("pool", p["g2"][:, :], rwkv_g2[i], [], [p["g2"]])
        p["ngb"] = k.sbuf("ngb", [128, 2])
        k.ts(p["ngb"][:, :], p["gb"][:, :], -1.0, None, ALU.mult, None, [p["gb"]], [p["ngb"]])
        p["omka"] = k.sbuf("omka", [128, 4])
        k.ts(p["omka"][:, :], p["ka"][:, :], -1.0, 1.0, ALU.mult, ALU.add, [p["ka"]], [p["omka"]])
        PAB.append(p)
    PC = []
    for i in range(N_C):
        p = {}
        p["cw"] = k.sbuf("gcw", [128, 4, 24])
        for j in range(4):
            k.dma("pool", p["cw"][:, j, :], gdn_conv_w[i, j].rearrange("(c p) -> p c", p=128), [], [p["cw"]],
                  allow_slow_non_contiguous=True)
        p["alog"] = k.sbuf("alog", [8, 1])
        k.dma("pool", p["alog"][:, :], gdn_A_log[i].rearrange("(p o) -> p o", o=1), [], [p["alog"]])
        p["dtb"] = k.sbuf("dtb", [8, 1])
        k.dma("pool", p["dtb"][:, :], gdn_dt_bias[i].rearrange("(p o) -> p o", o=1), [], [p["dtb"]])
        p["nexpA"] = k.sbuf("nexpA", [8, 1])
        k.act(p["nexpA"][:, :], p["alog"][:, :], AF.Exp, [p["alog"]], [p["nexpA"]])
        k.ts(p["nexpA"][:, :], p["nexpA"][:, :], -1.0, None, ALU.mult, None, [p["nexpA"]], [p["nexpA"]])
        p["ng"] = k.sbuf("gdng", [128, 1])
        k.dma("pool", p["ng"][:, :], gdn_norm_g[i].rearrange("(p o) -> p o", o=1), [], [p["ng"]])
        p["wba"] = k.sbuf("wba", [128, 8, 16])
        k.dma("pool", p["wba"][:, :, :], w_in_c[i][:, 4096:4112].rearrange("(k p) c -> p k c", p=128).bitcast(F32), [], [p["wba"]])
        PC.append(p)
    PL = []
    for l in range(DEPTH):
        p = {}
        p["fcw"] = k.sbuf("fcw", [128, 3, 22])
        for j in range(3):
            k.dma("pool", p["fcw"][:, j, :], ffn_conv_w[l, j].rearrange("(c p) -> p c", p=128), [], [p["fcw"]],
                  allow_slow_non_contiguous=True)
        p["fcb"] = load_fm("fcb", ffn_conv_b[l], 22)
        p["lmg"] = load_fm("lmg", ln_mix_g[l], 8)
        p["lmb"] = load_fm("lmb", ln_mix_b[l], 8)
        p["lfg"] = load_fm("lfg", ln_ffn_g[l], 8)
        p["lfb"] = load_fm("lfb", ln_ffn_b[l], 8)
        PL.append(p)

    def zeros(name, shape):
        b = k.sbuf(name, shape)
        k.memset(b.t[:], 0.0, [b])
        return b

    P_gla = [zeros("pgla", [128, 2, 128]) for _ in range(N_AB)]
    P_rwkv = [zeros("prwkv", [128, 4, 128]) for _ in range(N_AB)]
    P_gdn = [zeros("pgdn", [128, 8, 128]) for _ in range(N_C)]
    C_xlast = [zeros("cxl", [128, 8, 1]) for _ in range(N_AB)]
    C_gconv = [zeros("cgc", [128, 24, 3]) for _ in range(N_C)]
    C_fconv = [zeros("cfc", [128, 22, 2]) for _ in range(DEPTH)]
    RAW = [zeros("srw_raw", [128, 4, 128]) for _ in range(1)]
    upad = zeros("upad", [128, 8, 128])
    vpad = zeros("vpad", [128, 8, 128])

    XE0 = k.rot("xe", [128, 8, NMAX + 16], 1)
    k.copy(XE0.t[:, :, 0:128].bitcast(F32R), upad[:, :, :], [upad], [XE0], eng="pool")
    k.copy(XE0.t[:, :, 128:144].bitcast(F32R), upad[:, :, 0:16], [upad], [XE0], eng="pool")
    WT_ELEMS = 2816
    GL = k.sbuf("gla", [128, 16, NMAX])
    GL_T = k.alias(GL, 16)
    AT = k.sbuf("aT", [128, 22, NMAX])
    AT_T = k.alias(AT, 22)
    ARENA = k.sbuf("arena", [128, 48, NMAX])
    ARENA_T = k.alias(ARENA, 48)

    def proj(W, kc, c0, ncols, rhs_fn, rbufs, N, consume, gw):
        g0 = c0
        while g0 < c0 + ncols:
            w = min(gw, c0 + ncols - g0)
            wt = k.rot("wt", [128, WT_ELEMS], 2, F32R)
            wvr = wt.t[:, 0:kc * w].rearrange("p (k c) -> p k c", c=w)
            wv = wvr.bitcast(F32)
            k.dma("sp", wvr, W[:, g0:g0 + w].rearrange("(k p) c -> p k c", p=128), [], [wt])
            j0 = 0
            while j0 < w:
                cw = min(128, w - j0)
                ps = k.psum()
                fast = (cw == 128 and N % 2 == 0)
                for kk in range(kc):
                    lh, rh = wv[:, kk, j0:j0 + cw], rhs_fn(kk)
                    if fast:
                        lh, rh = wvr[:, kk, j0:j0 + cw], rh.bitcast(F32R)
                    k.mm(ps, ps[0:cw, 0:N], lh, rh, kk == kc - 1, [wt] + rbufs)
                consume(g0 + j0, cw, ps)
                j0 += cw
            g0 += w

    def store_rows(src_fn, sbufs, nch, ncols, dst, view=None):
        for c8 in range(0, nch, 8):
            n8 = min(8, nch - c8)
            stg = k.rot("stg", [128, 1024], 1)
            for c4 in range(c8, c8 + n8, 4):
                ps = k.psum()
                n4 = min(4, c8 + n8 - c4)
                for j in range(n4):
                    if view is not None:
                        sl_ = 8 + j % 2
                        k.copy(GL[:, sl_, 0:ncols].rearrange("p (a b) -> p a b", b=view[1]), src_fn(c4 + j), sbufs, [GL_T[sl_]], eng="pool")
                        k.tr(ps, ps[0:ncols, j * 128:(j + 1) * 128], GL[:, sl_, 0:ncols], ident, [GL_T[sl_]] + RC)
                        continue
                    k.tr(ps, ps[0:ncols, j * 128:(j + 1) * 128], src_fn(c4 + j), ident, sbufs + RC)
                k.copy(stg[0:ncols, (c4 - c8) * 128:(c4 - c8 + n4) * 128], ps[0:ncols, 0:n4 * 128], [ps], [stg],
                       eng="act" if (c4 // 4) % 2 else "dve")
            k.dma("pool", dst[:, c8 * 128:(c8 + n8) * 128], stg[0:ncols, 0:n8 * 128], [stg], [])

    def layer_norm(tb, N, g, b, xo):
        sq = k.rot("oT", [128, 8, NMAX], 1)
        k.act(sq[:, :, 0:N], tb[:, :, 0:N], AF.Square, [tb], [sq])
        ps1 = k.psum()
        for m in range(8):
            k.mm(ps1, ps1[:, 0:N], ones, tb[:, m, 0:N], m == 7, [tb] + RC)
        ps2 = k.psum()
        for m in range(8):
            k.mm(ps2, ps2[:, 0:N], ones, sq[:, m, 0:N], m == 7, [sq] + RC)
        st = k.rot("pc_st", [128, 4, NMAX], 1)
        mean, var, rstd = st[:, 0, 0:N], st[:, 1, 0:N], st[:, 2, 0:N]
        k.ts(mean, ps1[:, 0:N], 1.0 / D, None, ALU.mult, None, [ps1], [st])
        k.tt(st[:, 3, 0:N], mean, mean, ALU.mult, [st], [st])
        k.stt(var, ps2[:, 0:N], 1.0 / D, st[:, 3, 0:N], ALU.mult, ALU.subtract, [ps2, st], [st])
        k.ts(var, var, LN_EPS, None, ALU.add, None, [st], [st])
        k.act(var, var, AF.Sqrt, [st], [st])
        k.recip(rstd, var, [st], [st])
        for m in range(8):
            k.tt(sq[:, m, 0:N], tb[:, m, 0:N], mean, ALU.subtract, [tb, st], [sq])
            k.tt(sq[:, m, 0:N], sq[:, m, 0:N], rstd, ALU.mult, [sq, st], [sq])
            k.ts(xo[:, m, 0:N].bitcast(F32R), sq[:, m, 0:N], g[:, m:m + 1], b[:, m:m + 1], ALU.mult, ALU.add, [sq, g, b], [xo])

    def tri_solve(Pp, Pm, pbuf, R0, rbuf, L, w, Ls_nil, final_out, final_bufs):
        nlev = 0
        while (1 << nlev) < Ls_nil:
            nlev += 1
        cur_p, cur_m, cur_r, cur_rb = Pp, Pm, R0, rbuf
        cur_pb = pbuf
        if nlev == 0:
            k.copy(final_out, cur_r, [cur_rb], final_bufs)
            return
        for j in range(nlev):
            ps = k.psum()
            k.mm(ps, ps[0:L, 0:w], cur_p, cur_r, True, [cur_pb, cur_rb])
            if j == nlev - 1:
                k.tt(final_out, ps[0:L, 0:w], cur_r, ALU.add, [ps, cur_rb], final_bufs)
                break
            nr = k.rot("solr", [128, 128], 2)
            k.tt(nr[0:L, 0:w], ps[0:L, 0:w], cur_r, ALU.add, [ps, cur_rb], [nr])
            psq = k.psum()
            k.mm(psq, psq[0:L, 0:L], cur_m, cur_p, False, [cur_pb])
            k.mm(psq, psq[0:L, 128:128 + L], cur_p, cur_m, True, [cur_pb])
            npb = k.rot("solp", [128, 256], 2)
            k.copy(npb[0:L, 0:L], psq[0:L, 0:L], [psq], [npb], eng="act")
            k.copy(npb[0:L, 128:128 + L], psq[0:L, 128:128 + L], [psq], [npb], eng="act")
            cur_p, cur_m, cur_pb = npb[0:L, 0:L], npb[0:L, 128:128 + L], npb
            cur_r, cur_rb = nr[0:L, 0:w], nr

    def ab_mixer(i, tile, xT, mixT):
        kind, nseq, Ls, N = tile["kind"], tile["nseq"], tile["Ls"], tile["N"]
        P = PAB[i]
        Ne = nseq * (Ls + 1)
        xe = k.rot("xe", [128, 8, NMAX + 16], 1)
        if not tile.get("xe_init_done"):
            pass
        xev = xe.t[:, :, 0:Ne].rearrange("p c (s t) -> p c s t", t=Ls + 1)
        if kind == "p":
            k.copy(xev[:, :, 0, 0:1].bitcast(F32R), C_xlast[i][:, :, :], [C_xlast[i]], [xe], eng="pool")
        else:
            xl = k.rot("stg", [128, 1024], 1)
            k.dma("pool", xl[0:nseq, 0:D], st_shift[i], [], [xl])
            for c4 in (0, 4):
                ps = k.psum()
                for j in range(4):
                    k.tr(ps, ps[:, j * nseq:(j + 1) * nseq], xl[0:nseq, (c4 + j) * 128:(c4 + j + 1) * 128],
                         ident[0:nseq, 0:nseq], [xl] + RC)
                k.copy(xev[:, c4:c4 + 4, :, 0].bitcast(F32R), ps[:, 0:4 * nseq].rearrange("p (c s) -> p c s", s=nseq), [ps], [xe])
        k.copy(xev[:, :, :, 1:Ls + 1].bitcast(F32R), xT[:, :, 0:N].rearrange("p c (s t) -> p c s t", t=Ls), [xT], [xe], eng="pool")
        if kind == "p":
            k.copy(C_xlast[i][:, :, :], xT[:, :, N - 1:N], [xT, xe], [C_xlast[i]], eng="pool")
            if tile["last"]:
                store_rows(lambda c: C_xlast[i][:, c, :], [C_xlast[i]], 8, 1, o_shift[0][i])
        else:
            xlv = xT[:, :, 0:N].rearrange("p c (s t) -> p c s t", t=Ls)
            store_rows(lambda c: xlv[:, c, :, Ls - 1], [xT], 8, nseq, o_shift[1][i])

        z = k.rot("z", [128, 33, NMAX + 4], 1)
        zc = tile["zc"]

        def cons(c_abs, cw, ps):
            if c_abs < 1024:
                ci = c_abs // 128
            elif c_abs == 1024:
                ci = 8
            else:
                ci = 9 + (c_abs - 1040) // 128
            k.copy(z[0:cw, ci, 0:Ne], ps[0:cw, 0:Ne], [ps], [zc[ci]], eng="act")

        W = w_in_ab[i]
        Np = Ne + (Ne % 2)
        proj(W, 8, 0, 1024, lambda kk: xe[:, kk, 0:Np], [xe], Np, cons, 256)
        proj(W, 8, 1024, 16, lambda kk: xe[:, kk, 0:Np], [xe], Np, cons, 256)
        proj(W, 8, 1040, AB_COLS - 1040, lambda kk: xe[:, kk, 0:Np], [xe], Np, cons, 256)

        def zv(ci, rows=slice(0, 128)):
            return z.t[rows, ci, 0:Ne].rearrange("p (s t) -> p s t", t=Ls + 1)[:, :, 1:Ls + 1]

        def zp(ci, rows=slice(0, 128)):
            return z.t[rows, ci, 0:Ne].rearrange("p (s t) -> p s t", t=Ls + 1)[:, :, 0:Ls]

        def v3(ap):
            return ap.rearrange("p (s t) -> p s t", t=Ls)

        g = GL
        gA = GL_T
        for c in range(2):
            ps = k.psum()
            k.mm(ps, ps[:, 0:Ne], P["gw2"][0:16, c * 128:(c + 1) * 128], z[0:16, 8, 0:Ne], True, [P["gw2"], zc[8]])
            pv = ps[:, 0:Ne].rearrange("p (s t) -> p s t", t=Ls + 1)[:, :, 1:Ls + 1]
            k.act(v3(g[:, 10 + c, 0:N]), pv, AF.Exp, [ps, P["ngb"]], [gA[10 + c]], bias=P["ngb"][:, c:c + 1], scale=-1.0)
            k.act(g[:, 10 + c, 0:N], g[:, 10 + c, 0:N], AF.Ln, [gA[10 + c]], [gA[10 + c]], bias=1.0)
            rst = ones[:, 0:N] if kind == "p" else cst[:, C_RST:C_RST + N]
            k.scan(g[:, c, 0:N], rst, g[:, 10 + c, 0:N], [gA[10 + c]] + RC, [gA[c]])
            k.act(g[:, 2 + c, 0:N], g[:, c, 0:N], AF.Exp, [gA[c]], [gA[2 + c]], scale=-1.0 / 16)
            k.act(g[:, 4 + c, 0:N], g[:, c, 0:N], AF.Exp, [gA[c]], [gA[4 + c]], scale=1.0 / 16)
            k.stt(v3(g[:, 6 + c, 0:N]), zv(c), 0.125, v3(g[:, 2 + c, 0:N]), ALU.mult, ALU.mult, [zc[c], gA[2 + c]], [gA[6 + c]])
            k.tt(v3(g[:, 8 + c, 0:N]), zv(2 + c), v3(g[:, 4 + c, 0:N]), ALU.mult, [zc[2 + c], gA[4 + c]], [gA[8 + c]])
        for h in range(4):
            k.act(v3(g[:, 12 + h, 0:N]), zv(9 + h), AF.Silu, [zc[9 + h]], [gA[12 + h]])

        r = ARENA
        rA = ARENA_T
        zm = k.rot("zm", [128, 14, NMAX], 1)
        zmA = k.alias(zm, 14)
        for c in range(14):
            ci = 13 + c
            k.tt(v3(zm[:, c, 0:N]), zp(ci), zv(ci), ALU.subtract, [zc[ci]], [zmA[c]], eng="pool")
            k.stt(v3(zm[:, c, 0:N]), v3(zm[:, c, 0:N]), P["mu"][:, c:c + 1], zv(ci), ALU.mult, ALU.add,
                  [zmA[c], P["mu"], zc[ci]], [zmA[c]])
        k.act(r[0:64, 12, 0:N], zm[0:64, 12, 0:N], AF.Tanh, [zmA[12]], [rA[12]])
        k.copy(r[64:128, 12, 0:N], zm[64:128, 12, 0:N], [zmA[12]], [rA[12]])
        k.act(r[:, 13, 0:N], zm[:, 13, 0:N], AF.Sigmoid, [zmA[13]], [rA[13]])
        rst = ones[:, 0:N] if kind == "p" else cst[:, C_RST:C_RST + N]
        for c in range(4):
            cs = slice(c * 128, (c + 1) * 128)
            ps = k.psum()
            k.mm(ps, ps[:, 0:N], P["w2a2"][0:64, cs], r[0:64, 12, 0:N], True, [P["w2a2"], rA[12]])
            k.act(r[:, 14 + c, 0:N], ps[:, 0:N], AF.Sigmoid, [ps, P["w0"]], [rA[14 + c]], bias=P["w0"][:, c:c + 1])
            ps = k.psum()
            k.mm(ps, ps[:, 0:N], P["w2a2"][64:128, cs], r[64:128, 12, 0:N], True, [P["w2a2"], rA[12]])
            k.act(r[:, 18 + c, 0:N], ps[:, 0:N], AF.Sigmoid, [ps, P["a0"]], [rA[18 + c]], bias=P["a0"][:, c:c + 1])
            ps = k.psum()
            k.mm(ps, ps[:, 0:N], P["g2"][:, cs], r[:, 13, 0:N], True, [P["g2"], rA[13]])
            k.copy(r[:, 22 + c, 0:N], ps[:, 0:N], [ps], [rA[22 + c]], eng="act")
            k.ts(r[:, 26 + c, 0:N], zm[:, 4 + c, 0:N], P["kk"][:, c:c + 1], None, ALU.mult, None, [zmA[4 + c], P["kk"]], [rA[26 + c]])
            k.act(r[:, 38 + c, 0:N], r[:, 26 + c, 0:N], AF.Square, [rA[26 + c]], [rA[38 + c]])
            ps = k.psum()
            k.mm(ps, ps[:, 0:N], bo64, r[:, 38 + c, 0:N], True, [rA[38 + c]] + RC)
            k.act(r[:, 38 + c, 0:N], ps[:, 0:N], AF.Sqrt, [ps], [rA[38 + c]], bias=NORM_EPS)
            k.recip(r[:, 38 + c, 0:N], r[:, 38 + c, 0:N], [rA[38 + c]], [rA[38 + c]])
            k.tt(r[:, 26 + c, 0:N], r[:, 26 + c, 0:N], r[:, 38 + c, 0:N], ALU.mult, [rA[26 + c], rA[38 + c]], [rA[26 + c]])
            k.ts(r[:, 30 + c, 0:N], r[:, 18 + c, 0:N], P["ka"][:, c:c + 1], P["omka"][:, c:c + 1], ALU.mult, ALU.add,
                 [rA[18 + c], P["ka"], P["omka"]], [rA[30 + c]])
            k.tt(r[:, 30 + c, 0:N], r[:, 30 + c, 0:N], zm[:, 4 + c, 0:N], ALU.mult, [rA[30 + c], zmA[4 + c]], [rA[30 + c]])
            k.tt(r[:, 42 + c, 0:N], r[:, 26 + c, 0:N], r[:, 18 + c, 0:N], ALU.mult, [rA[26 + c], rA[18 + c]], [rA[42 + c]])
            k.scan(r[:, 34 + c, 0:N], rst, r[:, 14 + c, 0:N], [rA[14 + c]] + RC, [rA[34 + c]])
        ar = k.rot("ar", [128, 4, 2, NMAX], 1)
        bk = k.rot("bk", [128, 4, 2, NMAX], 1)
        arA = k.alias(ar, 4)
        bkA = k.alias(bk, 4)
        for c in range(4):
            k.act(r[:, 38 + c, 0:N], r[:, 34 + c, 0:N], AF.Exp, [rA[34 + c]], [rA[38 + c]], scale=-C0)
            k.tt(ar[:, c, 1, 0:N], zm[:, c, 0:N], r[:, 38 + c, 0:N], ALU.mult, [zmA[c], rA[38 + c]], [arA[c]])
            k.tt(r[:, 38 + c, 0:N], r[:, 34 + c, 0:N], r[:, 14 + c, 0:N], ALU.subtract, [rA[34 + c], rA[14 + c]], [rA[38 + c]])
            k.act(r[:, 38 + c, 0:N], r[:, 38 + c, 0:N], AF.Exp, [rA[38 + c]], [rA[38 + c]], scale=-C0)
            k.stt(ar[:, c, 0, 0:N], r[:, 26 + c, 0:N], -1.0, r[:, 38 + c, 0:N], ALU.mult, ALU.mult, [rA[26 + c], rA[38 + c]], [arA[c]])
            k.act(r[:, 38 + c, 0:N], r[:, 34 + c, 0:N], AF.Exp, [rA[34 + c]], [rA[38 + c]], scale=C0)
            k.tt(bk[:, c, 0, 0:N], r[:, 42 + c, 0:N], r[:, 38 + c, 0:N], ALU.mult, [rA[42 + c], rA[38 + c]], [bkA[c]])
            k.tt(bk[:, c, 1, 0:N], r[:, 30 + c, 0:N], r[:, 38 + c, 0:N], ALU.mult, [rA[30 + c], rA[38 + c]], [bkA[c]])

        oT = k.rot("oT", [128, 8, NMAX], 1)
        oA = k.alias(oT, 8)

        for s in range(nseq):
            c0 = s * Ls
            L = Ls
            cols = slice(c0, c0 + L)
            last = c0 + L - 1
            if kind == "p":
                Mg, Mr = P_gla[i], P_rwkv[i]
            else:
                Mg = k.rot("sgla", [128, 2, 128], 2)
                k.dma("sp", Mg[:, :, :], st_gla[i, s].rearrange("(c two) kk v -> (two kk) c v", two=2), [], [Mg])
                raw = RAW[0]
                for two in range(2):
                    k.dma("sp", raw[two * 64:(two + 1) * 64, :, two * 64:(two + 1) * 64],
                          st_rwkv[i, s].rearrange("(c two) v kk -> two v c kk", two=2)[two], [], [raw])
                Mr = k.rot("srw", [128, 4, 128], 1)
                ps = k.psum()
                for c in range(4):
                    k.tr(ps, ps[:, c * 128:(c + 1) * 128], raw[:, c, :], ident, [raw] + RC)
                k.copy(Mr[:, :, :], ps[:, 0:512].rearrange("p (c v) -> p c v", v=128), [ps], [Mr], eng="act")

            vt = k.rot("gvt", [128, 4, 128], 1)
            ps = k.psum()
            for h in range(4):
                k.tr(ps, ps[0:L, h * 128:(h + 1) * 128], z.t[:, 4 + h, s * (Ls + 1) + 1:s * (Ls + 1) + 1 + L], ident, [zc[4 + h]] + RC)
            k.copy(vt[0:L, :, :], ps[0:L, 0:512].rearrange("p (h v) -> p h v", v=128), [ps], [vt], eng="act")
            kt = k.rot("gkt", [128, 2, 128], 1)
            ps = k.psum()
            for c in range(2):
                k.ts(g[:, 10 + c, cols], g[:, c, cols], g[:, c, last:last + 1], None, ALU.subtract, None, [gA[c]], [gA[10 + c]])
                k.act(g[:, 10 + c, cols], g[:, 10 + c, cols], AF.Exp, [gA[10 + c]], [gA[10 + c]], scale=1.0 / 16)
                k.tt(g[:, 10 + c, cols], g[:, 10 + c, cols],
                     z.t[:, 2 + c, s * (Ls + 1) + 1:s * (Ls + 1) + 1 + L], ALU.mult, [gA[10 + c], zc[2 + c]], [gA[10 + c]])
                k.tr(ps, ps[0:L, c * 128:(c + 1) * 128], g[:, 10 + c, cols], ident, [gA[10 + c]] + RC)
            k.copy(kt[0:L, :, :], ps[0:L, 0:256].rearrange("p (c v) -> p c v", v=128), [ps], [kt])
            for h in range(4):
                c, two = h // 2, h % 2
                rows = slice(two * 64, two * 64 + 64)
                ps = k.psum()
                k.mm(ps, ps[0:L, 0:L], g[rows, 8 + c, cols], g[rows, 6 + c, cols], True, [gA[8 + c], gA[6 + c]])
                at = k.rot("gat", [128, 128], 2)
                k.tt(at[0:L, 0:L], ps[0:L, 0:L], cm(C_IU, L, L), ALU.mult, [ps] + RC, [at])
                po = k.psum()
                k.mm(po, po[:, 0:L], Mg[rows, c, :], g[rows, 6 + c, cols], False, [Mg, gA[6 + c]])
                k.mm(po, po[:, 0:L], vt[0:L, h, :], at[0:L, 0:L], True, [vt, at])
                k.copy(oT[:, h, cols], po[:, 0:L], [po], [oA[h]], eng="act")
                pu = k.psum()
                k.mm(pu, pu[:, 0:128], kt[0:L, c, :], vt[0:L, h, :], True, [kt, vt])
                k.stt(Mg[rows, c, :], Mg[rows, c, :], g[rows, 2 + c, last:last + 1], pu[rows, 0:128], ALU.mult, ALU.add,
                      [Mg, gA[2 + c], pu], [Mg])

            vtr = vpad
            ps = k.psum()
            for c in range(4):
                k.tr(ps, ps[0:L, c * 128:(c + 1) * 128], zm[:, 8 + c, cols], ident, [zmA[8 + c]] + RC)
            for c in range(4):
                for two in range(2):
                    h = 2 * c + two
                    k.copy(vpad[0:L, h, two * 64:two * 64 + 64], ps[0:L, c * 128 + two * 64:c * 128 + two * 64 + 64], [ps], [vpad],
                           eng="act" if two else "dve")
            ket = k.rot("rket", [128, 4, 2, 128], 1)
            for c in range(4):
                k.ts(r[:, 38 + c, cols], r[:, 34 + c, cols], r[:, 34 + c, last:last + 1], None, ALU.subtract, None, [rA[34 + c]], [rA[38 + c]])
                k.act(r[:, 38 + c, cols], r[:, 38 + c, cols], AF.Exp, [rA[38 + c]], [rA[38 + c]], scale=C0)
                tmp = k.rot("rtmp", [128, 2, 128], 1)
                k.tt(tmp[:, 0, 0:L], r[:, 30 + c, cols], r[:, 38 + c, cols], ALU.mult, [rA[30 + c], rA[38 + c]], [tmp])
                k.tt(tmp[:, 1, 0:L], r[:, 42 + c, cols], r[:, 38 + c, cols], ALU.mult, [rA[42 + c], rA[38 + c]], [tmp])
                ps = k.psum()
                k.tr(ps, ps[0:L, 0:128], tmp[:, 0, 0:L], ident, [tmp] + RC)
                k.tr(ps, ps[0:L, 128:256], tmp[:, 1, 0:L], ident, [tmp] + RC)
                k.copy(ket[0:L, c, :, :], ps[0:L, 0:256].rearrange("p (a v) -> p a v", v=128), [ps], [ket], eng="act")
            for c in range(4):
                NK = []
                for two in range(2):
                    rows = slice(two * 64, two * 64 + 64)
                    pb = k.psum()
                    k.mm(pb, pb[0:L, 0:L], bk[rows, c, 0, cols], ar[rows, c, 0, cols], False, [bkA[c], arA[c]])
                    k.mm(pb, pb[0:L, L:2 * L], bk[rows, c, 0, cols], ar[rows, c, 1, cols], True, [bkA[c], arA[c]])
                    pk = k.psum()
                    k.mm(pk, pk[0:L, 0:L], bk[rows, c, 1, cols], ar[rows, c, 0, cols], False, [bkA[c], arA[c]])
                    k.mm(pk, pk[0:L, L:2 * L], bk[rows, c, 1, cols], ar[rows, c, 1, cols], True, [bkA[c], arA[c]])
                    pn = k.psum()
                    k.mm(pn, pn[0:L, 0:L], ar[rows, c, 0, cols], bk[rows, c, 0, cols], True, [bkA[c], arA[c]])
                    nb = k.rot("rnb", [128, 5, 128], 2)
                    k.tt(nb[0:L, 0, 0:L], pb[0:L, 0:L], cm(C_SU, L, L), ALU.mult, [pb] + RC, [nb])
                    k.tt(nb[0:L, 1, 0:L], pb[0:L, L:2 * L], cm(C_IU, L, L), ALU.mult, [pb] + RC, [nb])
                    k.tt(nb[0:L, 2, 0:L], pk[0:L, 0:L], cm(C_SU, L, L), ALU.mult, [pk] + RC, [nb])
                    k.tt(nb[0:L, 3, 0:L], pk[0:L, L:2 * L], cm(C_IU, L, L), ALU.mult, [pk] + RC, [nb])
                    k.tt(nb[0:L, 4, 0:L], pn[0:L, 0:L], cm(C_SL, L, L), ALU.mult, [pn] + RC, [nb])
                    NK.append(nb)
                pr = k.psum()
                k.mm(pr, pr[0:L, 0:128], ar[:, c, 0, cols], Mr[:, c, :], False, [arA[c], Mr])
                for two in range(2):
                    k.mm(pr, pr[0:L, 0:128], NK[two][0:L, 2, 0:L], vpad[0:L, 2 * c + two, :], two == 1, [NK[two], vpad])
                r0 = k.rot("rr0", [128, 128], 1)
                k.copy(r0[0:L, :], pr[0:L, 0:128], [pr], [r0], eng="act")
                for two in range(2):
                    h = 2 * c + two
                    tri_solve(NK[two][0:L, 0, 0:L], NK[two][0:L, 4, 0:L], NK[two], r0[0:L, two * 64:two * 64 + 64], r0, L, 64, L,
                              upad[0:L, h, two * 64:two * 64 + 64], [upad])
                py = k.psum()
                k.mm(py, py[:, 0:L], Mr[:, c, :], ar[:, c, 1, cols], False, [Mr, arA[c]])
                for two in range(2):
                    h = 2 * c + two
                    k.mm(py, py[:, 0:L], upad[0:L, h, :], NK[two][0:L, 1, 0:L], False, [upad, NK[two]])
                    k.mm(py, py[:, 0:L], vpad[0:L, h, :], NK[two][0:L, 3, 0:L], two == 1, [vpad, NK[two]])
                k.copy(oT[:, 4 + c, cols], py[:, 0:L], [py], [oA[4 + c]], eng="act")
                pu = k.psum()
                k.mm(pu, pu[:, 0:128], ket[0:L, c, 0, :], vpad[0:L, 2 * c, :], False, [ket, vpad])
                k.mm(pu, pu[:, 0:128], ket[0:L, c, 0, :], vpad[0:L, 2 * c + 1, :], False, [ket, vpad])
                k.mm(pu, pu[:, 0:128], ket[0:L, c, 1, :], upad[0:L, 2 * c, :], False, [ket, upad])
                k.mm(pu, pu[:, 0:128], ket[0:L, c, 1, :], upad[0:L, 2 * c + 1, :], True, [ket, upad])
                ut = k.rot("rut", [128, 128], 1)
                k.tt(ut[:, :], pu[:, 0:128], bo64, ALU.mult, [pu] + RC, [ut])
                de = k.rot("rde", [128, 1], 2)
                k.act(de[:, :], r[:, 34 + c, last:last + 1], AF.Exp, [rA[34 + c]], [de], scale=-C0)
                k.stt(Mr[:, c, :], Mr[:, c, :], de[:, 0:1], ut[:, :], ALU.mult, ALU.add, [Mr, de, ut], [Mr])

            if kind == "s" or tile["last"]:
                gi = 0 if kind == "p" else 1
                si = 0 if kind == "p" else s
                k.dma("pool", o_gla[gi][i, si].rearrange("(c two) kk v -> (two kk) c v", two=2), Mg[:, :, :], [Mg], [])
                ps = k.psum()
                for c in range(4):
                    k.tr(ps, ps[:, c * 128:(c + 1) * 128], Mr[:, c, :], ident, [Mr] + RC)
                so = k.rot("srw_out", [128, 4, 128], 1)
                k.copy(so[:, :, :], ps[:, 0:512].rearrange("p (c v) -> p c v", v=128), [ps], [so])
                for two in range(2):
                    k.dma("pool", o_rwkv[gi][i, si].rearrange("(c two) v kk -> two v c kk", two=2)[two],
                          so[two * 64:(two + 1) * 64, :, two * 64:(two + 1) * 64], [so], [])

        for h in range(4):
            sq = k.rot("pc_sq", [128, NMAX], 2)
            k.act(sq[:, 0:N], oT[:, h, 0:N], AF.Square, [oA[h]], [sq])
            ps = k.psum()
            k.mm(ps, ps[:, 0:N], ones, sq[:, 0:N], True, [sq] + RC)
            k.act(sq[:, 0:N], ps[:, 0:N], AF.Sqrt, [ps], [sq], bias=NORM_EPS, scale=1.0 / 128)
            k.recip(sq[:, 0:N], sq[:, 0:N], [sq], [sq])
            k.tt(sq[:, 0:N], sq[:, 0:N], oT[:, h, 0:N], ALU.mult, [sq, oA[h]], [sq])
            k.stt(mixT[:, h, 0:N].bitcast(F32R), sq[:, 0:N], P["gng"][:, h:h + 1], g[:, 12 + h, 0:N], ALU.mult, ALU.mult,
                  [sq, P["gng"], gA[12 + h]], [mixT])
        for c in range(4):
            sq = k.rot("pc_sq", [128, NMAX], 2)
            st = k.rot("pc_st", [128, 4, NMAX], 1)
            k.act(sq[:, 0:N], oT[:, 4 + c, 0:N], AF.Square, [oA[4 + c]], [sq])
            p1 = k.psum()
            k.mm(p1, p1[:, 0:N], bo64, oT[:, 4 + c, 0:N], True, [oA[4 + c]] + RC)
            p2 = k.psum()
            k.mm(p2, p2[:, 0:N], bo64, sq[:, 0:N], True, [sq] + RC)
            k.ts(st[:, 0, 0:N], p1[:, 0:N], 1.0 / 64, None, ALU.mult, None, [p1], [st])
            k.tt(st[:, 1, 0:N], st[:, 0, 0:N], st[:, 0, 0:N], ALU.mult, [st], [st])
            k.stt(st[:, 1, 0:N], p2[:, 0:N], 1.0 / 64, st[:, 1, 0:N], ALU.mult, ALU.subtract, [p2, st], [st])
            k.act(st[:, 1, 0:N], st[:, 1, 0:N], AF.Sqrt, [st], [st], bias=GN_EPS)
            k.recip(st[:, 1, 0:N], st[:, 1, 0:N], [st], [st])
            k.tt(st[:, 2, 0:N], oT[:, 4 + c, 0:N], st[:, 0, 0:N], ALU.subtract, [oA[4 + c], st], [st])
            k.tt(st[:, 2, 0:N], st[:, 2, 0:N], st[:, 1, 0:N], ALU.mult, [st], [st])
            k.ts(st[:, 2, 0:N], st[:, 2, 0:N], P["lg"][:, c:c + 1], P["lb"][:, c:c + 1], ALU.mult, ALU.add, [st, P["lg"], P["lb"]], [st])
            k.stt(sq[:, 0:N], zm[:, c, 0:N], P["rk"][:, c:c + 1], r[:, 30 + c, 0:N], ALU.mult, ALU.mult,
                  [zmA[c], P["rk"], rA[30 + c]], [sq])
            p3 = k.psum()
            k.mm(p3, p3[:, 0:N], bo64, sq[:, 0:N], True, [sq] + RC)
            k.tt(st[:, 3, 0:N], p3[:, 0:N], zm[:, 8 + c, 0:N], ALU.mult, [p3, zmA[8 + c]], [st])
            k.tt(st[:, 2, 0:N], st[:, 2, 0:N], st[:, 3, 0:N], ALU.add, [st], [st])
            k.tt(mixT[:, 4 + c, 0:N].bitcast(F32R), st[:, 2, 0:N], r[:, 22 + c, 0:N], ALU.mult, [st, rA[22 + c]], [mixT])

    def c_mixer(i, tile, xT, mixT):
        kind, nseq, Ls, N = tile["kind"], tile["nseq"], tile["Ls"], tile["N"]
        P = PC[i]
        Ne = nseq * (Ls + 3)
        z = k.rot("z", [128, 33, NMAX + 4], 1)
        zc = tile["zc"]
        ze = lambda ci: z.t[:, ci, 0:Ne].rearrange("p (s t) -> p s t", t=Ls + 3)
        if kind == "p":
            k.copy(z.t[:, 0:24, 0:3], C_gconv[i][:, :, :], [C_gconv[i]], zc[0:24], eng="pool")
        else:
            for c4 in range(0, 24, 4):
                if c4 % 8 == 0:
                    cb = k.rot("stg", [128, 1024], 1)
                    k.dma("pool", cb[0:nseq * 3, :], st_gconv[i].rearrange("s r c -> (s r) c")[:, c4 * 128:c4 * 128 + 1024], [], [cb])
                ps = k.psum()
                for j in range(4):
                    k.tr(ps, ps[:, j * 48:j * 48 + nseq * 3], cb[0:nseq * 3, (c4 % 8 + j) * 128:(c4 % 8 + j + 1) * 128],
                         ident[0:nseq * 3, 0:nseq * 3], [cb] + RC)
                for j in range(4):
                    k.copy(ze(c4 + j)[:, :, 0:3], ps[:, j * 48:j * 48 + nseq * 3].rearrange("p (s r) -> p s r", r=3), [ps], [zc[c4 + j]])

        def v3(ap):
            return ap.rearrange("p (s t) -> p s t", t=Ls)

        def cons(c_abs, cw, ps):
            ci = c_abs // 128
            if ci < 24:
                k.copy(ze(ci)[:, :, 3:Ls + 3], v3(ps[:, 0:N]), [ps], [zc[ci]], eng="act")
            else:
                k.act(z[:, ci, 0:N], ps[:, 0:N], AF.Silu, [ps], [zc[ci]])

        proj(w_in_c[i], 8, 0, 4096, lambda kk: xT[:, kk, 0:N], [xT], N, cons, 256)
        if kind == "p":
            k.copy(C_gconv[i][:, :, :], z.t[:, 0:24, Ls:Ls + 3], zc[0:24], [C_gconv[i]], eng="pool")
            if tile["last"]:
                store_rows(lambda c: C_gconv[i][:, c, :], [C_gconv[i]], 24, 3, o_gconv[0][i, 0])
        else:
            store_rows(lambda c: ze(c)[:, :, Ls:Ls + 3], zc[0:24], 24, nseq * 3, o_gconv[1][i].rearrange("s r c -> (s r) c"), view=(nseq, 3))
        q = ARENA
        qA = ARENA_T
        for ci in range(24):
            e = ze(ci)
            tmp = k.rot("ctmp", [128, NMAX], 2)
            k.ts(v3(tmp[:, 0:N]), e[:, :, 0:Ls], P["cw"][:, 0, ci:ci + 1], None, ALU.mult, None, [zc[ci], P["cw"]], [tmp])
            for j in (1, 2, 3):
                k.stt(v3(tmp[:, 0:N]), e[:, :, j:j + Ls], P["cw"][:, j, ci:ci + 1], v3(tmp[:, 0:N]), ALU.mult, ALU.add,
                      [zc[ci], P["cw"], tmp], [tmp])
            k.act(q[:, ci, 0:N], tmp[:, 0:N], AF.Silu, [tmp], [qA[ci]])
        for ci in range(16):
            sq = k.rot("pc_sq", [128, NMAX], 2)
            k.act(sq[:, 0:N], q[:, ci, 0:N], AF.Square, [qA[ci]], [sq])
            ps = k.psum()
            k.mm(ps, ps[:, 0:N], ones, sq[:, 0:N], True, [sq] + RC)
            k.act(sq[:, 0:N], ps[:, 0:N], AF.Sqrt, [ps], [sq], bias=NORM_EPS)
            k.recip(sq[:, 0:N], sq[:, 0:N], [sq], [sq])
            if ci < 8:
                k.stt(q[:, ci, 0:N], q[:, ci, 0:N], 128.0 ** -0.5, sq[:, 0:N], ALU.mult, ALU.mult, [qA[ci], sq], [qA[ci]])
            else:
                k.tt(q[:, ci, 0:N], q[:, ci, 0:N], sq[:, 0:N], ALU.mult, [qA[ci], sq], [qA[ci]])
        GBT = GL_T[10:16]
        pb_ = k.psum()
        for kk in range(8):
            k.mm(pb_, pb_[0:8, 0:N], P["wba"][:, kk, 0:8], xT[:, kk, 0:N], kk == 7, [P["wba"], xT])
        k.act(GL[0:8, 11, 0:N], pb_[0:8, 0:N], AF.Sigmoid, [pb_], GBT)
        pa_ = k.psum()
        for kk in range(8):
            k.mm(pa_, pa_[0:8, 0:N], P["wba"][:, kk, 8:16], xT[:, kk, 0:N], kk == 7, [P["wba"], xT])
        k.act(GL[0:8, 15, 0:N], pa_[0:8, 0:N], AF.Exp, [pa_, P["dtb"]], GBT, bias=P["dtb"][:, 0:1])
        k.act(GL[0:8, 15, 0:N], GL[0:8, 15, 0:N], AF.Ln, GBT, GBT, bias=1.0)
        k.ts(GL[0:8, 14, 0:N], GL[0:8, 15, 0:N], P["nexpA"][:, 0:1], None, ALU.mult, None, GBT + [P["nexpA"]], GBT)
        rst = ones[0:8, 0:N] if kind == "p" else cst[0:8, C_RST:C_RST + N]
        k.scan(GL[0:8, 10, 0:N], rst, GL[0:8, 14, 0:N], GBT + RC, GBT)
        k.act(GL[0:8, 13, 0:N], GL[0:8, 10, 0:N], AF.Exp, GBT, GBT)
        k.ts(GL[0:8, 13, 0:N], GL[0:8, 13, 0:N], -1.0, None, ALU.mult, None, GBT, GBT)

        oT = k.rot("oT", [128, 8, NMAX], 1)
        oA = k.alias(oT, 8)
        for s in range(nseq):
            c0 = s * Ls
            L = Ls
            cols = slice(c0, c0 + L)
            last = c0 + L - 1
            if kind == "p":
                Ms = P_gdn[i]
                MsT = [Ms]
            else:
                Ms = GL
                MsT = GL_T[0:8]
                k.dma("sp", Ms[:, 0:8, :], st_gdn[i, s].rearrange("h kk v -> kk h v"), [], MsT)
            k.ts(GL[0:8, 12, cols], GL[0:8, 10, cols], GL[0:8, 10, last:last + 1], None, ALU.subtract, None, GBT, GBT)
            k.act(GL[0:8, 12, cols], GL[0:8, 12, cols], AF.Exp, GBT, GBT, scale=-1.0)
            ps = k.psum()
            for j in range(4):
                k.tr(ps, ps[0:L, j * 8:(j + 1) * 8], GL[0:8, 10 + j, cols], ident[0:8, 0:8], GBT + RC)
            tm = k.rot("gtm", [128, 4, 8], 2)
            k.copy(tm[0:L, :, :], ps[0:L, 0:32].rearrange("p (a h) -> p a h", h=8), [ps], [tm])
            for h in range(8):
                gm = k.rot("ggm", [8, 2, NMAX], 1)
                k.ts(gm[0:8, :, 0:L], GL[0:8, 10:12, cols], ident[0:8, h:h + 1], None, ALU.mult, None, GBT + RC, [gm])
                pg = k.psum()
                k.mm(pg, pg[:, 0:L], ones[0:8, :], gm[0:8, 0, 0:L], False, [gm] + RC)
                k.mm(pg, pg[:, 128:128 + L], ones[0:8, :], gm[0:8, 1, 0:L], True, [gm] + RC)
                pq = k.psum()
                k.mm(pq, pq[0:L, 0:L], q[:, 8 + h, cols], q[:, h, cols], False, [qA[8 + h], qA[h]])
                k.mm(pq, pq[0:L, 128:128 + L], q[:, 8 + h, cols], q[:, 8 + h, cols], True, [qA[8 + h]])
                w = k.rot("gw", [128, 6, 128], 1)
                gam_p = tm[0:L, 0, h:h + 1]
                k.ts(w[0:L, 0, 0:L], pg[0:L, 0:L], gam_p, 0.0, ALU.subtract, ALU.min, [pg, tm], [w])
                k.act(w[0:L, 0, 0:L], w[0:L, 0, 0:L], AF.Exp, [w], [w])
                k.ts(w[0:L, 1, 0:L], pg[0:L, 0:L], gam_p, 0.0, ALU.subtract, ALU.max, [pg, tm], [w])
                k.act(w[0:L, 1, 0:L], w[0:L, 1, 0:L], AF.Exp, [w], [w], scale=-1.0)
                k.tt(w[0:L, 4, 0:L], w[0:L, 0, 0:L], cm(C_IU, L, L), ALU.mult, [w] + RC, [w])
                k.tt(w[0:L, 4, 0:L], w[0:L, 4, 0:L], pq[0:L, 0:L], ALU.mult, [w, pq], [w])
                k.tt(w[0:L, 2, 0:L], w[0:L, 0, 0:L], pg[0:L, 128:128 + L], ALU.mult, [w, pg], [w])
                k.tt(w[0:L, 2, 0:L], w[0:L, 2, 0:L], cm(C_NSU, L, L), ALU.mult, [w] + RC, [w])
                k.tt(w[0:L, 2, 0:L], w[0:L, 2, 0:L], pq[0:L, 128:128 + L], ALU.mult, [w, pq], [w])
                k.stt(w[0:L, 3, 0:L], w[0:L, 1, 0:L], tm[0:L, 1, h:h + 1], cm(C_NSL, L, L), ALU.mult, ALU.mult, [w, tm] + RC, [w])
                k.tt(w[0:L, 3, 0:L], w[0:L, 3, 0:L], pq[0:L, 128:128 + L], ALU.mult, [w, pq], [w])
                k.act(w[:, 5, 0:L], pg[:, 0:L], AF.Exp, [pg], [w])
                qg = k.rot("gqg", [128, 128], 2)
                k.tt(qg[:, 0:L], q[:, h, cols], w[:, 5, 0:L], ALU.mult, [qA[h], w], [qg])
                pt = k.psum()
                k.tr(pt, pt[0:L, 0:128], q[:, 16 + h, cols], ident, [qA[16 + h]] + RC)
                k.tr(pt, pt[0:L, 128:256], q[:, 8 + h, cols], ident, [qA[8 + h]] + RC)
                vk = k.rot("gvk", [128, 2, 128], 2)
                k.copy(vk[0:L, 0, :], pt[0:L, 0:128], [pt], [vk], eng="act")
                k.ts(vk[0:L, 1, :], pt[0:L, 128:256], tm[0:L, 2, h:h + 1], None, ALU.mult, None, [pt, tm], [vk])
                pk = k.psum()
                k.mm(pk, pk[0:L, 0:128], q[:, 8 + h, cols], Ms[:, h, :], True, [qA[8 + h]] + MsT)
                rr = k.rot("grr", [128, 128], 2)
                k.stt(rr[0:L, :], pk[0:L, 0:128], tm[0:L, 3, h:h + 1], vk[0:L, 0, :], ALU.mult, ALU.add, [pk, tm, vk], [rr])
                k.ts(rr[0:L, :], rr[0:L, :], tm[0:L, 1, h:h + 1], None, ALU.mult, None, [rr, tm], [rr])
                dl = k.rot("gdl", [128, 128], 2)
                tri_solve(w[0:L, 2, 0:L], w[0:L, 3, 0:L], w, rr[0:L, :], rr, L, 128, L, dl[0:L, :], [dl])
                po = k.psum()
                k.mm(po, po[:, 0:L], Ms[:, h, :], qg[:, 0:L], False, MsT + [qg])
                k.mm(po, po[:, 0:L], dl[0:L, :], w[0:L, 4, 0:L], True, [dl, w])
                k.copy(oT[:, h, cols], po[:, 0:L], [po], [oA[h]], eng="act")
                pu = k.psum()
                k.mm(pu, pu[:, 0:128], vk[0:L, 1, :], dl[0:L, :], True, [vk, dl])
                k.stt(Ms[:, h, :], Ms[:, h, :], w[:, 5, L - 1:L], pu[:, 0:128], ALU.mult, ALU.add, MsT + [w, pu], MsT)
            if kind == "s" or tile["last"]:
                gi = 0 if kind == "p" else 1
                si = 0 if kind == "p" else s
                k.dma("pool", o_gdn[gi][i, si].rearrange("h kk v -> kk h v"), Ms[:, 0:8, :], MsT, [])
        for h in range(8):
            sq = k.rot("pc_sq", [128, NMAX], 2)
            k.act(sq[:, 0:N], oT[:, h, 0:N], AF.Square, [oA[h]], [sq])
            ps = k.psum()
            k.mm(ps, ps[:, 0:N], ones, sq[:, 0:N], True, [sq] + RC)
            k.act(sq[:, 0:N], ps[:, 0:N], AF.Sqrt, [ps], [sq], bias=NORM_EPS, scale=1.0 / 128)
            k.recip(sq[:, 0:N], sq[:, 0:N], [sq], [sq])
            k.tt(sq[:, 0:N], sq[:, 0:N], oT[:, h, 0:N], ALU.mult, [sq, oA[h]], [sq])
            k.stt(mixT[:, h, 0:N].bitcast(F32R), sq[:, 0:N], P["ng"][:, 0:1], z[:, 24 + h, 0:N], ALU.mult, ALU.mult, [sq, P["ng"], zc[24 + h]], [mixT])

    def ffn(l, tile, xT, tb):
        kind, nseq, Ls, N = tile["kind"], tile["nseq"], tile["Ls"], tile["N"]
        P = PL[l]
        Ne = nseq * (Ls + 2)
        ge = k.rot("z", [128, 33, NMAX + 4], 1)
        zc = tile["zc"]
        gev = lambda ci: ge.t[:, ci, 0:Ne].rearrange("p (s t) -> p s t", t=Ls + 2)
        if kind == "p":
            k.copy(ge.t[:, 0:22, 0:2], C_fconv[l][:, :, :], [C_fconv[l]], zc[0:22], eng="pool")
        else:
            for c4 in range(0, 22, 4):
                n4 = min(4, 22 - c4)
                if c4 % 8 == 0:
                    wcb = min(1024, D_FF - c4 * 128)
                    cb = k.rot("stg", [128, 1024], 1)
                    k.dma("pool", cb[0:nseq * 2, 0:wcb], st_fconv[l].rearrange("s r c -> (s r) c")[:, c4 * 128:c4 * 128 + wcb], [], [cb])
                ps = k.psum()
                for j in range(n4):
                    k.tr(ps, ps[:, j * 32:j * 32 + nseq * 2], cb[0:nseq * 2, (c4 % 8 + j) * 128:(c4 % 8 + j + 1) * 128],
                         ident[0:nseq * 2, 0:nseq * 2], [cb] + RC)
                for j in range(n4):
                    k.copy(gev(c4 + j)[:, :, 0:2], ps[:, j * 32:j * 32 + nseq * 2].rearrange("p (s r) -> p s r", r=2), [ps], [zc[c4 + j]])
        aT = AT
        aA = AT_T

        def v3(ap):
            return ap.rearrange("p (s t) -> p s t", t=Ls)

        def cons(c_abs, cw, ps):
            ci = c_abs // 128
            if ci < 22:
                k.copy(gev(ci)[:, :, 2:Ls + 2], v3(ps[:, 0:N]), [ps], [zc[ci]], eng="act")
                e = gev(ci)
                tmp = k.rot("ctmp", [128, NMAX], 2)
                k.ts(v3(tmp[:, 0:N]), e[:, :, 0:Ls], P["fcw"][:, 0, ci:ci + 1], None, ALU.mult, None, [zc[ci], P["fcw"]], [tmp])
                for j in (1, 2):
                    k.stt(v3(tmp[:, 0:N]), e[:, :, j:j + Ls], P["fcw"][:, j, ci:ci + 1], v3(tmp[:, 0:N]), ALU.mult, ALU.add,
                          [zc[ci], P["fcw"], tmp], [tmp])
                k.act(AT[:, ci, 0:N].bitcast(F32R), tmp[:, 0:N], AF.Silu, [tmp, P["fcb"]], [aA[ci]], bias=P["fcb"][:, ci:ci + 1])
            else:
                j = ci - 22
                k.tt(AT[:, j, 0:N].bitcast(F32R), AT[:, j, 0:N], ps[:, 0:N], ALU.mult, [aA[j], ps], [aA[j]])

        proj(w_up[l], 8, 0, 2 * D_FF, lambda kk: xT[:, kk, 0:N], [xT], N, cons, 256)
        if kind == "p":
            k.copy(C_fconv[l][:, :, :], ge.t[:, 0:22, Ls:Ls + 2], zc[0:22], [C_fconv[l]], eng="pool")
            if tile["last"]:
                store_rows(lambda c: C_fconv[l][:, c, :], [C_fconv[l]], 22, 2, o_fconv[0][l, 0])
        else:
            store_rows(lambda c: gev(c)[:, :, Ls:Ls + 2], zc[0:22], 22, nseq * 2, o_fconv[1][l].rearrange("s r c -> (s r) c"), view=(nseq, 2))

        def cons2(c_abs, cw, ps):
            m = c_abs // 128
            k.stt(tb[:, m, 0:N], xT[:, m, 0:N], ALPHA, ps[:, 0:N], ALU.mult, ALU.add, [xT, ps], [tb])

        proj(w_down[l], 22, 0, D, lambda kk: AT[:, kk, 0:N], aA, N, cons2, 128)

    tiles = [dict(kind="p", nseq=1, Ls=N_META, N=N_META, r0=0)]
    r0 = N_META
    while r0 < TP:
        tiles.append(dict(kind="p", nseq=1, Ls=128, N=128, r0=r0))
        r0 += 128
    for t in tiles:
        t["last"] = False
    tiles[-1]["last"] = True
    tiles.append(dict(kind="s", nseq=NS, Ls=4, N=NS * 4, r0=0, last=True))
    zbuf0 = None
    for tile in tiles:
        N = tile["N"]
        tile["first_rw"] = False
        src = xp if tile["kind"] == "p" else xs
        xtm = k.rot("stg", [128, 1024], 1)
        k.dma("sp", xtm[0:N, 0:D], src[tile["r0"]:tile["r0"] + N, :], [], [xtm])
        xT = k.rot("xT", [128, 8, NMAX], 2)
        for c4 in (0, 4):
            ps = k.psum()
            for j in range(4):
                k.tr(ps, ps[:, j * 128:j * 128 + N], xtm[0:N, (c4 + j) * 128:(c4 + j + 1) * 128], ident[0:N, 0:N], [xtm] + RC)
            k.copy(xT[:, c4:c4 + 4, 0:N].bitcast(F32R), ps[:, 0:512].rearrange("p (c t) -> p c t", t=128)[:, :, 0:N], [ps], [xT])
        zb = k.rot("z", [128, 33, NMAX + 4], 1)
        if zbuf0 is None:
            zbuf0 = k.alias(zb, 33)
        tile["zc"] = zbuf0
        for l in range(DEPTH):
            i = l // 2
            mixT = k.rot("mixT", [128, 8, NMAX], 1)
            if l % 2 == 0:
                ab_mixer(i, tile, xT, mixT)
                Wo = w_out_ab[i]
            else:
                c_mixer(i, tile, xT, mixT)
                Wo = w_out_c[i]
            tb = k.rot("tb", [128, 8, NMAX], 1)

            def cons_o(c_abs, cw, ps, xT=xT, tb=tb):
                m = c_abs // 128
                k.stt(tb[:, m, 0:N], xT[:, m, 0:N], ALPHA, ps[:, 0:N], ALU.mult, ALU.add, [xT, ps], [tb])

            proj(Wo, 8, 0, D, lambda kk: mixT[:, kk, 0:N], [mixT], N, cons_o, 256)
            x2 = k.rot("xT", [128, 8, NMAX], 2)
            layer_norm(tb, N, PL[l]["lmg"], PL[l]["lmb"], x2)
            tb2 = k.rot("tb", [128, 8, NMAX], 1)
            ffn(l, tile, x2, tb2)
            x3 = k.rot("xT", [128, 8, NMAX], 2)
            layer_norm(tb2, N, PL[l]["lfg"], PL[l]["lfb"], x3)
            xT = x3
        if tile["kind"] == "s":
            store_rows(lambda c: xT[:, c, 0:N], [xT], 8, N, y_s[:, :])
        elif tile["r0"] >= N_META:
            store_rows(lambda c: xT[:, c, 0:N], [xT], 8, N, y_p[tile["r0"] - N_META:tile["r0"] - N_META + N, :])
    k.finish()
    stats = (k.n_inst, k.n_wait)
    es.close()
    return nc, stats


_W_NAMES = ["w_in_ab", "gla_gate_w2", "gla_gate_b", "gla_norm_g", "rwkv_mu", "rwkv_w0", "rwkv_w2", "rwkv_a0", "rwkv_a2",
            "rwkv_g2", "rwkv_k_k", "rwkv_k_a", "rwkv_r_k", "rwkv_ln_g", "rwkv_ln_b", "w_out_ab", "w_in_c", "gdn_conv_w",
            "gdn_A_log", "gdn_dt_bias", "gdn_norm_g", "w_out_c", "w_up", "ffn_conv_w", "ffn_conv_b", "w_down",
            "ln_mix_g", "ln_mix_b", "ln_ffn_g", "ln_ffn_b"]


def make_in_maps(inputs, n_cores, depth):
    f = lambda a: np.ascontiguousarray(np.asarray(a, dtype=np.float32))
    xprompt = f(inputs["x_prompt"])
    B, SEQ, _ = xprompt.shape
    xsample = f(inputs["x_sample"])
    NS = xsample.shape[0] // n_cores
    meta = f(inputs["meta_tokens"])
    consts = make_consts()
    shared = {"consts": consts}
    for nm in _W_NAMES:
        a = f(inputs[nm])
        if nm == "rwkv_r_k":
            a = a.reshape(a.shape[0], 512)
        shared[nm] = a
    maps = []
    for c in range(n_cores):
        m = dict(shared)
        m["xp"] = np.ascontiguousarray(np.concatenate([meta, xprompt[c]], axis=0))
        sl = slice(c * NS, (c + 1) * NS)
        m["xs"] = np.ascontiguousarray(xsample[sl].reshape(NS * 4, D))
        m["st_gla"] = f(inputs["state_gla"][:, sl])
        m["st_rwkv"] = f(inputs["state_rwkv"][:, sl])
        m["st_shift"] = f(inputs["state_rwkv_shift"][:, sl])
        m["st_gdn"] = f(inputs["state_gdn"][:, sl])
        m["st_gconv"] = f(inputs["state_gdn_conv"][:, sl])
        m["st_fconv"] = f(inputs["state_ffn_conv"][:, sl])
        maps.append(m)
    return maps, SEQ, NS


def gather(results, n_cores, NS):
    cat = lambda nm, ax: np.concatenate([np.asarray(r[nm]) for r in results], axis=ax)
    y_p = np.stack([np.asarray(r["y_p"]) for r in results], axis=0)
    y_s = cat("y_s", 0).reshape(n_cores * NS, 4, D)
    outs = [y_p, y_s]
    for g in ("p", "s"):
        for nm in ("gla", "rwkv", "shift", "gdn", "gconv", "fconv"):
            outs.append(cat("%s_%s" % (g, nm), 1))
    return tuple(np.ascontiguousarray(o, dtype=np.float32) for o in outs)


def kernel(**inputs):
    n_cores = 8
    maps, SEQ, NS = make_in_maps(inputs, n_cores, DEPTH_FULL)
    nc, _ = build(SEQ, NS, DEPTH_FULL)
    res = run_bass_kernel_spmd(nc, maps, core_ids=list(range(n_cores)))
    return gather(res.results, n_cores, NS)
```

```python
import math
import sys
from contextlib import ExitStack

import numpy as np
import concourse.bass as bass
import concourse.mybir as mybir
from concourse.bass_utils import run_bass_kernel_spmd

F32 = mybir.dt.float32
F32R = mybir.dt.float32r
AF = mybir.ActivationFunctionType
ALU = mybir.AluOpType

D = 1024
DEPTH_FULL = 4
N_META = 16
GLA_COLS = 1552
AB_COLS = 3344
C_COLS = 4112
D_FF = 2816
ALPHA = (2.0 * DEPTH_FULL) ** 0.25
LN_EPS = 1e-5
NORM_EPS = 1e-6
GN_EPS = 64e-5
C0 = math.exp(-0.5)


class Buf:
    __slots__ = ("name", "t", "w", "r", "fresh")

    def __init__(self, name, t):
        self.name = name
        self.t = t
        self.w = None
        self.r = {}
        self.fresh = True

    def __getitem__(self, idx):
        return self.t[idx]


class K:
    ENG = ("pe", "act", "dve", "pool")

    def __init__(self, nc, es, n_dma_sems=8):
        self.nc = nc
        self.es = es
        self.engobj = {"pe": nc.tensor, "act": nc.scalar, "dve": nc.vector, "pool": nc.gpsimd, "sp": nc.sync}
        self.sem = {}
        self.cnt = {}
        for e in self.ENG:
            self.sem[e] = es.enter_context(nc.semaphore("sem_" + e))
            self.cnt[e] = 0
        self.dq = {}
        for q in ("sp", "pool"):
            keys = []
            for j in range(n_dma_sems):
                k = "d_%s_%d" % (q, j)
                self.sem[k] = es.enter_context(nc.semaphore(k))
                self.cnt[k] = 0
                keys.append(k)
            self.dq[q] = [keys, 0]
        self.known = {e: {} for e in ("pe", "act", "dve", "pool", "sp")}
        self.n_inst = 0
        self.n_wait = 0
        self.tags = None
        self._psum = []
        self._psum_i = 0
        self._uid = 0
        self._rot = {}
        self._alias = {}

    def sbuf(self, name, shape, dtype=F32):
        self._uid += 1
        t = self.es.enter_context(self.nc.sbuf_tensor("%s_%d" % (name, self._uid), list(shape), dtype))
        return Buf(name, t)

    def alias(self, buf, n):
        key = (id(buf), n)
        if key not in self._alias:
            self._alias[key] = [Buf("%s_%d" % (buf.name, i), buf.t) for i in range(n)]
        return self._alias[key]

    def init_psum(self, n=8):
        for i in range(n):
            t = self.es.enter_context(self.nc.psum_tensor("psb%d" % i, [128, 512], F32))
            self._psum.append(Buf("ps%d" % i, t))

    def psum(self):
        b = self._psum[self._psum_i % len(self._psum)]
        self._psum_i += 1
        b.fresh = True
        return b

    def rot(self, name, shape, n, dtype=F32):
        if name not in self._rot:
            self._rot[name] = [[self.sbuf(name, shape, dtype) for _ in range(n)], 0]
        lst = self._rot[name]
        b = lst[0][lst[1] % len(lst[0])]
        lst[1] += 1
        return b

    def _deps(self, eng, reads, writes, is_dma=False):
        deps = {}

        def add(k, v, war):
            if k == eng and not is_dma:
                if eng == "pe":
                    return
            if deps.get(k, 0) < v:
                deps[k] = v

        for b in reads:
            if b.w is not None:
                add(b.w[0], b.w[1], False)
        for b in writes:
            if b.w is not None:
                add(b.w[0], b.w[1], False)
            for k, v in b.r.items():
                add(k, v, True)
        return deps

    def _wait(self, eng, deps):
        kn = self.known[eng]
        eo = self.engobj[eng]
        for k, v in deps.items():
            if kn.get(k, 0) < v:
                eo.wait_ge(self.sem[k], v)
                kn[k] = v
                self.n_wait += 1

    def op(self, eng, fn, reads=(), writes=()):
        deps = self._deps(eng, reads, writes)
        self._wait(eng, deps)
        inst = fn(self.engobj[eng])
        self.cnt[eng] += 1
        c = self.cnt[eng]
        if self.tags is not None:
            self.tags[eng].append(sys._getframe(2).f_lineno)
        inst.then_inc(self.sem[eng], 1)
        self.n_inst += 1
        for b in reads:
            if b.r.get(eng, 0) < c:
                b.r[eng] = c
        for b in writes:
            b.w = (eng, c)
            b.r = {}
        return inst

    def dma(self, q, out, in_, reads=(), writes=(), **kw):
        deps = self._deps(q, reads, writes, True)
        keys, i = self.dq[q]
        k = keys[i % len(keys)]
        self.dq[q][1] = i + 1
        if self.cnt[k] > 0:
            deps[k] = max(deps.get(k, 0), self.cnt[k])
        self._wait(q, deps)
        inst = self.engobj[q].dma_start(out=out, in_=in_, **kw)
        self.cnt[k] += 16
        c = self.cnt[k]
        inst.then_inc(self.sem[k], 16)
        self.n_inst += 1
        for b in reads:
            if b.r.get(k, 0) < c:
                b.r[k] = c
        for b in writes:
            b.w = (k, c)
            b.r = {}
        return inst

    def finish(self):
        deps = {}
        for q in self.dq:
            for k in self.dq[q][0]:
                if self.cnt[k] > 0:
                    deps[k] = self.cnt[k]
        for e in self.ENG:
            if self.cnt[e] > 0:
                deps[e] = self.cnt[e]
        self._wait("sp", deps)

    def mm(self, bank, out, lhsT, rhs, last, reads):
        st = bank.fresh
        bank.fresh = False
        return self.op("pe", lambda e: e.matmul(out, lhsT=lhsT, rhs=rhs, start=st, stop=last), reads, [bank])

    def tr(self, bank, out, in_, ident, reads):
        bank.fresh = False
        return self.op("pe", lambda e: e.transpose(out, in_, ident), reads, [bank])

    def act(self, out, in_, func, reads, writes, bias=None, scale=None):
        kw = {}
        if bias is not None:
            kw["bias"] = bias
        if scale is not None:
            kw["scale"] = scale
        return self.op("act", lambda e: e.activation(out=out, in_=in_, func=func, **kw), reads, writes)

    def tt(self, out, in0, in1, op, reads, writes, eng="dve"):
        return self.op(eng, lambda e: e.tensor_tensor(out=out, in0=in0, in1=in1, op=op), reads, writes)

    def ts(self, out, in0, s1, s2, op0, op1, reads, writes, eng="dve"):
        if op1 is None:
            return self.op(eng, lambda e: e.tensor_scalar(out=out, in0=in0, scalar1=s1, scalar2=None, op0=op0), reads, writes)
        return self.op(eng, lambda e: e.tensor_scalar(out=out, in0=in0, scalar1=s1, scalar2=s2, op0=op0, op1=op1), reads, writes)

    def stt(self, out, in0, scalar, in1, op0, op1, reads, writes):
        return self.op("dve", lambda e: e.scalar_tensor_tensor(out=out, in0=in0, scalar=scalar, in1=in1, op0=op0, op1=op1), reads, writes)

    def copy(self, out, in_, reads, writes, eng="dve"):
        if eng == "act":
            return self.op("act", lambda e: e.copy(out=out, in_=in_), reads, writes)
        return self.op(eng, lambda e: e.tensor_copy(out=out, in_=in_), reads, writes)

    def memset(self, out, val, writes, eng="pool"):
        return self.op(eng, lambda e: e.memset(out, val), (), writes)

    def recip(self, out, in_, reads, writes):
        return self.op("dve", lambda e: e.reciprocal(out=out, in_=in_), reads, writes)

    def scan(self, out, data0, data1, reads, writes):
        return self.op("dve", lambda e: e.tensor_tensor_scan(out=out, data0=data0, data1=data1, initial=0.0,
                                                             op0=ALU.mult, op1=ALU.add), reads, writes)


class Slot:
    __slots__ = ("fn", "tok")

    def __init__(self, fn, tok):
        self.fn = fn
        self.tok = tok

    def __call__(self, rows, cols):
        return self.fn(rows, cols)


def slots3(buf, toks, idxs):
    return [Slot((lambda r, c, j=j: buf.t[r, j, c]), (toks[j] if isinstance(toks, list) else toks)) for j in idxs]


def slots4(buf, toks, idxs):
    return [Slot((lambda r, c, a=j // 2, b=j % 2: buf.t[r, a, b, c]), (toks[j // 2] if isinstance(toks, list) else toks)) for j in idxs]


C_ID, C_ONES, C_BO, C_IU, C_SU, C_SL, C_NSL, C_NSU = [i * 128 for i in range(8)]
C_RST = 8 * 128
C_SEL = 9 * 128
C_W = C_SEL


def make_consts():
    c = np.zeros((128, C_W), np.float32)
    i = np.arange(128)
    c[:, C_ID:C_ID + 128] = np.eye(128)
    c[:, C_ONES:C_ONES + 128] = 1.0
    bo = (i[:, None] // 64 == i[None, :] // 64).astype(np.float32)
    c[:, C_BO:C_BO + 128] = bo
    c[:, C_IU:C_IU + 128] = (i[:, None] <= i[None, :])
    c[:, C_SU:C_SU + 128] = (i[:, None] < i[None, :])
    c[:, C_SL:C_SL + 128] = (i[:, None] > i[None, :])
    c[:, C_NSL:C_NSL + 128] = -(i[:, None] > i[None, :]).astype(np.float32)
    c[:, C_NSU:C_NSU + 128] = -(i[:, None] < i[None, :]).astype(np.float32)
    c[:, C_RST:C_RST + 128] = (i[None, :] % 4 != 0)
    return c


TAGS = None
SEQ_GDN = False
SEQ_RW = False


def build(SEQ, NS, DEPTH):
    N_AB = (DEPTH + 1) // 2
    N_C = DEPTH // 2
    TP = N_META + SEQ
    nc = bass.Bass("TRN2", target_bir_lowering=False)
    nc.dge_precook = False

    def din(name, shape, dt=F32):
        return nc.dram_tensor(name, list(shape), dt, kind="ExternalInput").ap()

    def dout(name, shape):
        return nc.dram_tensor(name, list(shape), F32, kind="ExternalOutput").ap()

    xp = din("xp", [TP, D])
    xs = din("xs", [NS * 4, D])
    st_gla = din("st_gla", [N_AB, NS, 4, 64, 128])
    st_rwkv = din("st_rwkv", [N_AB, NS, 8, 64, 64])
    st_shift = din("st_shift", [N_AB, NS, D])
    st_gdn = din("st_gdn", [max(N_C, 1), NS, 8, 128, 128])
    st_gconv = din("st_gconv", [max(N_C, 1), NS, 3, 3072])
    st_fconv = din("st_fconv", [DEPTH, NS, 2, D_FF])
    consts = din("consts", [128, C_W])
    w_in_ab = din("w_in_ab", [N_AB, D, AB_COLS], F32R)
    gla_gate_w2 = din("gla_gate_w2", [N_AB, 16, 256])
    gla_gate_b = din("gla_gate_b", [N_AB, 256])
    gla_norm_g = din("gla_norm_g", [N_AB, 512])
    rwkv_mu = din("rwkv_mu", [N_AB, 1792])
    rwkv_w0 = din("rwkv_w0", [N_AB, 512])
    rwkv_w2 = din("rwkv_w2", [N_AB, 64, 512])
    rwkv_a0 = din("rwkv_a0", [N_AB, 512])
    rwkv_a2 = din("rwkv_a2", [N_AB, 64, 512])
    rwkv_g2 = din("rwkv_g2", [N_AB, 128, 512])
    rwkv_k_k = din("rwkv_k_k", [N_AB, 512])
    rwkv_k_a = din("rwkv_k_a", [N_AB, 512])
    rwkv_r_k = din("rwkv_r_k", [N_AB, 512])
    rwkv_ln_g = din("rwkv_ln_g", [N_AB, 512])
    rwkv_ln_b = din("rwkv_ln_b", [N_AB, 512])
    w_out_ab = din("w_out_ab", [N_AB, D, D], F32R)
    w_in_c = din("w_in_c", [max(N_C, 1), D, C_COLS], F32R)
    gdn_conv_w = din("gdn_conv_w", [max(N_C, 1), 4, 3072])
    gdn_A_log = din("gdn_A_log", [max(N_C, 1), 8])
    gdn_dt_bias = din("gdn_dt_bias", [max(N_C, 1), 8])
    gdn_norm_g = din("gdn_norm_g", [max(N_C, 1), 128])
    w_out_c = din("w_out_c", [max(N_C, 1), D, D], F32R)
    w_up = din("w_up", [DEPTH, D, 2 * D_FF], F32R)
    ffn_conv_w = din("ffn_conv_w", [DEPTH, 3, D_FF])
    ffn_conv_b = din("ffn_conv_b", [DEPTH, D_FF])
    w_down = din("w_down", [DEPTH, D_FF, D], F32R)
    ln_mix_g = din("ln_mix_g", [DEPTH, D])
    ln_mix_b = din("ln_mix_b", [DEPTH, D])
    ln_ffn_g = din("ln_ffn_g", [DEPTH, D])
    ln_ffn_b = din("ln_ffn_b", [DEPTH, D])

    y_p = dout("y_p", [SEQ, D])
    y_s = dout("y_s", [NS * 4, D])
    o_gla = [dout("p_gla", [N_AB, 1, 4, 64, 128]), dout("s_gla", [N_AB, NS, 4, 64, 128])]
    o_rwkv = [dout("p_rwkv", [N_AB, 1, 8, 64, 64]), dout("s_rwkv", [N_AB, NS, 8, 64, 64])]
    o_shift = [dout("p_shift", [N_AB, 1, D]), dout("s_shift", [N_AB, NS, D])]
    o_gdn = [dout("p_gdn", [max(N_C, 1), 1, 8, 128, 128]), dout("s_gdn", [max(N_C, 1), NS, 8, 128, 128])]
    o_gconv = [dout("p_gconv", [max(N_C, 1), 1, 3, 3072]), dout("s_gconv", [max(N_C, 1), NS, 3, 3072])]
    o_fconv = [dout("p_fconv", [DEPTH, 1, 2, D_FF]), dout("s_fconv", [DEPTH, NS, 2, D_FF])]

    es = ExitStack()
    k = K(nc, es)
    if TAGS is not None:
        k.tags = TAGS
    k.init_psum()
    NMAX = 128

    cst = k.sbuf("cst", [128, C_W])
    k.dma("sp", cst[:, :], consts[:, :], [], [cst])
    RC = [cst]

    def cm(off, r, c):
        return cst[0:r, off:off + c]

    ident = cst[:, C_ID:C_ID + 128]
    ones = cst[:, C_ONES:C_ONES + 128]
    bo64 = cst[:, C_BO:C_BO + 128]

    def load_fm(name, src, nch):
        b = k.sbuf(name, [128, nch])
        k.dma("pool", b[:, :], src.rearrange("(c p) -> p c", p=128), [], [b], allow_slow_non_contiguous=True)
        return b

    PAB = []
    for i in range(N_AB):
        p = {}
        p["gb"] = load_fm("gb", gla_gate_b[i], 2)
        p["gng"] = load_fm("gng", gla_norm_g[i], 4)
        p["mu"] = load_fm("mu", rwkv_mu[i], 14)
        for nm, src in (("w0", rwkv_w0), ("a0", rwkv_a0), ("kk", rwkv_k_k), ("ka", rwkv_k_a), ("rk", rwkv_r_k),
                        ("lg", rwkv_ln_g), ("lb", rwkv_ln_b)):
            p[nm] = load_fm(nm, src[i], 4)
        p["gw2"] = k.sbuf("gw2", [16, 256])
        k.dma("pool", p["gw2"][:, :], gla_gate_w2[i], [], [p["gw2"]])
        p["w2a2"] = k.sbuf("w2a2", [128, 512])
        k.dma("pool", p["w2a2"][0:64, :], rwkv_w2[i], [], [p["w2a2"]])
        k.dma("pool", p["w2a2"][64:128, :], rwkv_a2[i], [], [p["w2a2"]])
        p["g2"] = k.sbuf("g2", [128, 512])
        k.dma("pool", p["g2"][:, :], rwkv_g2[i], [], [p["g2"]])
        p["ngb"] = k.sbuf("ngb", [128, 2])
        k.ts(p["ngb"][:, :], p["gb"][:, :], -1.0, None, ALU.mult, None, [p["gb"]], [p["ngb"]])
        p["omka"] = k.sbuf("omka", [128, 4])
        k.ts(p["omka"][:, :], p["ka"][:, :], -1.0, 1.0, ALU.mult, ALU.add, [p["ka"]], [p["omka"]])
        PAB.append(p)
    PC = []
    for i in range(N_C):
        p = {}
        p["cw"] = k.sbuf("gcw", [128, 4, 24])
        for j in range(4):
            k.dma("pool", p["cw"][:, j, :], gdn_conv_w[i, j].rearrange("(c p) -> p c", p=128), [], [p["cw"]],
                  allow_slow_non_contiguous=True)
        p["alog"] = k.sbuf("alog", [8, 1])
        k.dma("pool", p["alog"][:, :], gdn_A_log[i].rearrange("(p o) -> p o", o=1), [], [p["alog"]])
        p["dtb"] = k.sbuf("dtb", [8, 1])
        k.dma("pool", p["dtb"][:, :], gdn_dt_bias[i].rearrange("(p o) -> p o", o=1), [], [p["dtb"]])
        p["nexpA"] = k.sbuf("nexpA", [8, 1])
        k.act(p["nexpA"][:, :], p["alog"][:, :], AF.Exp, [p["alog"]], [p["nexpA"]])
        k.ts(p["nexpA"][:, :], p["nexpA"][:, :], -1.0, None, ALU.mult, None, [p["nexpA"]], [p["nexpA"]])
        p["ng"] = k.sbuf("gdng", [128, 1])
        k.dma("pool", p["ng"][:, :], gdn_norm_g[i].rearrange("(p o) -> p o", o=1), [], [p["ng"]])
        p["wba"] = k.sbuf("wba", [128, 8, 16])
        k.dma("pool", p["wba"][:, :, :], w_in_c[i][:, 4096:4112].rearrange("(k p) c -> p k c", p=128).bitcast(F32), [], [p["wba"]])
        PC.append(p)
    PL = []
    for l in range(DEPTH):
        p = {}
        p["fcw"] = k.sbuf("fcw", [128, 3, 22])
        for j in range(3):
            k.dma("pool", p["fcw"][:, j, :], ffn_conv_w[l, j].rearrange("(c p) -> p c", p=128), [], [p["fcw"]],
                  allow_slow_non_contiguous=True)
        p["fcb"] = load_fm("fcb", ffn_conv_b[l], 22)
        p["lmg"] = load_fm("lmg", ln_mix_g[l], 8)
        p["lmb"] = load_fm("lmb", ln_mix_b[l], 8)
        p["lfg"] = load_fm("lfg", ln_ffn_g[l], 8)
        p["lfb"] = load_fm("lfb", ln_ffn_b[l], 8)
        PL.append(p)

    def zeros(name, shape):
        b = k.sbuf(name, shape)
        k.memset(b.t[:], 0.0, [b])
        return b

    P_gla = [zeros("pgla", [128, 2, 128]) for _ in range(N_AB)]
    P_rwkv = [zeros("prwkv", [128, 4, 128]) for _ in range(N_AB)]
    P_gdn = [zeros("pgdn", [128, 8, 128]) for _ in range(N_C)]
    C_xlast = [zeros("cxl", [128, 8, 1]) for _ in range(N_AB)]
    C_gconv = [zeros("cgc", [128, 24, 3]) for _ in range(N_C)]
    C_fconv = [zeros("cfc", [128, 22, 2]) for _ in range(DEPTH)]
    RAW = [zeros("srw_raw", [128, 4, 128]) for _ in range(1)]
    upad = zeros("upad", [128, 8, 128])
    vpad = zeros("vpad", [128, 8, 128])

    XE0 = k.rot("xe", [128, 8, NMAX + 16], 1)
    k.copy(XE0.t[:, :, 0:128].bitcast(F32R), upad[:, :, :], [upad], [XE0], eng="pool")
    k.copy(XE0.t[:, :, 128:144].bitcast(F32R), upad[:, :, 0:16], [upad], [XE0], eng="pool")
    WT_ELEMS = 2816
    GL = k.sbuf("gla", [128, 16, NMAX])
    GL_T = k.alias(GL, 16)
    AT = k.sbuf("aT", [128, 22, NMAX])
    AT_T = k.alias(AT, 22)
    ARENA = k.sbuf("arena", [128, 48, NMAX])
    ARENA_T = k.alias(ARENA, 48)

    def proj(W, kc, c0, ncols, rhs_fn, rbufs, N, consume, gw):
        g0 = c0
        while g0 < c0 + ncols:
            w = min(gw, c0 + ncols - g0)
            wt = k.rot("wt", [128, WT_ELEMS], 2, F32R)
            wvr = wt.t[:, 0:kc * w].rearrange("p (k c) -> p k c", c=w)
            wv = wvr.bitcast(F32)
            k.dma("sp", wvr, W[:, g0:g0 + w].rearrange("(k p) c -> p k c", p=128), [], [wt])
            j0 = 0
            while j0 < w:
                cw = min(128, w - j0)
                ps = k.psum()
                fast = (cw == 128 and N % 2 == 0)
                for kk in range(kc):
                    lh, rh = wv[:, kk, j0:j0 + cw], rhs_fn(kk)
                    if fast:
                        lh, rh = wvr[:, kk, j0:j0 + cw], rh.bitcast(F32R)
                    k.mm(ps, ps[0:cw, 0:N], lh, rh, kk == kc - 1, [wt] + rbufs)
                consume(g0 + j0, cw, ps)
                j0 += cw
            g0 += w

    def store_rows(src_fn, sbufs, nch, ncols, dst, view=None):
        for c8 in range(0, nch, 8):
            n8 = min(8, nch - c8)
            stg = k.rot("stg", [128, 1024], 1)
            for c4 in range(c8, c8 + n8, 4):
                ps = k.psum()
                n4 = min(4, c8 + n8 - c4)
                for j in range(n4):
                    if view is not None:
                        sl_ = 8 + j % 2
                        k.copy(GL[:, sl_, 0:ncols].rearrange("p (a b) -> p a b", b=view[1]), src_fn(c4 + j), sbufs, [GL_T[sl_]], eng="pool")
                        k.tr(ps, ps[0:ncols, j * 128:(j + 1) * 128], GL[:, sl_, 0:ncols], ident, [GL_T[sl_]] + RC)
                        continue
                    k.tr(ps, ps[0:ncols, j * 128:(j + 1) * 128], src_fn(c4 + j), ident, sbufs + RC)
                k.copy(stg[0:ncols, (c4 - c8) * 128:(c4 - c8 + n4) * 128], ps[0:ncols, 0:n4 * 128], [ps], [stg],
                       eng="act" if (c4 // 4) % 2 else "dve")
            k.dma("pool", dst[:, c8 * 128:(c8 + n8) * 128], stg[0:ncols, 0:n8 * 128], [stg], [])

    def layer_norm(tb, N, g, b, xo):
        sq = k.rot("oT", [128, 8, NMAX], 1)
        k.act(sq[:, :, 0:N], tb[:, :, 0:N], AF.Square, [tb], [sq])
        ps1 = k.psum()
        for m in range(8):
            k.mm(ps1, ps1[:, 0:N], ones, tb[:, m, 0:N], m == 7, [tb] + RC)
        ps2 = k.psum()
        for m in range(8):
            k.mm(ps2, ps2[:, 0:N], ones, sq[:, m, 0:N], m == 7, [sq] + RC)
        st = k.rot("pc_st", [128, 4, NMAX], 1)
        mean, var, rstd = st[:, 0, 0:N], st[:, 1, 0:N], st[:, 2, 0:N]
        k.ts(mean, ps1[:, 0:N], 1.0 / D, None, ALU.mult, None, [ps1], [st])
        k.tt(st[:, 3, 0:N], mean, mean, ALU.mult, [st], [st])
        k.stt(var, ps2[:, 0:N], 1.0 / D, st[:, 3, 0:N], ALU.mult, ALU.subtract, [ps2, st], [st])
        k.ts(var, var, LN_EPS, None, ALU.add, None, [st], [st])
        k.act(var, var, AF.Sqrt, [st], [st])
        k.recip(rstd, var, [st], [st])
        for m in range(8):
            k.tt(sq[:, m, 0:N], tb[:, m, 0:N], mean, ALU.subtract, [tb, st], [sq])
            k.tt(sq[:, m, 0:N], sq[:, m, 0:N], rstd, ALU.mult, [sq, st], [sq])
            k.ts(xo[:, m, 0:N].bitcast(F32R), sq[:, m, 0:N], g[:, m:m + 1], b[:, m:m + 1], ALU.mult, ALU.add, [sq, g, b], [xo])

    def tri_solve(Pp, Pm, pbuf, R0, rbuf, L, w, Ls_nil, final_out, final_bufs):
        nlev = 0
        while (1 << nlev) < Ls_nil:
            nlev += 1
        cur_p, cur_m, cur_r, cur_rb = Pp, Pm, R0, rbuf
        cur_pb = pbuf
        if nlev == 0:
            k.copy(final_out, cur_r, [cur_rb], final_bufs)
            return
        for j in range(nlev):
            ps = k.psum()
            k.mm(ps, ps[0:L, 0:w], cur_p, cur_r, True, [cur_pb, cur_rb])
            if j == nlev - 1:
                k.tt(final_out, ps[0:L, 0:w], cur_r, ALU.add, [ps, cur_rb], final_bufs)
                break
            nr = k.rot("solr", [128, 128], 2)
            k.tt(nr[0:L, 0:w], ps[0:L, 0:w], cur_r, ALU.add, [ps, cur_rb], [nr])
            psq = k.psum()
            k.mm(psq, psq[0:L, 0:L], cur_m, cur_p, False, [cur_pb])
            k.mm(psq, psq[0:L, 128:128 + L], cur_p, cur_m, True, [cur_pb])
            npb = k.rot("solp", [128, 256], 2)
            k.copy(npb[0:L, 0:L], psq[0:L, 0:L], [psq], [npb], eng="act")
            k.copy(npb[0:L, 128:128 + L], psq[0:L, 128:128 + L], [psq], [npb], eng="act")
            cur_p, cur_m, cur_pb = npb[0:L, 0:L], npb[0:L, 128:128 + L], npb
            cur_r, cur_rb = nr[0:L, 0:w], nr

    def tri_solve_multi(jobs, Ls_nil):
        nlev = 0
        while (1 << nlev) < Ls_nil:
            nlev += 1
        if nlev == 0:
            for jb in jobs:
                k.copy(jb["out"], jb["R0"], jb["rtoks"], jb["otoks"])
            return
        for jb in jobs:
            jb["cp"], jb["cm"], jb["cpt"] = jb["Pp"], jb["Pm"], jb["ptoks"]
            jb["cr"], jb["crt"] = jb["R0"], jb["rtoks"]
        for j in range(nlev):
            lastlev = (j == nlev - 1)
            pss = []
            for jb in jobs:
                L, w = jb["L"], jb["w"]
                ps = k.psum()
                k.mm(ps, ps[0:L, 0:w], jb["cp"], jb["cr"], True, jb["cpt"] + jb["crt"])
                pss.append(ps)
            for jb, ps in zip(jobs, pss):
                L, w = jb["L"], jb["w"]
                if lastlev:
                    k.tt(jb["out"], ps[0:L, 0:w], jb["cr"], ALU.add, [ps] + jb["crt"], jb["otoks"])
                else:
                    rs = jb["RA"] if j % 2 == 0 else jb["RB"]
                    k.tt(rs(slice(0, L), slice(0, w)), ps[0:L, 0:w], jb["cr"], ALU.add, [ps] + jb["crt"], [rs.tok])
                    jb["nr"], jb["nrt"] = rs(slice(0, L), slice(0, w)), [rs.tok]
            if lastlev:
                break
            pqs = []
            for jb in jobs:
                L = jb["L"]
                psq = k.psum()
                k.mm(psq, psq[0:L, 0:L], jb["cm"], jb["cp"], False, jb["cpt"])
                k.mm(psq, psq[0:L, 128:128 + L], jb["cp"], jb["cm"], True, jb["cpt"])
                pqs.append(psq)
            for ji, (jb, psq) in enumerate(zip(jobs, pqs)):
                L = jb["L"]
                pp = jb["PA"] if j % 2 == 0 else jb["PB"]
                k.copy(pp[0](slice(0, L), slice(0, L)), psq[0:L, 0:L], [psq], [pp[0].tok], eng="act")
                k.copy(pp[1](slice(0, L), slice(0, L)), psq[0:L, 128:128 + L], [psq], [pp[1].tok], eng="act")
                jb["cp"], jb["cm"] = pp[0](slice(0, L), slice(0, L)), pp[1](slice(0, L), slice(0, L))
                jb["cpt"] = [pp[0].tok, pp[1].tok]
                jb["cr"], jb["crt"] = jb["nr"], jb["nrt"]

    def ab_mixer(i, tile, xT, mixT):
        kind, nseq, Ls, N = tile["kind"], tile["nseq"], tile["Ls"], tile["N"]
        P = PAB[i]
        Ne = nseq * (Ls + 1)
        xe = k.rot("xe", [128, 8, NMAX + 16], 1)
        if not tile.get("xe_init_done"):
            pass
        xev = xe.t[:, :, 0:Ne].rearrange("p c (s t) -> p c s t", t=Ls + 1)
        if kind == "p":
            k.copy(xev[:, :, 0, 0:1].bitcast(F32R), C_xlast[i][:, :, :], [C_xlast[i]], [xe], eng="pool")
        else:
            xl = k.rot("stg", [128, 1024], 1)
            k.dma("pool", xl[0:nseq, 0:D], st_shift[i], [], [xl])
            for c4 in (0, 4):
                ps = k.psum()
                for j in range(4):
                    k.tr(ps, ps[:, j * nseq:(j + 1) * nseq], xl[0:nseq, (c4 + j) * 128:(c4 + j + 1) * 128],
                         ident[0:nseq, 0:nseq], [xl] + RC)
                k.copy(xev[:, c4:c4 + 4, :, 0].bitcast(F32R), ps[:, 0:4 * nseq].rearrange("p (c s) -> p c s", s=nseq), [ps], [xe])
        k.copy(xev[:, :, :, 1:Ls + 1].bitcast(F32R), xT[:, :, 0:N].rearrange("p c (s t) -> p c s t", t=Ls), [xT], [xe], eng="pool")
        if kind == "p":
            k.copy(C_xlast[i][:, :, :], xT[:, :, N - 1:N], [xT, xe], [C_xlast[i]], eng="pool")
            if tile["last"]:
                store_rows(lambda c: C_xlast[i][:, c, :], [C_xlast[i]], 8, 1, o_shift[0][i])
        else:
            xlv = xT[:, :, 0:N].rearrange("p c (s t) -> p c s t", t=Ls)
            store_rows(lambda c: xlv[:, c, :, Ls - 1], [xT], 8, nseq, o_shift[1][i])

        z = k.rot("z", [128, 33, NMAX + 4], 1)
        zc = tile["zc"]

        def cons(c_abs, cw, ps):
            if c_abs < 1024:
                ci = c_abs // 128
            elif c_abs == 1024:
                ci = 8
            else:
                ci = 9 + (c_abs - 1040) // 128
            k.copy(z[0:cw, ci, 0:Ne], ps[0:cw, 0:Ne], [ps], [zc[ci]], eng="act")

        W = w_in_ab[i]
        Np = Ne + (Ne % 2)
        proj(W, 8, 0, 1024, lambda kk: xe[:, kk, 0:Np], [xe], Np, cons, 256)
        proj(W, 8, 1024, 16, lambda kk: xe[:, kk, 0:Np], [xe], Np, cons, 256)
        proj(W, 8, 1040, AB_COLS - 1040, lambda kk: xe[:, kk, 0:Np], [xe], Np, cons, 256)

        def zv(ci, rows=slice(0, 128)):
            return z.t[rows, ci, 0:Ne].rearrange("p (s t) -> p s t", t=Ls + 1)[:, :, 1:Ls + 1]

        def zp(ci, rows=slice(0, 128)):
            return z.t[rows, ci, 0:Ne].rearrange("p (s t) -> p s t", t=Ls + 1)[:, :, 0:Ls]

        def v3(ap):
            return ap.rearrange("p (s t) -> p s t", t=Ls)

        g = GL
        gA = GL_T
        for c in range(2):
            ps = k.psum()
            k.mm(ps, ps[:, 0:Ne], P["gw2"][0:16, c * 128:(c + 1) * 128], z[0:16, 8, 0:Ne], True, [P["gw2"], zc[8]])
            pv = ps[:, 0:Ne].rearrange("p (s t) -> p s t", t=Ls + 1)[:, :, 1:Ls + 1]
            k.act(v3(g[:, 10 + c, 0:N]), pv, AF.Exp, [ps, P["ngb"]], [gA[10 + c]], bias=P["ngb"][:, c:c + 1], scale=-1.0)
            k.act(g[:, 10 + c, 0:N], g[:, 10 + c, 0:N], AF.Ln, [gA[10 + c]], [gA[10 + c]], bias=1.0)
            rst = ones[:, 0:N] if kind == "p" else cst[:, C_RST:C_RST + N]
            k.scan(g[:, c, 0:N], rst, g[:, 10 + c, 0:N], [gA[10 + c]] + RC, [gA[c]])
            k.act(g[:, 2 + c, 0:N], g[:, c, 0:N], AF.Exp, [gA[c]], [gA[2 + c]], scale=-1.0 / 16)
            k.act(g[:, 4 + c, 0:N], g[:, c, 0:N], AF.Exp, [gA[c]], [gA[4 + c]], scale=1.0 / 16)
            k.stt(v3(g[:, 6 + c, 0:N]), zv(c), 0.125, v3(g[:, 2 + c, 0:N]), ALU.mult, ALU.mult, [zc[c], gA[2 + c]], [gA[6 + c]])
            k.tt(v3(g[:, 8 + c, 0:N]), zv(2 + c), v3(g[:, 4 + c, 0:N]), ALU.mult, [zc[2 + c], gA[4 + c]], [gA[8 + c]])
        for h in range(4):
            k.act(v3(g[:, 12 + h, 0:N]), zv(9 + h), AF.Silu, [zc[9 + h]], [gA[12 + h]])

        r = ARENA
        rA = ARENA_T
        zm = k.rot("zm", [128, 14, NMAX], 1)
        zmA = k.alias(zm, 14)
        for c in range(14):
            ci = 13 + c
            k.tt(v3(zm[:, c, 0:N]), zp(ci), zv(ci), ALU.subtract, [zc[ci]], [zmA[c]], eng="pool")
            k.stt(v3(zm[:, c, 0:N]), v3(zm[:, c, 0:N]), P["mu"][:, c:c + 1], zv(ci), ALU.mult, ALU.add,
                  [zmA[c], P["mu"], zc[ci]], [zmA[c]])
        k.act(r[0:64, 12, 0:N], zm[0:64, 12, 0:N], AF.Tanh, [zmA[12]], [rA[12]])
        k.copy(r[64:128, 12, 0:N], zm[64:128, 12, 0:N], [zmA[12]], [rA[12]])
        k.act(r[:, 13, 0:N], zm[:, 13, 0:N], AF.Sigmoid, [zmA[13]], [rA[13]])
        rst = ones[:, 0:N] if kind == "p" else cst[:, C_RST:C_RST + N]
        for c in range(4):
            cs = slice(c * 128, (c + 1) * 128)
            ps = k.psum()
            k.mm(ps, ps[:, 0:N], P["w2a2"][0:64, cs], r[0:64, 12, 0:N], True, [P["w2a2"], rA[12]])
            k.act(r[:, 14 + c, 0:N], ps[:, 0:N], AF.Sigmoid, [ps, P["w0"]], [rA[14 + c]], bias=P["w0"][:, c:c + 1])
            ps = k.psum()
            k.mm(ps, ps[:, 0:N], P["w2a2"][64:128, cs], r[64:128, 12, 0:N], True, [P["w2a2"], rA[12]])
            k.act(r[:, 18 + c, 0:N], ps[:, 0:N], AF.Sigmoid, [ps, P["a0"]], [rA[18 + c]], bias=P["a0"][:, c:c + 1])
            ps = k.psum()
            k.mm(ps, ps[:, 0:N], P["g2"][:, cs], r[:, 13, 0:N], True, [P["g2"], rA[13]])
            k.copy(r[:, 22 + c, 0:N], ps[:, 0:N], [ps], [rA[22 + c]], eng="act")
            k.ts(r[:, 26 + c, 0:N], zm[:, 4 + c, 0:N], P["kk"][:, c:c + 1], None, ALU.mult, None, [zmA[4 + c], P["kk"]], [rA[26 + c]])
            k.act(r[:, 38 + c, 0:N], r[:, 26 + c, 0:N], AF.Square, [rA[26 + c]], [rA[38 + c]])
            ps = k.psum()
            k.mm(ps, ps[:, 0:N], bo64, r[:, 38 + c, 0:N], True, [rA[38 + c]] + RC)
            k.act(r[:, 38 + c, 0:N], ps[:, 0:N], AF.Sqrt, [ps], [rA[38 + c]], bias=NORM_EPS)
            k.recip(r[:, 38 + c, 0:N], r[:, 38 + c, 0:N], [rA[38 + c]], [rA[38 + c]])
            k.tt(r[:, 26 + c, 0:N], r[:, 26 + c, 0:N], r[:, 38 + c, 0:N], ALU.mult, [rA[26 + c], rA[38 + c]], [rA[26 + c]])
            k.ts(r[:, 30 + c, 0:N], r[:, 18 + c, 0:N], P["ka"][:, c:c + 1], P["omka"][:, c:c + 1], ALU.mult, ALU.add,
                 [rA[18 + c], P["ka"], P["omka"]], [rA[30 + c]])
            k.tt(r[:, 30 + c, 0:N], r[:, 30 + c, 0:N], zm[:, 4 + c, 0:N], ALU.mult, [rA[30 + c], zmA[4 + c]], [rA[30 + c]])
            k.tt(r[:, 42 + c, 0:N], r[:, 26 + c, 0:N], r[:, 18 + c, 0:N], ALU.mult, [rA[26 + c], rA[18 + c]], [rA[42 + c]])
            k.scan(r[:, 34 + c, 0:N], rst, r[:, 14 + c, 0:N], [rA[14 + c]] + RC, [rA[34 + c]])
        ar = k.rot("ar", [128, 4, 2, NMAX], 1)
        bk = k.rot("bk", [128, 4, 2, NMAX], 1)
        arA = k.alias(ar, 4)
        bkA = k.alias(bk, 4)
        for c in range(4):
            k.act(r[:, 38 + c, 0:N], r[:, 34 + c, 0:N], AF.Exp, [rA[34 + c]], [rA[38 + c]], scale=-C0)
            k.tt(ar[:, c, 1, 0:N], zm[:, c, 0:N], r[:, 38 + c, 0:N], ALU.mult, [zmA[c], rA[38 + c]], [arA[c]])
            k.tt(r[:, 38 + c, 0:N], r[:, 34 + c, 0:N], r[:, 14 + c, 0:N], ALU.subtract, [rA[34 + c], rA[14 + c]], [rA[38 + c]])
            k.act(r[:, 38 + c, 0:N], r[:, 38 + c, 0:N], AF.Exp, [rA[38 + c]], [rA[38 + c]], scale=-C0)
            k.stt(ar[:, c, 0, 0:N], r[:, 26 + c, 0:N], -1.0, r[:, 38 + c, 0:N], ALU.mult, ALU.mult, [rA[26 + c], rA[38 + c]], [arA[c]])
            k.act(r[:, 38 + c, 0:N], r[:, 34 + c, 0:N], AF.Exp, [rA[34 + c]], [rA[38 + c]], scale=C0)
            k.tt(bk[:, c, 0, 0:N], r[:, 42 + c, 0:N], r[:, 38 + c, 0:N], ALU.mult, [rA[42 + c], rA[38 + c]], [bkA[c]])
            k.tt(bk[:, c, 1, 0:N], r[:, 30 + c, 0:N], r[:, 38 + c, 0:N], ALU.mult, [rA[30 + c], rA[38 + c]], [bkA[c]])

        oT = k.rot("oT", [128, 8, NMAX], 1)
        oA = k.alias(oT, 8)

        for s in range(nseq):
            c0 = s * Ls
            L = Ls
            cols = slice(c0, c0 + L)
            last = c0 + L - 1
            if kind == "p":
                Mg, Mr = P_gla[i], P_rwkv[i]
            else:
                Mg = k.rot("sgla", [128, 2, 128], 2)
                k.dma("sp", Mg[:, :, :], st_gla[i, s].rearrange("(c two) kk v -> (two kk) c v", two=2), [], [Mg])
                raw = RAW[0]
                for two in range(2):
                    k.dma("sp", raw[two * 64:(two + 1) * 64, :, two * 64:(two + 1) * 64],
                          st_rwkv[i, s].rearrange("(c two) v kk -> two v c kk", two=2)[two], [], [raw])
                Mr = k.rot("srw", [128, 4, 128], 1)
                ps = k.psum()
                for c in range(4):
                    k.tr(ps, ps[:, c * 128:(c + 1) * 128], raw[:, c, :], ident, [raw] + RC)
                k.copy(Mr[:, :, :], ps[:, 0:512].rearrange("p (c v) -> p c v", v=128), [ps], [Mr], eng="act")

            vt = k.rot("gvt", [128, 4, 128], 1)
            ps = k.psum()
            for h in range(4):
                k.tr(ps, ps[0:L, h * 128:(h + 1) * 128], z.t[:, 4 + h, s * (Ls + 1) + 1:s * (Ls + 1) + 1 + L], ident, [zc[4 + h]] + RC)
            k.copy(vt[0:L, :, :], ps[0:L, 0:512].rearrange("p (h v) -> p h v", v=128), [ps], [vt], eng="act")
            kt = k.rot("gkt", [128, 2, 128], 1)
            ps = k.psum()
            for c in range(2):
                k.ts(g[:, 10 + c, cols], g[:, c, cols], g[:, c, last:last + 1], None, ALU.subtract, None, [gA[c]], [gA[10 + c]])
                k.act(g[:, 10 + c, cols], g[:, 10 + c, cols], AF.Exp, [gA[10 + c]], [gA[10 + c]], scale=1.0 / 16)
                k.tt(g[:, 10 + c, cols], g[:, 10 + c, cols],
                     z.t[:, 2 + c, s * (Ls + 1) + 1:s * (Ls + 1) + 1 + L], ALU.mult, [gA[10 + c], zc[2 + c]], [gA[10 + c]])
                k.tr(ps, ps[0:L, c * 128:(c + 1) * 128], g[:, 10 + c, cols], ident, [gA[10 + c]] + RC)
            k.copy(kt[0:L, :, :], ps[0:L, 0:256].rearrange("p (c v) -> p c v", v=128), [ps], [kt])
            for h in range(4):
                c, two = h // 2, h % 2
                rows = slice(two * 64, two * 64 + 64)
                ps = k.psum()
                k.mm(ps, ps[0:L, 0:L], g[rows, 8 + c, cols], g[rows, 6 + c, cols], True, [gA[8 + c], gA[6 + c]])
                at = k.rot("gat", [128, 128], 2)
                k.tt(at[0:L, 0:L], ps[0:L, 0:L], cm(C_IU, L, L), ALU.mult, [ps] + RC, [at])
                po = k.psum()
                k.mm(po, po[:, 0:L], Mg[rows, c, :], g[rows, 6 + c, cols], False, [Mg, gA[6 + c]])
                k.mm(po, po[:, 0:L], vt[0:L, h, :], at[0:L, 0:L], True, [vt, at])
                k.copy(oT[:, h, cols], po[:, 0:L], [po], [oA[h]], eng="act")
                pu = k.psum()
                k.mm(pu, pu[:, 0:128], kt[0:L, c, :], vt[0:L, h, :], True, [kt, vt])
                k.stt(Mg[rows, c, :], Mg[rows, c, :], g[rows, 2 + c, last:last + 1], pu[rows, 0:128], ALU.mult, ALU.add,
                      [Mg, gA[2 + c], pu], [Mg])

            vtr = vpad
            ps = k.psum()
            for c in range(4):
                k.tr(ps, ps[0:L, c * 128:(c + 1) * 128], zm[:, 8 + c, cols], ident, [zmA[8 + c]] + RC)
            for c in range(4):
                for two in range(2):
                    h = 2 * c + two
                    k.copy(vpad[0:L, h, two * 64:two * 64 + 64], ps[0:L, c * 128 + two * 64:c * 128 + two * 64 + 64], [ps], [vpad],
                           eng="act" if two else "dve")
            ket = k.rot("rket", [128, 4, 2, 128], 1)
            for c in range(4):
                k.ts(r[:, 38 + c, cols], r[:, 34 + c, cols], r[:, 34 + c, last:last + 1], None, ALU.subtract, None, [rA[34 + c]], [rA[38 + c]])
                k.act(r[:, 38 + c, cols], r[:, 38 + c, cols], AF.Exp, [rA[38 + c]], [rA[38 + c]], scale=C0)
                tmp = k.rot("rtmp", [128, 2, 128], 1)
                k.tt(tmp[:, 0, 0:L], r[:, 30 + c, cols], r[:, 38 + c, cols], ALU.mult, [rA[30 + c], rA[38 + c]], [tmp])
                k.tt(tmp[:, 1, 0:L], r[:, 42 + c, cols], r[:, 38 + c, cols], ALU.mult, [rA[42 + c], rA[38 + c]], [tmp])
                ps = k.psum()
                k.tr(ps, ps[0:L, 0:128], tmp[:, 0, 0:L], ident, [tmp] + RC)
                k.tr(ps, ps[0:L, 128:256], tmp[:, 1, 0:L], ident, [tmp] + RC)
                k.copy(ket[0:L, c, :, :], ps[0:L, 0:256].rearrange("p (a v) -> p a v", v=128), [ps], [ket], eng="act")
            tbs = k.rot("tb", [128, 8, NMAX], 1)
            SC = slots3(ARENA, ARENA_T, range(0, 12)) + slots3(tbs, tbs, range(0, 8))
            sl_ = slice(0, L)
            for cp in (0, 2):
                NKS, jobs = {}, []
                for c in (cp, cp + 1):
                    NK = []
                    for two in range(2):
                        rows = slice(two * 64, two * 64 + 64)
                        pb = k.psum()
                        k.mm(pb, pb[0:L, 0:L], bk[rows, c, 0, cols], ar[rows, c, 0, cols], False, [bkA[c], arA[c]])
                        k.mm(pb, pb[0:L, L:2 * L], bk[rows, c, 0, cols], ar[rows, c, 1, cols], True, [bkA[c], arA[c]])
                        pk = k.psum()
                        k.mm(pk, pk[0:L, 0:L], bk[rows, c, 1, cols], ar[rows, c, 0, cols], False, [bkA[c], arA[c]])
                        k.mm(pk, pk[0:L, L:2 * L], bk[rows, c, 1, cols], ar[rows, c, 1, cols], True, [bkA[c], arA[c]])
                        pn = k.psum()
                        k.mm(pn, pn[0:L, 0:L], ar[rows, c, 0, cols], bk[rows, c, 0, cols], True, [bkA[c], arA[c]])
                        nb = k.rot("rnb", [128, 5, 128], 4)
                        k.tt(nb[0:L, 0, 0:L], pb[0:L, 0:L], cm(C_SU, L, L), ALU.mult, [pb] + RC, [nb])
                        k.tt(nb[0:L, 1, 0:L], pb[0:L, L:2 * L], cm(C_IU, L, L), ALU.mult, [pb] + RC, [nb])
                        k.tt(nb[0:L, 2, 0:L], pk[0:L, 0:L], cm(C_SU, L, L), ALU.mult, [pk] + RC, [nb])
                        k.tt(nb[0:L, 3, 0:L], pk[0:L, L:2 * L], cm(C_IU, L, L), ALU.mult, [pk] + RC, [nb])
                        k.tt(nb[0:L, 4, 0:L], pn[0:L, 0:L], cm(C_SL, L, L), ALU.mult, [pn] + RC, [nb])
                        NK.append(nb)
                    NKS[c] = NK
                    pr = k.psum()
                    k.mm(pr, pr[0:L, 0:128], ar[:, c, 0, cols], Mr[:, c, :], False, [arA[c], Mr])
                    for two in range(2):
                        k.mm(pr, pr[0:L, 0:128], NK[two][0:L, 2, 0:L], vpad[0:L, 2 * c + two, :], two == 1, [NK[two], vpad])
                    r0 = k.rot("rr0", [128, 128], 2)
                    k.copy(r0[0:L, :], pr[0:L, 0:128], [pr], [r0], eng="act")
                    for two in range(2):
                        h = 2 * c + two
                        ji = len(jobs)
                        sc = SC[5 * ji:5 * ji + 5]
                        rsl = sc[4]
                        RA_ = Slot(rsl.fn, rsl.tok)
                        RB_ = Slot((lambda r_, c_, f=rsl.fn: f(r_, slice(64 + c_.start, 64 + c_.stop))), rsl.tok)
                        jobs.append(dict(Pp=NK[two][0:L, 0, 0:L], Pm=NK[two][0:L, 4, 0:L], ptoks=[NK[two]],
                                         R0=r0[0:L, two * 64:two * 64 + 64], rtoks=[r0], L=L, w=64,
                                         out=upad[0:L, h, two * 64:two * 64 + 64], otoks=[upad],
                                         PA=sc[0:2], PB=sc[2:4], RA=RA_, RB=RB_))
                if SEQ_RW:
                    for jb_ in jobs:
                        tri_solve_multi([jb_], L)
                else:
                    tri_solve_multi(jobs, L)
                for c in (cp, cp + 1):
                    NK = NKS[c]
                    py = k.psum()
                    k.mm(py, py[:, 0:L], Mr[:, c, :], ar[:, c, 1, cols], False, [Mr, arA[c]])
                    for two in range(2):
                        h = 2 * c + two
                        k.mm(py, py[:, 0:L], upad[0:L, h, :], NK[two][0:L, 1, 0:L], False, [upad, NK[two]])
                        k.mm(py, py[:, 0:L], vpad[0:L, h, :], NK[two][0:L, 3, 0:L], two == 1, [vpad, NK[two]])
                    k.copy(oT[:, 4 + c, cols], py[:, 0:L], [py], [oA[4 + c]], eng="act")
                    pu = k.psum()
                    k.mm(pu, pu[:, 0:128], ket[0:L, c, 0, :], vpad[0:L, 2 * c, :], False, [ket, vpad])
                    k.mm(pu, pu[:, 0:128], ket[0:L, c, 0, :], vpad[0:L, 2 * c + 1, :], False, [ket, vpad])
                    k.mm(pu, pu[:, 0:128], ket[0:L, c, 1, :], upad[0:L, 2 * c, :], False, [ket, upad])
                    k.mm(pu, pu[:, 0:128], ket[0:L, c, 1, :], upad[0:L, 2 * c + 1, :], True, [ket, upad])
                    ut = k.rot("rut", [128, 128], 1)
                    k.tt(ut[:, :], pu[:, 0:128], bo64, ALU.mult, [pu] + RC, [ut])
                    de = k.rot("rde", [128, 1], 2)
                    k.act(de[:, :], r[:, 34 + c, last:last + 1], AF.Exp, [rA[34 + c]], [de], scale=-C0)
                    k.stt(Mr[:, c, :], Mr[:, c, :], de[:, 0:1], ut[:, :], ALU.mult, ALU.add, [Mr, de, ut], [Mr])

            if kind == "s" or tile["last"]:
                gi = 0 if kind == "p" else 1
                si = 0 if kind == "p" else s
                k.dma("pool", o_gla[gi][i, si].rearrange("(c two) kk v -> (two kk) c v", two=2), Mg[:, :, :], [Mg], [])
                ps = k.psum()
                for c in range(4):
                    k.tr(ps, ps[:, c * 128:(c + 1) * 128], Mr[:, c, :], ident, [Mr] + RC)
                so = k.rot("srw_out", [128, 4, 128], 1)
                k.copy(so[:, :, :], ps[:, 0:512].rearrange("p (c v) -> p c v", v=128), [ps], [so])
                for two in range(2):
                    k.dma("pool", o_rwkv[gi][i, si].rearrange("(c two) v kk -> two v c kk", two=2)[two],
                          so[two * 64:(two + 1) * 64, :, two * 64:(two + 1) * 64], [so], [])

        for h in range(4):
            sq = k.rot("pc_sq", [128, NMAX], 2)
            k.act(sq[:, 0:N], oT[:, h, 0:N], AF.Square, [oA[h]], [sq])
            ps = k.psum()
            k.mm(ps, ps[:, 0:N], ones, sq[:, 0:N], True, [sq] + RC)
            k.act(sq[:, 0:N], ps[:, 0:N], AF.Sqrt, [ps], [sq], bias=NORM_EPS, scale=1.0 / 128)
            k.recip(sq[:, 0:N], sq[:, 0:N], [sq], [sq])
            k.tt(sq[:, 0:N], sq[:, 0:N], oT[:, h, 0:N], ALU.mult, [sq, oA[h]], [sq])
            k.stt(mixT[:, h, 0:N].bitcast(F32R), sq[:, 0:N], P["gng"][:, h:h + 1], g[:, 12 + h, 0:N], ALU.mult, ALU.mult,
                  [sq, P["gng"], gA[12 + h]], [mixT])
        for c in range(4):
            sq = k.rot("pc_sq", [128, NMAX], 2)
            st = k.rot("pc_st", [128, 4, NMAX], 1)
            k.act(sq[:, 0:N], oT[:, 4 + c, 0:N], AF.Square, [oA[4 + c]], [sq])
            p1 = k.psum()
            k.mm(p1, p1[:, 0:N], bo64, oT[:, 4 + c, 0:N], True, [oA[4 + c]] + RC)
            p2 = k.psum()
            k.mm(p2, p2[:, 0:N], bo64, sq[:, 0:N], True, [sq] + RC)
            k.ts(st[:, 0, 0:N], p1[:, 0:N], 1.0 / 64, None, ALU.mult, None, [p1], [st])
            k.tt(st[:, 1, 0:N], st[:, 0, 0:N], st[:, 0, 0:N], ALU.mult, [st], [st])
            k.stt(st[:, 1, 0:N], p2[:, 0:N], 1.0 / 64, st[:, 1, 0:N], ALU.mult, ALU.subtract, [p2, st], [st])
            k.act(st[:, 1, 0:N], st[:, 1, 0:N], AF.Sqrt, [st], [st], bias=GN_EPS)
            k.recip(st[:, 1, 0:N], st[:, 1, 0:N], [st], [st])
            k.tt(st[:, 2, 0:N], oT[:, 4 + c, 0:N], st[:, 0, 0:N], ALU.subtract, [oA[4 + c], st], [st])
            k.tt(st[:, 2, 0:N], st[:, 2, 0:N], st[:, 1, 0:N], ALU.mult, [st], [st])
            k.ts(st[:, 2, 0:N], st[:, 2, 0:N], P["lg"][:, c:c + 1], P["lb"][:, c:c + 1], ALU.mult, ALU.add, [st, P["lg"], P["lb"]], [st])
            k.stt(sq[:, 0:N], zm[:, c, 0:N], P["rk"][:, c:c + 1], r[:, 30 + c, 0:N], ALU.mult, ALU.mult,
                  [zmA[c], P["rk"], rA[30 + c]], [sq])
            p3 = k.psum()
            k.mm(p3, p3[:, 0:N], bo64, sq[:, 0:N], True, [sq] + RC)
            k.tt(st[:, 3, 0:N], p3[:, 0:N], zm[:, 8 + c, 0:N], ALU.mult, [p3, zmA[8 + c]], [st])
            k.tt(st[:, 2, 0:N], st[:, 2, 0:N], st[:, 3, 0:N], ALU.add, [st], [st])
            k.tt(mixT[:, 4 + c, 0:N].bitcast(F32R), st[:, 2, 0:N], r[:, 22 + c, 0:N], ALU.mult, [st, rA[22 + c]], [mixT])

    def c_mixer(i, tile, xT, mixT):
        kind, nseq, Ls, N = tile["kind"], tile["nseq"], tile["Ls"], tile["N"]
        P = PC[i]
        Ne = nseq * (Ls + 3)
        z = k.rot("z", [128, 33, NMAX + 4], 1)
        zc = tile["zc"]
        ze = lambda ci: z.t[:, ci, 0:Ne].rearrange("p (s t) -> p s t", t=Ls + 3)
        if kind == "p":
            k.copy(z.t[:, 0:24, 0:3], C_gconv[i][:, :, :], [C_gconv[i]], zc[0:24], eng="pool")
        else:
            for c4 in range(0, 24, 4):
                if c4 % 8 == 0:
                    cb = k.rot("stg", [128, 1024], 1)
                    k.dma("pool", cb[0:nseq * 3, :], st_gconv[i].rearrange("s r c -> (s r) c")[:, c4 * 128:c4 * 128 + 1024], [], [cb])
                ps = k.psum()
                for j in range(4):
                    k.tr(ps, ps[:, j * 48:j * 48 + nseq * 3], cb[0:nseq * 3, (c4 % 8 + j) * 128:(c4 % 8 + j + 1) * 128],
                         ident[0:nseq * 3, 0:nseq * 3], [cb] + RC)
                for j in range(4):
                    k.copy(ze(c4 + j)[:, :, 0:3], ps[:, j * 48:j * 48 + nseq * 3].rearrange("p (s r) -> p s r", r=3), [ps], [zc[c4 + j]])

        def v3(ap):
            return ap.rearrange("p (s t) -> p s t", t=Ls)

        def cons(c_abs, cw, ps):
            ci = c_abs // 128
            if ci < 24:
                k.copy(ze(ci)[:, :, 3:Ls + 3], v3(ps[:, 0:N]), [ps], [zc[ci]], eng="act")
            else:
                k.act(z[:, ci, 0:N], ps[:, 0:N], AF.Silu, [ps], [zc[ci]])

        proj(w_in_c[i], 8, 0, 4096, lambda kk: xT[:, kk, 0:N], [xT], N, cons, 256)
        if kind == "p":
            k.copy(C_gconv[i][:, :, :], z.t[:, 0:24, Ls:Ls + 3], zc[0:24], [C_gconv[i]], eng="pool")
            if tile["last"]:
                store_rows(lambda c: C_gconv[i][:, c, :], [C_gconv[i]], 24, 3, o_gconv[0][i, 0])
        else:
            store_rows(lambda c: ze(c)[:, :, Ls:Ls + 3], zc[0:24], 24, nseq * 3, o_gconv[1][i].rearrange("s r c -> (s r) c"), view=(nseq, 3))
        q = ARENA
        qA = ARENA_T
        for ci in range(24):
            e = ze(ci)
            tmp = k.rot("ctmp", [128, NMAX], 2)
            k.ts(v3(tmp[:, 0:N]), e[:, :, 0:Ls], P["cw"][:, 0, ci:ci + 1], None, ALU.mult, None, [zc[ci], P["cw"]], [tmp])
            for j in (1, 2, 3):
                k.stt(v3(tmp[:, 0:N]), e[:, :, j:j + Ls], P["cw"][:, j, ci:ci + 1], v3(tmp[:, 0:N]), ALU.mult, ALU.add,
                      [zc[ci], P["cw"], tmp], [tmp])
            k.act(q[:, ci, 0:N], tmp[:, 0:N], AF.Silu, [tmp], [qA[ci]])
        for ci in range(16):
            sq = k.rot("pc_sq", [128, NMAX], 2)
            k.act(sq[:, 0:N], q[:, ci, 0:N], AF.Square, [qA[ci]], [sq])
            ps = k.psum()
            k.mm(ps, ps[:, 0:N], ones, sq[:, 0:N], True, [sq] + RC)
            k.act(sq[:, 0:N], ps[:, 0:N], AF.Sqrt, [ps], [sq], bias=NORM_EPS)
            k.recip(sq[:, 0:N], sq[:, 0:N], [sq], [sq])
            if ci < 8:
                k.stt(q[:, ci, 0:N], q[:, ci, 0:N], 128.0 ** -0.5, sq[:, 0:N], ALU.mult, ALU.mult, [qA[ci], sq], [qA[ci]])
            else:
                k.tt(q[:, ci, 0:N], q[:, ci, 0:N], sq[:, 0:N], ALU.mult, [qA[ci], sq], [qA[ci]])
        GBT = GL_T[10:16]
        pb_ = k.psum()
        for kk in range(8):
            k.mm(pb_, pb_[0:8, 0:N], P["wba"][:, kk, 0:8], xT[:, kk, 0:N], kk == 7, [P["wba"], xT])
        k.act(GL[0:8, 11, 0:N], pb_[0:8, 0:N], AF.Sigmoid, [pb_], GBT)
        pa_ = k.psum()
        for kk in range(8):
            k.mm(pa_, pa_[0:8, 0:N], P["wba"][:, kk, 8:16], xT[:, kk, 0:N], kk == 7, [P["wba"], xT])
        k.act(GL[0:8, 15, 0:N], pa_[0:8, 0:N], AF.Exp, [pa_, P["dtb"]], GBT, bias=P["dtb"][:, 0:1])
        k.act(GL[0:8, 15, 0:N], GL[0:8, 15, 0:N], AF.Ln, GBT, GBT, bias=1.0)
        k.ts(GL[0:8, 14, 0:N], GL[0:8, 15, 0:N], P["nexpA"][:, 0:1], None, ALU.mult, None, GBT + [P["nexpA"]], GBT)
        rst = ones[0:8, 0:N] if kind == "p" else cst[0:8, C_RST:C_RST + N]
        k.scan(GL[0:8, 10, 0:N], rst, GL[0:8, 14, 0:N], GBT + RC, GBT)
        k.act(GL[0:8, 13, 0:N], GL[0:8, 10, 0:N], AF.Exp, GBT, GBT)
        k.ts(GL[0:8, 13, 0:N], GL[0:8, 13, 0:N], -1.0, None, ALU.mult, None, GBT, GBT)

        oT = k.rot("oT", [128, 8, NMAX], 1)
        oA = k.alias(oT, 8)
        zm_ = k.rot("zm", [128, 14, NMAX], 1)
        ar_ = k.rot("ar", [128, 4, 2, NMAX], 1)
        bk_ = k.rot("bk", [128, 4, 2, NMAX], 1)
        rn0 = k.rot("rnb", [128, 5, 128], 4)
        rn1 = k.rot("rnb", [128, 5, 128], 4)
        rke = k.rot("rket", [128, 4, 2, 128], 1)
        zmT, arT, bkT = k.alias(zm_, 14), k.alias(ar_, 4), k.alias(bk_, 4)
        LANES = []
        for lane in range(4):
            if lane == 0:
                w6 = slots3(zm_, zmT, range(0, 6)); x5 = slots3(rn0, rn0, range(5))
            elif lane == 1:
                w6 = slots3(zm_, zmT, range(6, 12)); x5 = slots3(rn1, rn1, range(5))
            elif lane == 2:
                w6 = slots4(ar_, arT, range(0, 6)); x5 = slots4(rke, rke, range(5))
            else:
                w6 = slots4(bk_, bkT, range(0, 6))
                x5 = slots3(zm_, zmT, range(12, 14)) + slots4(ar_, arT, range(6, 8)) + slots4(bk_, bkT, range(6, 7))
            sv = slots3(ARENA, ARENA_T, range(24 + 6 * lane, 30 + 6 * lane))
            LANES.append(w6 + x5 + sv)
        for s in range(nseq):
            c0 = s * Ls
            L = Ls
            cols = slice(c0, c0 + L)
            last = c0 + L - 1
            if kind == "p":
                Ms = P_gdn[i]
                MsT = [Ms]
            else:
                Ms = GL
                MsT = GL_T[0:8]
                k.dma("sp", Ms[:, 0:8, :], st_gdn[i, s].rearrange("h kk v -> kk h v"), [], MsT)
            k.ts(GL[0:8, 12, cols], GL[0:8, 10, cols], GL[0:8, 10, last:last + 1], None, ALU.subtract, None, GBT, GBT)
            k.act(GL[0:8, 12, cols], GL[0:8, 12, cols], AF.Exp, GBT, GBT, scale=-1.0)
            ps = k.psum()
            for j in range(4):
                k.tr(ps, ps[0:L, j * 8:(j + 1) * 8], GL[0:8, 10 + j, cols], ident[0:8, 0:8], GBT + RC)
            tm = k.rot("gtm", [128, 4, 8], 2)
            k.copy(tm[0:L, :, :], ps[0:L, 0:32].rearrange("p (a h) -> p a h", h=8), [ps], [tm])
            sl_ = slice(0, L)
            for g4 in (0, 4):
                st1 = []
                for lane in range(4):
                    h = g4 + lane
                    LS = LANES[lane]
                    W = LS[0:6]
                    QG, VV, KE, RR, DL = LS[6:11]
                    gm = k.rot("ggm", [8, 2, NMAX], 2)
                    k.ts(gm[0:8, :, 0:L], GL[0:8, 10:12, cols], ident[0:8, h:h + 1], None, ALU.mult, None, GBT + RC, [gm])
                    pg = k.psum()
                    k.mm(pg, pg[:, 0:L], ones[0:8, :], gm[0:8, 0, 0:L], False, [gm] + RC)
                    k.mm(pg, pg[:, 128:128 + L], ones[0:8, :], gm[0:8, 1, 0:L], True, [gm] + RC)
                    pq = k.psum()
                    k.mm(pq, pq[0:L, 0:L], q[:, 8 + h, cols], q[:, h, cols], False, [qA[8 + h], qA[h]])
                    k.mm(pq, pq[0:L, 128:128 + L], q[:, 8 + h, cols], q[:, 8 + h, cols], True, [qA[8 + h]])
                    gam_p = tm[0:L, 0, h:h + 1]
                    k.ts(W[0](sl_, sl_), pg[0:L, 0:L], gam_p, 0.0, ALU.subtract, ALU.min, [pg, tm], [W[0].tok])
                    k.act(W[0](sl_, sl_), W[0](sl_, sl_), AF.Exp, [W[0].tok], [W[0].tok])
                    k.ts(W[1](sl_, sl_), pg[0:L, 0:L], gam_p, 0.0, ALU.subtract, ALU.max, [pg, tm], [W[1].tok])
                    k.act(W[1](sl_, sl_), W[1](sl_, sl_), AF.Exp, [W[1].tok], [W[1].tok], scale=-1.0)
                    k.tt(W[4](sl_, sl_), W[0](sl_, sl_), cm(C_IU, L, L), ALU.mult, [W[0].tok] + RC, [W[4].tok])
                    k.tt(W[4](sl_, sl_), W[4](sl_, sl_), pq[0:L, 0:L], ALU.mult, [W[4].tok, pq], [W[4].tok])
                    k.tt(W[2](sl_, sl_), W[0](sl_, sl_), pg[0:L, 128:128 + L], ALU.mult, [W[0].tok, pg], [W[2].tok])
                    k.tt(W[2](sl_, sl_), W[2](sl_, sl_), cm(C_NSU, L, L), ALU.mult, [W[2].tok] + RC, [W[2].tok])
                    k.tt(W[2](sl_, sl_), W[2](sl_, sl_), pq[0:L, 128:128 + L], ALU.mult, [W[2].tok, pq], [W[2].tok])
                    k.stt(W[3](sl_, sl_), W[1](sl_, sl_), tm[0:L, 1, h:h + 1], cm(C_NSL, L, L), ALU.mult, ALU.mult,
                          [W[1].tok, tm] + RC, [W[3].tok])
                    k.tt(W[3](sl_, sl_), W[3](sl_, sl_), pq[0:L, 128:128 + L], ALU.mult, [W[3].tok, pq], [W[3].tok])
                    fa = slice(0, 128)
                    k.act(W[5](fa, sl_), pg[:, 0:L], AF.Exp, [pg], [W[5].tok])
                    k.tt(QG(fa, sl_), q[:, h, cols], W[5](fa, sl_), ALU.mult, [qA[h], W[5].tok], [QG.tok])
                    pt = k.psum()
                    k.tr(pt, pt[0:L, 0:128], q[:, 16 + h, cols], ident, [qA[16 + h]] + RC)
                    k.tr(pt, pt[0:L, 128:256], q[:, 8 + h, cols], ident, [qA[8 + h]] + RC)
                    k.copy(VV(sl_, fa), pt[0:L, 0:128], [pt], [VV.tok], eng="act")
                    k.ts(KE(sl_, fa), pt[0:L, 128:256], tm[0:L, 2, h:h + 1], None, ALU.mult, None, [pt, tm], [KE.tok])
                    pk = k.psum()
                    k.mm(pk, pk[0:L, 0:128], q[:, 8 + h, cols], Ms[:, h, :], True, [qA[8 + h]] + MsT)
                    k.stt(RR(sl_, fa), pk[0:L, 0:128], tm[0:L, 3, h:h + 1], VV(sl_, fa), ALU.mult, ALU.add, [pk, tm, VV.tok], [RR.tok])
                    k.ts(RR(sl_, fa), RR(sl_, fa), tm[0:L, 1, h:h + 1], None, ALU.mult, None, [RR.tok, tm], [RR.tok])
                    st1.append(dict(Pp=W[2](sl_, sl_), Pm=W[3](sl_, sl_), ptoks=[W[2].tok, W[3].tok], R0=RR(sl_, fa), rtoks=[RR.tok],
                                    L=L, w=128, out=DL(sl_, fa), otoks=[DL.tok], PA=LS[11:13], PB=LS[13:15], RA=LS[15], RB=LS[16]))
                if SEQ_GDN:
                    for jb_ in st1:
                        tri_solve_multi([jb_], L)
                else:
                    tri_solve_multi(st1, L)
                for lane in range(4):
                    h = g4 + lane
                    LS = LANES[lane]
                    W = LS[0:6]
                    QG, VV, KE, RR, DL = LS[6:11]
                    fa = slice(0, 128)
                    po = k.psum()
                    k.mm(po, po[:, 0:L], Ms[:, h, :], QG(fa, sl_), False, MsT + [QG.tok])
                    k.mm(po, po[:, 0:L], DL(sl_, fa), W[4](sl_, sl_), True, [DL.tok, W[4].tok])
                    k.copy(oT[:, h, cols], po[:, 0:L], [po], [oA[h]], eng="act")
                    pu = k.psum()
                    k.mm(pu, pu[:, 0:128], KE(sl_, fa), DL(sl_, fa), True, [KE.tok, DL.tok])
                    k.stt(Ms[:, h, :], Ms[:, h, :], W[5](fa, slice(L - 1, L)), pu[:, 0:128], ALU.mult, ALU.add, MsT + [W[5].tok, pu], MsT)
            if kind == "s" or tile["last"]:
                gi = 0 if kind == "p" else 1
                si = 0 if kind == "p" else s
                k.dma("pool", o_gdn[gi][i, si].rearrange("h kk v -> kk h v"), Ms[:, 0:8, :], MsT, [])
        for h in range(8):
            sq = k.rot("pc_sq", [128, NMAX], 2)
            k.act(sq[:, 0:N], oT[:, h, 0:N], AF.Square, [oA[h]], [sq])
            ps = k.psum()
            k.mm(ps, ps[:, 0:N], ones, sq[:, 0:N], True, [sq] + RC)
            k.act(sq[:, 0:N], ps[:, 0:N], AF.Sqrt, [ps], [sq], bias=NORM_EPS, scale=1.0 / 128)
            k.recip(sq[:, 0:N], sq[:, 0:N], [sq], [sq])
            k.tt(sq[:, 0:N], sq[:, 0:N], oT[:, h, 0:N], ALU.mult, [sq, oA[h]], [sq])
            k.stt(mixT[:, h, 0:N].bitcast(F32R), sq[:, 0:N], P["ng"][:, 0:1], z[:, 24 + h, 0:N], ALU.mult, ALU.mult, [sq, P["ng"], zc[24 + h]], [mixT])

    def ffn(l, tile, xT, tb):
        kind, nseq, Ls, N = tile["kind"], tile["nseq"], tile["Ls"], tile["N"]
        P = PL[l]
        Ne = nseq * (Ls + 2)
        ge = k.rot("z", [128, 33, NMAX + 4], 1)
        zc = tile["zc"]
        gev = lambda ci: ge.t[:, ci, 0:Ne].rearrange("p (s t) -> p s t", t=Ls + 2)
        if kind == "p":
            k.copy(ge.t[:, 0:22, 0:2], C_fconv[l][:, :, :], [C_fconv[l]], zc[0:22], eng="pool")
        else:
            for c4 in range(0, 22, 4):
                n4 = min(4, 22 - c4)
                if c4 % 8 == 0:
                    wcb = min(1024, D_FF - c4 * 128)
                    cb = k.rot("stg", [128, 1024], 1)
                    k.dma("pool", cb[0:nseq * 2, 0:wcb], st_fconv[l].rearrange("s r c -> (s r) c")[:, c4 * 128:c4 * 128 + wcb], [], [cb])
                ps = k.psum()
                for j in range(n4):
                    k.tr(ps, ps[:, j * 32:j * 32 + nseq * 2], cb[0:nseq * 2, (c4 % 8 + j) * 128:(c4 % 8 + j + 1) * 128],
                         ident[0:nseq * 2, 0:nseq * 2], [cb] + RC)
                for j in range(n4):
                    k.copy(gev(c4 + j)[:, :, 0:2], ps[:, j * 32:j * 32 + nseq * 2].rearrange("p (s r) -> p s r", r=2), [ps], [zc[c4 + j]])
        aT = AT
        aA = AT_T

        def v3(ap):
            return ap.rearrange("p (s t) -> p s t", t=Ls)

        def cons(c_abs, cw, ps):
            ci = c_abs // 128
            if ci < 22:
                k.copy(gev(ci)[:, :, 2:Ls + 2], v3(ps[:, 0:N]), [ps], [zc[ci]], eng="act")
                e = gev(ci)
                tmp = k.rot("ctmp", [128, NMAX], 2)
                k.ts(v3(tmp[:, 0:N]), e[:, :, 0:Ls], P["fcw"][:, 0, ci:ci + 1], None, ALU.mult, None, [zc[ci], P["fcw"]], [tmp])
                for j in (1, 2):
                    k.stt(v3(tmp[:, 0:N]), e[:, :, j:j + Ls], P["fcw"][:, j, ci:ci + 1], v3(tmp[:, 0:N]), ALU.mult, ALU.add,
                          [zc[ci], P["fcw"], tmp], [tmp])
                k.act(AT[:, ci, 0:N].bitcast(F32R), tmp[:, 0:N], AF.Silu, [tmp, P["fcb"]], [aA[ci]], bias=P["fcb"][:, ci:ci + 1])
            else:
                j = ci - 22
                k.tt(AT[:, j, 0:N].bitcast(F32R), AT[:, j, 0:N], ps[:, 0:N], ALU.mult, [aA[j], ps], [aA[j]])

        proj(w_up[l], 8, 0, 2 * D_FF, lambda kk: xT[:, kk, 0:N], [xT], N, cons, 256)
        if kind == "p":
            k.copy(C_fconv[l][:, :, :], ge.t[:, 0:22, Ls:Ls + 2], zc[0:22], [C_fconv[l]], eng="pool")
            if tile["last"]:
                store_rows(lambda c: C_fconv[l][:, c, :], [C_fconv[l]], 22, 2, o_fconv[0][l, 0])
        else:
            store_rows(lambda c: gev(c)[:, :, Ls:Ls + 2], zc[0:22], 22, nseq * 2, o_fconv[1][l].rearrange("s r c -> (s r) c"), view=(nseq, 2))

        def cons2(c_abs, cw, ps):
            m = c_abs // 128
            k.stt(tb[:, m, 0:N], xT[:, m, 0:N], ALPHA, ps[:, 0:N], ALU.mult, ALU.add, [xT, ps], [tb])

        proj(w_down[l], 22, 0, D, lambda kk: AT[:, kk, 0:N], aA, N, cons2, 128)

    tiles = [dict(kind="p", nseq=1, Ls=N_META, N=N_META, r0=0)]
    r0 = N_META
    while r0 < TP:
        tiles.append(dict(kind="p", nseq=1, Ls=128, N=128, r0=r0))
        r0 += 128
    for t in tiles:
        t["last"] = False
    tiles[-1]["last"] = True
    tiles.append(dict(kind="s", nseq=NS, Ls=4, N=NS * 4, r0=0, last=True))
    zbuf0 = None
    for tile in tiles:
        N = tile["N"]
        tile["first_rw"] = False
        src = xp if tile["kind"] == "p" else xs
        xtm = k.rot("stg", [128, 1024], 1)
        k.dma("sp", xtm[0:N, 0:D], src[tile["r0"]:tile["r0"] + N, :], [], [xtm])
        xT = k.rot("xT", [128, 8, NMAX], 2)
        for c4 in (0, 4):
            ps = k.psum()
            for j in range(4):
                k.tr(ps, ps[:, j * 128:j * 128 + N], xtm[0:N, (c4 + j) * 128:(c4 + j + 1) * 128], ident[0:N, 0:N], [xtm] + RC)
            k.copy(xT[:, c4:c4 + 4, 0:N].bitcast(F32R), ps[:, 0:512].rearrange("p (c t) -> p c t", t=128)[:, :, 0:N], [ps], [xT])
        zb = k.rot("z", [128, 33, NMAX + 4], 1)
        if zbuf0 is None:
            zbuf0 = k.alias(zb, 33)
        tile["zc"] = zbuf0
        for l in range(DEPTH):
            i = l // 2
            mixT = k.rot("mixT", [128, 8, NMAX], 1)
            if l % 2 == 0:
                ab_mixer(i, tile, xT, mixT)
                Wo = w_out_ab[i]
            else:
                c_mixer(i, tile, xT, mixT)
                Wo = w_out_c[i]
            tb = k.rot("tb", [128, 8, NMAX], 1)

            def cons_o(c_abs, cw, ps, xT=xT, tb=tb):
                m = c_abs // 128
                k.stt(tb[:, m, 0:N], xT[:, m, 0:N], ALPHA, ps[:, 0:N], ALU.mult, ALU.add, [xT, ps], [tb])

            proj(Wo, 8, 0, D, lambda kk: mixT[:, kk, 0:N], [mixT], N, cons_o, 256)
            x2 = k.rot("xT", [128, 8, NMAX], 2)
            layer_norm(tb, N, PL[l]["lmg"], PL[l]["lmb"], x2)
            tb2 = k.rot("tb", [128, 8, NMAX], 1)
            ffn(l, tile, x2, tb2)
            x3 = k.rot("xT", [128, 8, NMAX], 2)
            layer_norm(tb2, N, PL[l]["lfg"], PL[l]["lfb"], x3)
            xT = x3
        if tile["kind"] == "s":
            store_rows(lambda c: xT[:, c, 0:N], [xT], 8, N, y_s[:, :])
        elif tile["r0"] >= N_META:
            store_rows(lambda c: xT[:, c, 0:N], [xT], 8, N, y_p[tile["r0"] - N_META:tile["r0"] - N_META + N, :])
    k.finish()
    stats = (k.n_inst, k.n_wait)
    es.close()
    return nc, stats


_W_NAMES = ["w_in_ab", "gla_gate_w2", "gla_gate_b", "gla_norm_g", "rwkv_mu", "rwkv_w0", "rwkv_w2", "rwkv_a0", "rwkv_a2",
            "rwkv_g2", "rwkv_k_k", "rwkv_k_a", "rwkv_r_k", "rwkv_ln_g", "rwkv_ln_b", "w_out_ab", "w_in_c", "gdn_conv_w",
            "gdn_A_log", "gdn_dt_bias", "gdn_norm_g", "w_out_c", "w_up", "ffn_conv_w", "ffn_conv_b", "w_down",
            "ln_mix_g", "ln_mix_b", "ln_ffn_g", "ln_ffn_b"]


def make_in_maps(inputs, n_cores, depth):
    f = lambda a: np.ascontiguousarray(np.asarray(a, dtype=np.float32))
    xprompt = f(inputs["x_prompt"])
    B, SEQ, _ = xprompt.shape
    xsample = f(inputs["x_sample"])
    NS = xsample.shape[0] // n_cores
    meta = f(inputs["meta_tokens"])
    consts = make_consts()
    shared = {"consts": consts}
    for nm in _W_NAMES:
        a = f(inputs[nm])
        if nm == "rwkv_r_k":
            a = a.reshape(a.shape[0], 512)
        shared[nm] = a
    maps = []
    for c in range(n_cores):
        m = dict(shared)
        m["xp"] = np.ascontiguousarray(np.concatenate([meta, xprompt[c]], axis=0))
        sl = slice(c * NS, (c + 1) * NS)
        m["xs"] = np.ascontiguousarray(xsample[sl].reshape(NS * 4, D))
        m["st_gla"] = f(inputs["state_gla"][:, sl])
        m["st_rwkv"] = f(inputs["state_rwkv"][:, sl])
        m["st_shift"] = f(inputs["state_rwkv_shift"][:, sl])
        m["st_gdn"] = f(inputs["state_gdn"][:, sl])
        m["st_gconv"] = f(inputs["state_gdn_conv"][:, sl])
        m["st_fconv"] = f(inputs["state_ffn_conv"][:, sl])
        maps.append(m)
    return maps, SEQ, NS


def gather(results, n_cores, NS):
    cat = lambda nm, ax: np.concatenate([np.asarray(r[nm]) for r in results], axis=ax)
    y_p = np.stack([np.asarray(r["y_p"]) for r in results], axis=0)
    y_s = cat("y_s", 0).reshape(n_cores * NS, 4, D)
    outs = [y_p, y_s]
    for g in ("p", "s"):
        for nm in ("gla", "rwkv", "shift", "gdn", "gconv", "fconv"):
            outs.append(cat("%s_%s" % (g, nm), 1))
    return tuple(np.ascontiguousarray(o, dtype=np.float32) for o in outs)


def kernel(**inputs):
    n_cores = 8
    maps, SEQ, NS = make_in_maps(inputs, n_cores, DEPTH_FULL)
    nc, _ = build(SEQ, NS, DEPTH_FULL)
    res = run_bass_kernel_spmd(nc, maps, core_ids=list(range(n_cores)))
    return gather(res.results, n_cores, NS)
```

```python
import math
import sys
from contextlib import ExitStack

import numpy as np
import concourse.bass as bass
import concourse.mybir as mybir
from concourse.bass_utils import run_bass_kernel_spmd

F32 = mybir.dt.float32
F32R = mybir.dt.float32r
AF = mybir.ActivationFunctionType
ALU = mybir.AluOpType

D = 1024
DEPTH_FULL = 4
N_META = 16
GLA_COLS = 1552
AB_COLS = 3344
C_COLS = 4112
D_FF = 2816
ALPHA = (2.0 * DEPTH_FULL) ** 0.25
LN_EPS = 1e-5
NORM_EPS = 1e-6
GN_EPS = 64e-5
C0 = math.exp(-0.5)


class Buf:
    __slots__ = ("name", "t", "w", "r", "fresh")

    def __init__(self, name, t):
        self.name = name
        self.t = t
        self.w = None
        self.r = {}
        self.fresh = True

    def __getitem__(self, idx):
        return self.t[idx]


class K:
    ENG = ("pe", "act", "dve", "pool")

    def __init__(self, nc, es, n_dma_sems=8):
        self.nc = nc
        self.es = es
        self.engobj = {"pe": nc.tensor, "act": nc.scalar, "dve": nc.vector, "pool": nc.gpsimd, "sp": nc.sync}
        self.sem = {}
        self.cnt = {}
        for e in self.ENG:
            self.sem[e] = es.enter_context(nc.semaphore("sem_" + e))
            self.cnt[e] = 0
        self.dq = {}
        for q in ("sp", "pool"):
            keys = []
            for j in range(n_dma_sems):
                k = "d_%s_%d" % (q, j)
                self.sem[k] = es.enter_context(nc.semaphore(k))
                self.cnt[k] = 0
                keys.append(k)
            self.dq[q] = [keys, 0]
        self.known = {e: {} for e in ("pe", "act", "dve", "pool", "sp")}
        self.n_inst = 0
        self.n_wait = 0
        self.tags = None
        self._psum = []
        self._psum_i = 0
        self._uid = 0
        self._rot = {}
        self._alias = {}

    def sbuf(self, name, shape, dtype=F32):
        self._uid += 1
        t = self.es.enter_context(self.nc.sbuf_tensor("%s_%d" % (name, self._uid), list(shape), dtype))
        return Buf(name, t)

    def alias(self, buf, n):
        key = (id(buf), n)
        if key not in self._alias:
            self._alias[key] = [Buf("%s_%d" % (buf.name, i), buf.t) for i in range(n)]
        return self._alias[key]

    def init_psum(self, n=8):
        for i in range(n):
            t = self.es.enter_context(self.nc.psum_tensor("psb%d" % i, [128, 512], F32))
            self._psum.append(Buf("ps%d" % i, t))

    def psum(self):
        b = self._psum[self._psum_i % len(self._psum)]
        self._psum_i += 1
        b.fresh = True
        return b

    def rot(self, name, shape, n, dtype=F32):
        if name not in self._rot:
            self._rot[name] = [[self.sbuf(name, shape, dtype) for _ in range(n)], 0]
        lst = self._rot[name]
        b = lst[0][lst[1] % len(lst[0])]
        lst[1] += 1
        return b

    def _deps(self, eng, reads, writes, is_dma=False):
        deps = {}

        def add(k, v, war):
            if k == eng and not is_dma:
                if eng == "pe":
                    return
            if deps.get(k, 0) < v:
                deps[k] = v

        for b in reads:
            if b.w is not None:
                add(b.w[0], b.w[1], False)
        for b in writes:
            if b.w is not None:
                add(b.w[0], b.w[1], False)
            for k, v in b.r.items():
                add(k, v, True)
        return deps

    def _wait(self, eng, deps):
        kn = self.known[eng]
        eo = self.engobj[eng]
        for k, v in deps.items():
            if kn.get(k, 0) < v:
                eo.wait_ge(self.sem[k], v)
                kn[k] = v
                self.n_wait += 1

    def op(self, eng, fn, reads=(), writes=()):
        deps = self._deps(eng, reads, writes)
        self._wait(eng, deps)
        inst = fn(self.engobj[eng])
        self.cnt[eng] += 1
        c = self.cnt[eng]
        if self.tags is not None:
            self.tags[eng].append(sys._getframe(2).f_lineno)
        inst.then_inc(self.sem[eng], 1)
        self.n_inst += 1
        for b in reads:
            if b.r.get(eng, 0) < c:
                b.r[eng] = c
        for b in writes:
            b.w = (eng, c)
            b.r = {}
        return inst

    def dma(self, q, out, in_, reads=(), writes=(), **kw):
        deps = self._deps(q, reads, writes, True)
        keys, i = self.dq[q]
        k = keys[i % len(keys)]
        self.dq[q][1] = i + 1
        if self.cnt[k] > 0:
            deps[k] = max(deps.get(k, 0), self.cnt[k])
        self._wait(q, deps)
        inst = self.engobj[q].dma_start(out=out, in_=in_, **kw)
        self.cnt[k] += 16
        c = self.cnt[k]
        inst.then_inc(self.sem[k], 16)
        self.n_inst += 1
        for b in reads:
            if b.r.get(k, 0) < c:
                b.r[k] = c
        for b in writes:
            b.w = (k, c)
            b.r = {}
        return inst

    def finish(self):
        deps = {}
        for q in self.dq:
            for k in self.dq[q][0]:
                if self.cnt[k] > 0:
                    deps[k] = self.cnt[k]
        for e in self.ENG:
            if self.cnt[e] > 0:
                deps[e] = self.cnt[e]
        self._wait("sp", deps)

    def mm(self, bank, out, lhsT, rhs, last, reads):
        st = bank.fresh
        bank.fresh = False
        return self.op("pe", lambda e: e.matmul(out, lhsT=lhsT, rhs=rhs, start=st, stop=last), reads, [bank])

    def tr(self, bank, out, in_, ident, reads):
        bank.fresh = False
        return self.op("pe", lambda e: e.transpose(out, in_, ident), reads, [bank])

    def act(self, out, in_, func, reads, writes, bias=None, scale=None):
        kw = {}
        if bias is not None:
            kw["bias"] = bias
        if scale is not None:
            kw["scale"] = scale
        return self.op("act", lambda e: e.activation(out=out, in_=in_, func=func, **kw), reads, writes)

    def tt(self, out, in0, in1, op, reads, writes, eng="dve"):
        return self.op(eng, lambda e: e.tensor_tensor(out=out, in0=in0, in1=in1, op=op), reads, writes)

    def ts(self, out, in0, s1, s2, op0, op1, reads, writes, eng="dve"):
        if op1 is None:
            return self.op(eng, lambda e: e.tensor_scalar(out=out, in0=in0, scalar1=s1, scalar2=None, op0=op0), reads, writes)
        return self.op(eng, lambda e: e.tensor_scalar(out=out, in0=in0, scalar1=s1, scalar2=s2, op0=op0, op1=op1), reads, writes)

    def stt(self, out, in0, scalar, in1, op0, op1, reads, writes):
        return self.op("dve", lambda e: e.scalar_tensor_tensor(out=out, in0=in0, scalar=scalar, in1=in1, op0=op0, op1=op1), reads, writes)

    def copy(self, out, in_, reads, writes, eng="dve"):
        if eng == "act":
            return self.op("act", lambda e: e.copy(out=out, in_=in_), reads, writes)
        return self.op(eng, lambda e: e.tensor_copy(out=out, in_=in_), reads, writes)

    def memset(self, out, val, writes, eng="pool"):
        return self.op(eng, lambda e: e.memset(out, val), (), writes)

    def recip(self, out, in_, reads, writes):
        return self.op("dve", lambda e: e.reciprocal(out=out, in_=in_), reads, writes)

    def scan(self, out, data0, data1, reads, writes):
        return self.op("dve", lambda e: e.tensor_tensor_scan(out=out, data0=data0, data1=data1, initial=0.0,
                                                             op0=ALU.mult, op1=ALU.add), reads, writes)


class Slot:
    __slots__ = ("fn", "tok")

    def __init__(self, fn, tok):
        self.fn = fn
        self.tok = tok

    def __call__(self, rows, cols):
        return self.fn(rows, cols)


def slots3(buf, toks, idxs):
    return [Slot((lambda r, c, j=j: buf.t[r, j, c]), (toks[j] if isinstance(toks, list) else toks)) for j in idxs]


def slots4(buf, toks, idxs):
    return [Slot((lambda r, c, a=j // 2, b=j % 2: buf.t[r, a, b, c]), (toks[j // 2] if isinstance(toks, list) else toks)) for j in idxs]


C_ID, C_ONES, C_BO, C_IU, C_SU, C_SL, C_NSL, C_NSU = [i * 128 for i in range(8)]
C_RST = 8 * 128
C_SEL = 9 * 128
C_W = C_SEL


def make_consts():
    c = np.zeros((128, C_W), np.float32)
    i = np.arange(128)
    c[:, C_ID:C_ID + 128] = np.eye(128)
    c[:, C_ONES:C_ONES + 128] = 1.0
    bo = (i[:, None] // 64 == i[None, :] // 64).astype(np.float32)
    c[:, C_BO:C_BO + 128] = bo
    c[:, C_IU:C_IU + 128] = (i[:, None] <= i[None, :])
    c[:, C_SU:C_SU + 128] = (i[:, None] < i[None, :])
    c[:, C_SL:C_SL + 128] = (i[:, None] > i[None, :])
    c[:, C_NSL:C_NSL + 128] = -(i[:, None] > i[None, :]).astype(np.float32)
    c[:, C_NSU:C_NSU + 128] = -(i[:, None] < i[None, :]).astype(np.float32)
    c[:, C_RST:C_RST + 128] = (i[None, :] % 4 != 0)
    return c


TAGS = None
SEQ_GDN = False
SEQ_RW = False


def build(SEQ, NS, DEPTH):
    N_AB = (DEPTH + 1) // 2
    N_C = DEPTH // 2
    TP = N_META + SEQ
    nc = bass.Bass("TRN2", target_bir_lowering=False)
    nc.dge_precook = False

    def din(name, shape, dt=F32):
        return nc.dram_tensor(name, list(shape), dt, kind="ExternalInput").ap()

    def dout(name, shape):
        return nc.dram_tensor(name, list(shape), F32, kind="ExternalOutput").ap()

    xp = din("xp", [TP, D])
    xs = din("xs", [NS * 4, D])
    st_gla = din("st_gla", [N_AB, NS, 4, 64, 128])
    st_rwkv = din("st_rwkv", [N_AB, NS, 8, 64, 64])
    st_shift = din("st_shift", [N_AB, NS, D])
    st_gdn = din("st_gdn", [max(N_C, 1), NS, 8, 128, 128])
    st_gconv = din("st_gconv", [max(N_C, 1), NS, 3, 3072])
    st_fconv = din("st_fconv", [DEPTH, NS, 2, D_FF])
    consts = din("consts", [128, C_W])
    w_in_ab = din("w_in_ab", [N_AB, D, AB_COLS], F32R)
    gla_gate_w2 = din("gla_gate_w2", [N_AB, 16, 256])
    gla_gate_b = din("gla_gate_b", [N_AB, 256])
    gla_norm_g = din("gla_norm_g", [N_AB, 512])
    rwkv_mu = din("rwkv_mu", [N_AB, 1792])
    rwkv_w0 = din("rwkv_w0", [N_AB, 512])
    rwkv_w2 = din("rwkv_w2", [N_AB, 64, 512])
    rwkv_a0 = din("rwkv_a0", [N_AB, 512])
    rwkv_a2 = din("rwkv_a2", [N_AB, 64, 512])
    rwkv_g2 = din("rwkv_g2", [N_AB, 128, 512])
    rwkv_k_k = din("rwkv_k_k", [N_AB, 512])
    rwkv_k_a = din("rwkv_k_a", [N_AB, 512])
    rwkv_r_k = din("rwkv_r_k", [N_AB, 512])
    rwkv_ln_g = din("rwkv_ln_g", [N_AB, 512])
    rwkv_ln_b = din("rwkv_ln_b", [N_AB, 512])
    w_out_ab = din("w_out_ab", [N_AB, D, D], F32R)
    w_in_c = din("w_in_c", [max(N_C, 1), D, C_COLS], F32R)
    gdn_conv_w = din("gdn_conv_w", [max(N_C, 1), 4, 3072])
    gdn_A_log = din("gdn_A_log", [max(N_C, 1), 8])
    gdn_dt_bias = din("gdn_dt_bias", [max(N_C, 1), 8])
    gdn_norm_g = din("gdn_norm_g", [max(N_C, 1), 128])
    w_out_c = din("w_out_c", [max(N_C, 1), D, D], F32R)
    w_up = din("w_up", [DEPTH, D, 2 * D_FF], F32R)
    ffn_conv_w = din("ffn_conv_w", [DEPTH, 3, D_FF])
    ffn_conv_b = din("ffn_conv_b", [DEPTH, D_FF])
    w_down = din("w_down", [DEPTH, D_FF, D], F32R)
    ln_mix_g = din("ln_mix_g", [DEPTH, D])
    ln_mix_b = din("ln_mix_b", [DEPTH, D])
    ln_ffn_g = din("ln_ffn_g", [DEPTH, D])
    ln_ffn_b = din("ln_ffn_b", [DEPTH, D])

    y_p = dout("y_p", [SEQ, D])
    y_s = dout("y_s", [NS * 4, D])
    o_gla = [dout("p_gla", [N_AB, 1, 4, 64, 128]), dout("s_gla", [N_AB, NS, 4, 64, 128])]
    o_rwkv = [dout("p_rwkv", [N_AB, 1, 8, 64, 64]), dout("s_rwkv", [N_AB, NS, 8, 64, 64])]
    o_shift = [dout("p_shift", [N_AB, 1, D]), dout("s_shift", [N_AB, NS, D])]
    o_gdn = [dout("p_gdn", [max(N_C, 1), 1, 8, 128, 128]), dout("s_gdn", [max(N_C, 1), NS, 8, 128, 128])]
    o_gconv = [dout("p_gconv", [max(N_C, 1), 1, 3, 3072]), dout("s_gconv", [max(N_C, 1), NS, 3, 3072])]
    o_fconv = [dout("p_fconv", [DEPTH, 1, 2, D_FF]), dout("s_fconv", [DEPTH, NS, 2, D_FF])]

    es = ExitStack()
    k = K(nc, es)
    if TAGS is not None:
        k.tags = TAGS
    k.init_psum()
    NMAX = 128

    cst = k.sbuf("cst", [128, C_W])
    k.dma("sp", cst[:, :], consts[:, :], [], [cst])
    RC = [cst]

    def cm(off, r, c):
        return cst[0:r, off:off + c]

    ident = cst[:, C_ID:C_ID + 128]
    ones = cst[:, C_ONES:C_ONES + 128]
    bo64 = cst[:, C_BO:C_BO + 128]

    def load_fm(name, src, nch):
        b = k.sbuf(name, [128, nch])
        k.dma("pool", b[:, :], src.rearrange("(c p) -> p c", p=128), [], [b], allow_slow_non_contiguous=True)
        return b

    PAB = []
    for i in range(N_AB):
        p = {}
        p["gb"] = load_fm("gb", gla_gate_b[i], 2)
        p["gng"] = load_fm("gng", gla_norm_g[i], 4)
        p["mu"] = load_fm("mu", rwkv_mu[i], 14)
        for nm, src in (("w0", rwkv_w0), ("a0", rwkv_a0), ("kk", rwkv_k_k), ("ka", rwkv_k_a), ("rk", rwkv_r_k),
                        ("lg", rwkv_ln_g), ("lb", rwkv_ln_b)):
            p[nm] = load_fm(nm, src[i], 4)
        p["gw2"] = k.sbuf("gw2", [16, 256])
        k.dma("pool", p["gw2"][:, :], gla_gate_w2[i], [], [p["gw2"]])
        p["w2a2"] = k.sbuf("w2a2", [128, 512])
        k.dma("pool", p["w2a2"][0:64, :], rwkv_w2[i], [], [p["w2a2"]])
        k.dma("pool", p["w2a2"][64:128, :], rwkv_a2[i], [], [p["w2a2"]])
        p["g2"] = k.sbuf("g2", [128, 512])
        k.dma("pool", p["g2"][:, :], rwkv_g2[i], [], [p["g2"]])
        p["ngb"] = k.sbuf("ngb", [128, 2])
        k.ts(p["ngb"][:, :], p["gb"][:, :], -1.0, None, ALU.mult, None, [p["gb"]], [p["ngb"]])
        p["omka"] = k.sbuf("omka", [128, 4])
        k.ts(p["omka"][:, :], p["ka"][:, :], -1.0, 1.0, ALU.mult, ALU.add, [p["ka"]], [p["omka"]])
        PAB.append(p)
    PC = []
    for i in range(N_C):
        p = {}
        p["cw"] = k.sbuf("gcw", [128, 4, 24])
        for j in range(4):
            k.dma("pool", p["cw"][:, j, :], gdn_conv_w[i, j].rearrange("(c p) -> p c", p=128), [], [p["cw"]],
                  allow_slow_non_contiguous=True)
        p["alog"] = k.sbuf("alog", [8, 1])
        k.dma("pool", p["alog"][:, :], gdn_A_log[i].rearrange("(p o) -> p o", o=1), [], [p["alog"]])
        p["dtb"] = k.sbuf("dtb", [8, 1])
        k.dma("pool", p["dtb"][:, :], gdn_dt_bias[i].rearrange("(p o) -> p o", o=1), [], [p["dtb"]])
        p["nexpA"] = k.sbuf("nexpA", [8, 1])
        k.act(p["nexpA"][:, :], p["alog"][:, :], AF.Exp, [p["alog"]], [p["nexpA"]])
        k.ts(p["nexpA"][:, :], p["nexpA"][:, :], -1.0, None, ALU.mult, None, [p["nexpA"]], [p["nexpA"]])
        p["ng"] = k.sbuf("gdng", [128, 1])
        k.dma("pool", p["ng"][:, :], gdn_norm_g[i].rearrange("(p o) -> p o", o=1), [], [p["ng"]])
        p["wba"] = k.sbuf("wba", [128, 8, 16])
        k.dma("pool", p["wba"][:, :, :], w_in_c[i][:, 4096:4112].rearrange("(k p) c -> p k c", p=128).bitcast(F32), [], [p["wba"]])
        PC.append(p)
    PL = []
    for l in range(DEPTH):
        p = {}
        p["fcw"] = k.sbuf("fcw", [128, 3, 22])
        for j in range(3):
            k.dma("pool", p["fcw"][:, j, :], ffn_conv_w[l, j].rearrange("(c p) -> p c", p=128), [], [p["fcw"]],
                  allow_slow_non_contiguous=True)
        p["fcb"] = load_fm("fcb", ffn_conv_b[l], 22)
        p["lmg"] = load_fm("lmg", ln_mix_g[l], 8)
        p["lmb"] = load_fm("lmb", ln_mix_b[l], 8)
        p["lfg"] = load_fm("lfg", ln_ffn_g[l], 8)
        p["lfb"] = load_fm("lfb", ln_ffn_b[l], 8)
        PL.append(p)

    def zeros(name, shape):
        b = k.sbuf(name, shape)
        k.memset(b.t[:], 0.0, [b])
        return b

    P_gla = [zeros("pgla", [128, 2, 128]) for _ in range(N_AB)]
    P_rwkv = [zeros("prwkv", [128, 4, 128]) for _ in range(N_AB)]
    P_gdn = [zeros("pgdn", [128, 8, 128]) for _ in range(N_C)]
    C_xlast = [zeros("cxl", [128, 8, 1]) for _ in range(N_AB)]
    C_gconv = [zeros("cgc", [128, 24, 3]) for _ in range(N_C)]
    C_fconv = [zeros("cfc", [128, 22, 2]) for _ in range(DEPTH)]
    RAW = [zeros("srw_raw", [128, 4, 128]) for _ in range(1)]
    upad = zeros("upad", [128, 8, 128])
    vpad = zeros("vpad", [128, 8, 128])

    XE0 = k.rot("xe", [128, 8, NMAX + 16], 1)
    k.copy(XE0.t[:, :, 0:128].bitcast(F32R), upad[:, :, :], [upad], [XE0], eng="pool")
    k.copy(XE0.t[:, :, 128:144].bitcast(F32R), upad[:, :, 0:16], [upad], [XE0], eng="pool")
    WT_ELEMS = 2048
    WT_N = 3
    GL = k.sbuf("gla", [128, 16, NMAX])
    GL_T = k.alias(GL, 16)
    AT = k.sbuf("aT", [128, 22, NMAX])
    AT_T = k.alias(AT, 22)
    ARENA = k.sbuf("arena", [128, 48, NMAX])
    ARENA_T = k.alias(ARENA, 48)

    def proj(W, kc, c0, ncols, rhs_fn, rbufs, N, consume, gw):
        g0 = c0
        while g0 < c0 + ncols:
            w = min(gw, c0 + ncols - g0)
            kmax = WT_ELEMS // w
            if kc > kmax:
                assert w <= 128
                cw = w
                ps = k.psum()
                fast = (cw == 128 and N % 2 == 0)
                for k0 in range(0, kc, kmax):
                    nk = min(kmax, kc - k0)
                    wt = k.rot("wt", [128, WT_ELEMS], WT_N, F32R)
                    wvr = wt.t[:, 0:nk * w].rearrange("p (k c) -> p k c", c=w)
                    wv = wvr.bitcast(F32)
                    k.dma("sp", wvr, W[k0 * 128:(k0 + nk) * 128, g0:g0 + w].rearrange("(k p) c -> p k c", p=128), [], [wt])
                    for kk in range(nk):
                        lh, rh = wv[:, kk, 0:cw], rhs_fn(k0 + kk)
                        if fast:
                            lh, rh = wvr[:, kk, 0:cw], rh.bitcast(F32R)
                        k.mm(ps, ps[0:cw, 0:N], lh, rh, k0 + kk == kc - 1, [wt] + rbufs)
                consume(g0, cw, ps)
                g0 += w
                continue
            wt = k.rot("wt", [128, WT_ELEMS], WT_N, F32R)
            wvr = wt.t[:, 0:kc * w].rearrange("p (k c) -> p k c", c=w)
            wv = wvr.bitcast(F32)
            k.dma("sp", wvr, W[:, g0:g0 + w].rearrange("(k p) c -> p k c", p=128), [], [wt])
            j0 = 0
            while j0 < w:
                cw = min(128, w - j0)
                ps = k.psum()
                fast = (cw == 128 and N % 2 == 0)
                for kk in range(kc):
                    lh, rh = wv[:, kk, j0:j0 + cw], rhs_fn(kk)
                    if fast:
                        lh, rh = wvr[:, kk, j0:j0 + cw], rh.bitcast(F32R)
                    k.mm(ps, ps[0:cw, 0:N], lh, rh, kk == kc - 1, [wt] + rbufs)
                consume(g0 + j0, cw, ps)
                j0 += cw
            g0 += w

    def store_rows(src_fn, sbufs, nch, ncols, dst, view=None):
        for c8 in range(0, nch, 8):
            n8 = min(8, nch - c8)
            stg = k.rot("stg", [128, 1024], 1)
            for c4 in range(c8, c8 + n8, 4):
                ps = k.psum()
                n4 = min(4, c8 + n8 - c4)
                for j in range(n4):
                    if view is not None:
                        sl_ = 8 + j % 2
                        k.copy(GL[:, sl_, 0:ncols].rearrange("p (a b) -> p a b", b=view[1]), src_fn(c4 + j), sbufs, [GL_T[sl_]], eng="pool")
                        k.tr(ps, ps[0:ncols, j * 128:(j + 1) * 128], GL[:, sl_, 0:ncols], ident, [GL_T[sl_]] + RC)
                        continue
                    k.tr(ps, ps[0:ncols, j * 128:(j + 1) * 128], src_fn(c4 + j), ident, sbufs + RC)
                k.copy(stg[0:ncols, (c4 - c8) * 128:(c4 - c8 + n4) * 128], ps[0:ncols, 0:n4 * 128], [ps], [stg],
                       eng="act" if (c4 // 4) % 2 else "dve")
            k.dma("pool", dst[:, c8 * 128:(c8 + n8) * 128], stg[0:ncols, 0:n8 * 128], [stg], [])

    def layer_norm(tb, N, g, b, xo):
        sq = k.rot("oT", [128, 8, NMAX], 1)
        k.act(sq[:, :, 0:N], tb[:, :, 0:N], AF.Square, [tb], [sq])
        ps1 = k.psum()
        for m in range(8):
            k.mm(ps1, ps1[:, 0:N], ones, tb[:, m, 0:N], m == 7, [tb] + RC)
        ps2 = k.psum()
        for m in range(8):
            k.mm(ps2, ps2[:, 0:N], ones, sq[:, m, 0:N], m == 7, [sq] + RC)
        st = k.rot("pc_st", [128, 4, NMAX], 1)
        mean, var, rstd = st[:, 0, 0:N], st[:, 1, 0:N], st[:, 2, 0:N]
        k.ts(mean, ps1[:, 0:N], 1.0 / D, None, ALU.mult, None, [ps1], [st])
        k.tt(st[:, 3, 0:N], mean, mean, ALU.mult, [st], [st])
        k.stt(var, ps2[:, 0:N], 1.0 / D, st[:, 3, 0:N], ALU.mult, ALU.subtract, [ps2, st], [st])
        k.ts(var, var, LN_EPS, None, ALU.add, None, [st], [st])
        k.act(var, var, AF.Sqrt, [st], [st])
        k.recip(rstd, var, [st], [st])
        for m in range(8):
            k.tt(sq[:, m, 0:N], tb[:, m, 0:N], mean, ALU.subtract, [tb, st], [sq])
            k.tt(sq[:, m, 0:N], sq[:, m, 0:N], rstd, ALU.mult, [sq, st], [sq])
            k.ts(xo[:, m, 0:N].bitcast(F32R), sq[:, m, 0:N], g[:, m:m + 1], b[:, m:m + 1], ALU.mult, ALU.add, [sq, g, b], [xo])

    def tri_solve(Pp, Pm, pbuf, R0, rbuf, L, w, Ls_nil, final_out, final_bufs):
        nlev = 0
        while (1 << nlev) < Ls_nil:
            nlev += 1
        cur_p, cur_m, cur_r, cur_rb = Pp, Pm, R0, rbuf
        cur_pb = pbuf
        if nlev == 0:
            k.copy(final_out, cur_r, [cur_rb], final_bufs)
            return
        for j in range(nlev):
            ps = k.psum()
            k.mm(ps, ps[0:L, 0:w], cur_p, cur_r, True, [cur_pb, cur_rb])
            if j == nlev - 1:
                k.tt(final_out, ps[0:L, 0:w], cur_r, ALU.add, [ps, cur_rb], final_bufs)
                break
            nr = k.rot("solr", [128, 128], 2)
            k.tt(nr[0:L, 0:w], ps[0:L, 0:w], cur_r, ALU.add, [ps, cur_rb], [nr])
            psq = k.psum()
            k.mm(psq, psq[0:L, 0:L], cur_m, cur_p, False, [cur_pb])
            k.mm(psq, psq[0:L, 128:128 + L], cur_p, cur_m, True, [cur_pb])
            npb = k.rot("solp", [128, 256], 2)
            k.copy(npb[0:L, 0:L], psq[0:L, 0:L], [psq], [npb], eng="act")
            k.copy(npb[0:L, 128:128 + L], psq[0:L, 128:128 + L], [psq], [npb], eng="act")
            cur_p, cur_m, cur_pb = npb[0:L, 0:L], npb[0:L, 128:128 + L], npb
            cur_r, cur_rb = nr[0:L, 0:w], nr

    def tri_solve_multi(jobs, Ls_nil):
        nlev = 0
        while (1 << nlev) < Ls_nil:
            nlev += 1
        if nlev == 0:
            for jb in jobs:
                k.copy(jb["out"], jb["R0"], jb["rtoks"], jb["otoks"])
            return
        for jb in jobs:
            jb["cp"], jb["cm"], jb["cpt"] = jb["Pp"], jb["Pm"], jb["ptoks"]
            jb["cr"], jb["crt"] = jb["R0"], jb["rtoks"]
        for j in range(nlev):
            lastlev = (j == nlev - 1)
            pss = []
            for jb in jobs:
                L, w = jb["L"], jb["w"]
                ps = k.psum()
                k.mm(ps, ps[0:L, 0:w], jb["cp"], jb["cr"], True, jb["cpt"] + jb["crt"])
                pss.append(ps)
            for jb, ps in zip(jobs, pss):
                L, w = jb["L"], jb["w"]
                if lastlev:
                    k.tt(jb["out"], ps[0:L, 0:w], jb["cr"], ALU.add, [ps] + jb["crt"], jb["otoks"])
                else:
                    rs = jb["RA"] if j % 2 == 0 else jb["RB"]
                    k.tt(rs(slice(0, L), slice(0, w)), ps[0:L, 0:w], jb["cr"], ALU.add, [ps] + jb["crt"], [rs.tok])
                    jb["nr"], jb["nrt"] = rs(slice(0, L), slice(0, w)), [rs.tok]
            if lastlev:
                break
            pqs = []
            for jb in jobs:
                L = jb["L"]
                psq = k.psum()
                k.mm(psq, psq[0:L, 0:L], jb["cm"], jb["cp"], False, jb["cpt"])
                k.mm(psq, psq[0:L, 128:128 + L], jb["cp"], jb["cm"], True, jb["cpt"])
                pqs.append(psq)
            for ji, (jb, psq) in enumerate(zip(jobs, pqs)):
                L = jb["L"]
                pp = jb["PA"] if j % 2 == 0 else jb["PB"]
                k.copy(pp[0](slice(0, L), slice(0, L)), psq[0:L, 0:L], [psq], [pp[0].tok], eng="act")
                k.copy(pp[1](slice(0, L), slice(0, L)), psq[0:L, 128:128 + L], [psq], [pp[1].tok], eng="act")
                jb["cp"], jb["cm"] = pp[0](slice(0, L), slice(0, L)), pp[1](slice(0, L), slice(0, L))
                jb["cpt"] = [pp[0].tok, pp[1].tok]
                jb["cr"], jb["crt"] = jb["nr"], jb["nrt"]

    def ab_mixer(i, tile, xT, mixT):
        kind, nseq, Ls, N = tile["kind"], tile["nseq"], tile["Ls"], tile["N"]
        P = PAB[i]
        Ne = nseq * (Ls + 1)
        xe = k.rot("xe", [128, 8, NMAX + 16], 1)
        if not tile.get("xe_init_done"):
            pass
        xev = xe.t[:, :, 0:Ne].rearrange("p c (s t) -> p c s t", t=Ls + 1)
        if kind == "p":
            k.copy(xev[:, :, 0, 0:1].bitcast(F32R), C_xlast[i][:, :, :], [C_xlast[i]], [xe], eng="pool")
        else:
            xl = k.rot("stg", [128, 1024], 1)
            k.dma("pool", xl[0:nseq, 0:D], st_shift[i], [], [xl])
            for c4 in (0, 4):
                ps = k.psum()
                for j in range(4):
                    k.tr(ps, ps[:, j * nseq:(j + 1) * nseq], xl[0:nseq, (c4 + j) * 128:(c4 + j + 1) * 128],
                         ident[0:nseq, 0:nseq], [xl] + RC)
                k.copy(xev[:, c4:c4 + 4, :, 0].bitcast(F32R), ps[:, 0:4 * nseq].rearrange("p (c s) -> p c s", s=nseq), [ps], [xe])
        k.copy(xev[:, :, :, 1:Ls + 1].bitcast(F32R), xT[:, :, 0:N].rearrange("p c (s t) -> p c s t", t=Ls), [xT], [xe], eng="pool")
        if kind == "p":
            k.copy(C_xlast[i][:, :, :], xT[:, :, N - 1:N], [xT, xe], [C_xlast[i]], eng="pool")
            if tile["last"]:
                store_rows(lambda c: C_xlast[i][:, c, :], [C_xlast[i]], 8, 1, o_shift[0][i])
        else:
            xlv = xT[:, :, 0:N].rearrange("p c (s t) -> p c s t", t=Ls)
            store_rows(lambda c: xlv[:, c, :, Ls - 1], [xT], 8, nseq, o_shift[1][i])

        z = k.rot("z", [128, 33, NMAX + 4], 1)
        zc = tile["zc"]

        def cons(c_abs, cw, ps):
            if c_abs < 1024:
                ci = c_abs // 128
            elif c_abs == 1024:
                ci = 8
            else:
                ci = 9 + (c_abs - 1040) // 128
            k.copy(z[0:cw, ci, 0:Ne], ps[0:cw, 0:Ne], [ps], [zc[ci]], eng="act")

        W = w_in_ab[i]
        Np = Ne + (Ne % 2)
        proj(W, 8, 0, 1024, lambda kk: xe[:, kk, 0:Np], [xe], Np, cons, 256)
        proj(W, 8, 1024, 16, lambda kk: xe[:, kk, 0:Np], [xe], Np, cons, 256)
        proj(W, 8, 1040, AB_COLS - 1040, lambda kk: xe[:, kk, 0:Np], [xe], Np, cons, 256)

        def zv(ci, rows=slice(0, 128)):
            return z.t[rows, ci, 0:Ne].rearrange("p (s t) -> p s t", t=Ls + 1)[:, :, 1:Ls + 1]

        def zp(ci, rows=slice(0, 128)):
            return z.t[rows, ci, 0:Ne].rearrange("p (s t) -> p s t", t=Ls + 1)[:, :, 0:Ls]

        def v3(ap):
            return ap.rearrange("p (s t) -> p s t", t=Ls)

        g = GL
        gA = GL_T
        for c in range(2):
            ps = k.psum()
            k.mm(ps, ps[:, 0:Ne], P["gw2"][0:16, c * 128:(c + 1) * 128], z[0:16, 8, 0:Ne], True, [P["gw2"], zc[8]])
            pv = ps[:, 0:Ne].rearrange("p (s t) -> p s t", t=Ls + 1)[:, :, 1:Ls + 1]
            k.act(v3(g[:, 10 + c, 0:N]), pv, AF.Exp, [ps, P["ngb"]], [gA[10 + c]], bias=P["ngb"][:, c:c + 1], scale=-1.0)
            k.act(g[:, 10 + c, 0:N], g[:, 10 + c, 0:N], AF.Ln, [gA[10 + c]], [gA[10 + c]], bias=1.0)
            rst = ones[:, 0:N] if kind == "p" else cst[:, C_RST:C_RST + N]
            k.scan(g[:, c, 0:N], rst, g[:, 10 + c, 0:N], [gA[10 + c]] + RC, [gA[c]])
            k.act(g[:, 2 + c, 0:N], g[:, c, 0:N], AF.Exp, [gA[c]], [gA[2 + c]], scale=-1.0 / 16)
            k.act(g[:, 4 + c, 0:N], g[:, c, 0:N], AF.Exp, [gA[c]], [gA[4 + c]], scale=1.0 / 16)
            k.stt(v3(g[:, 6 + c, 0:N]), zv(c), 0.125, v3(g[:, 2 + c, 0:N]), ALU.mult, ALU.mult, [zc[c], gA[2 + c]], [gA[6 + c]])
            k.tt(v3(g[:, 8 + c, 0:N]), zv(2 + c), v3(g[:, 4 + c, 0:N]), ALU.mult, [zc[2 + c], gA[4 + c]], [gA[8 + c]])
        for h in range(4):
            k.act(v3(g[:, 12 + h, 0:N]), zv(9 + h), AF.Silu, [zc[9 + h]], [gA[12 + h]])

        r = ARENA
        rA = ARENA_T
        zm = k.rot("zm", [128, 14, NMAX], 1)
        zmA = k.alias(zm, 14)
        for c in range(14):
            ci = 13 + c
            k.tt(v3(zm[:, c, 0:N]), zp(ci), zv(ci), ALU.subtract, [zc[ci]], [zmA[c]], eng="pool")
            k.stt(v3(zm[:, c, 0:N]), v3(zm[:, c, 0:N]), P["mu"][:, c:c + 1], zv(ci), ALU.mult, ALU.add,
                  [zmA[c], P["mu"], zc[ci]], [zmA[c]])
        k.act(r[0:64, 12, 0:N], zm[0:64, 12, 0:N], AF.Tanh, [zmA[12]], [rA[12]])
        k.copy(r[64:128, 12, 0:N], zm[64:128, 12, 0:N], [zmA[12]], [rA[12]])
        k.act(r[:, 13, 0:N], zm[:, 13, 0:N], AF.Sigmoid, [zmA[13]], [rA[13]])
        rst = ones[:, 0:N] if kind == "p" else cst[:, C_RST:C_RST + N]
        for c in range(4):
            cs = slice(c * 128, (c + 1) * 128)
            ps = k.psum()
            k.mm(ps, ps[:, 0:N], P["w2a2"][0:64, cs], r[0:64, 12, 0:N], True, [P["w2a2"], rA[12]])
            k.act(r[:, 14 + c, 0:N], ps[:, 0:N], AF.Sigmoid, [ps, P["w0"]], [rA[14 + c]], bias=P["w0"][:, c:c + 1])
            ps = k.psum()
            k.mm(ps, ps[:, 0:N], P["w2a2"][64:128, cs], r[64:128, 12, 0:N], True, [P["w2a2"], rA[12]])
            k.act(r[:, 18 + c, 0:N], ps[:, 0:N], AF.Sigmoid, [ps, P["a0"]], [rA[18 + c]], bias=P["a0"][:, c:c + 1])
            ps = k.psum()
            k.mm(ps, ps[:, 0:N], P["g2"][:, cs], r[:, 13, 0:N], True, [P["g2"], rA[13]])
            k.copy(r[:, 22 + c, 0:N], ps[:, 0:N], [ps], [rA[22 + c]], eng="act")
            k.ts(r[:, 26 + c, 0:N], zm[:, 4 + c, 0:N], P["kk"][:, c:c + 1], None, ALU.mult, None, [zmA[4 + c], P["kk"]], [rA[26 + c]])
            k.act(r[:, 38 + c, 0:N], r[:, 26 + c, 0:N], AF.Square, [rA[26 + c]], [rA[38 + c]])
            ps = k.psum()
            k.mm(ps, ps[:, 0:N], bo64, r[:, 38 + c, 0:N], True, [rA[38 + c]] + RC)
            k.act(r[:, 38 + c, 0:N], ps[:, 0:N], AF.Sqrt, [ps], [rA[38 + c]], bias=NORM_EPS)
            k.recip(r[:, 38 + c, 0:N], r[:, 38 + c, 0:N], [rA[38 + c]], [rA[38 + c]])
            k.tt(r[:, 26 + c, 0:N], r[:, 26 + c, 0:N], r[:, 38 + c, 0:N], ALU.mult, [rA[26 + c], rA[38 + c]], [rA[26 + c]])
            k.ts(r[:, 30 + c, 0:N], r[:, 18 + c, 0:N], P["ka"][:, c:c + 1], P["omka"][:, c:c + 1], ALU.mult, ALU.add,
                 [rA[18 + c], P["ka"], P["omka"]], [rA[30 + c]])
            k.tt(r[:, 30 + c, 0:N], r[:, 30 + c, 0:N], zm[:, 4 + c, 0:N], ALU.mult, [rA[30 + c], zmA[4 + c]], [rA[30 + c]])
            k.tt(r[:, 42 + c, 0:N], r[:, 26 + c, 0:N], r[:, 18 + c, 0:N], ALU.mult, [rA[26 + c], rA[18 + c]], [rA[42 + c]])
            k.scan(r[:, 34 + c, 0:N], rst, r[:, 14 + c, 0:N], [rA[14 + c]] + RC, [rA[34 + c]])
        ar = k.rot("ar", [128, 4, 2, NMAX], 1)
        bk = k.rot("bk", [128, 4, 2, NMAX], 1)
        arA = k.alias(ar, 4)
        bkA = k.alias(bk, 4)
        for c in range(4):
            k.act(r[:, 38 + c, 0:N], r[:, 34 + c, 0:N], AF.Exp, [rA[34 + c]], [rA[38 + c]], scale=-C0)
            k.tt(ar[:, c, 1, 0:N], zm[:, c, 0:N], r[:, 38 + c, 0:N], ALU.mult, [zmA[c], rA[38 + c]], [arA[c]])
            k.tt(r[:, 38 + c, 0:N], r[:, 34 + c, 0:N], r[:, 14 + c, 0:N], ALU.subtract, [rA[34 + c], rA[14 + c]], [rA[38 + c]])
            k.act(r[:, 38 + c, 0:N], r[:, 38 + c, 0:N], AF.Exp, [rA[38 + c]], [rA[38 + c]], scale=-C0)
            k.stt(ar[:, c, 0, 0:N], r[:, 26 + c, 0:N], -1.0, r[:, 38 + c, 0:N], ALU.mult, ALU.mult, [rA[26 + c], rA[38 + c]], [arA[c]])
            k.act(r[:, 38 + c, 0:N], r[:, 34 + c, 0:N], AF.Exp, [rA[34 + c]], [rA[38 + c]], scale=C0)
            k.tt(bk[:, c, 0, 0:N], r[:, 42 + c, 0:N], r[:, 38 + c, 0:N], ALU.mult, [rA[42 + c], rA[38 + c]], [bkA[c]])
            k.tt(bk[:, c, 1, 0:N], r[:, 30 + c, 0:N], r[:, 38 + c, 0:N], ALU.mult, [rA[30 + c], rA[38 + c]], [bkA[c]])

        oT = k.rot("oT", [128, 8, NMAX], 1)
        oA = k.alias(oT, 8)

        for s in range(nseq):
            c0 = s * Ls
            L = Ls
            cols = slice(c0, c0 + L)
            last = c0 + L - 1
            if kind == "p":
                Mg, Mr = P_gla[i], P_rwkv[i]
            else:
                Mg = k.rot("sgla", [128, 2, 128], 2)
                k.dma("sp", Mg[:, :, :], st_gla[i, s].rearrange("(c two) kk v -> (two kk) c v", two=2), [], [Mg])
                raw = RAW[0]
                for two in range(2):
                    k.dma("sp", raw[two * 64:(two + 1) * 64, :, two * 64:(two + 1) * 64],
                          st_rwkv[i, s].rearrange("(c two) v kk -> two v c kk", two=2)[two], [], [raw])
                Mr = k.rot("srw", [128, 4, 128], 1)
                ps = k.psum()
                for c in range(4):
                    k.tr(ps, ps[:, c * 128:(c + 1) * 128], raw[:, c, :], ident, [raw] + RC)
                k.copy(Mr[:, :, :], ps[:, 0:512].rearrange("p (c v) -> p c v", v=128), [ps], [Mr], eng="act")

            vt = k.rot("gvt", [128, 4, 128], 1)
            ps = k.psum()
            for h in range(4):
                k.tr(ps, ps[0:L, h * 128:(h + 1) * 128], z.t[:, 4 + h, s * (Ls + 1) + 1:s * (Ls + 1) + 1 + L], ident, [zc[4 + h]] + RC)
            k.copy(vt[0:L, :, :], ps[0:L, 0:512].rearrange("p (h v) -> p h v", v=128), [ps], [vt], eng="act")
            kt = k.rot("gkt", [128, 2, 128], 1)
            ps = k.psum()
            for c in range(2):
                k.ts(g[:, 10 + c, cols], g[:, c, cols], g[:, c, last:last + 1], None, ALU.subtract, None, [gA[c]], [gA[10 + c]])
                k.act(g[:, 10 + c, cols], g[:, 10 + c, cols], AF.Exp, [gA[10 + c]], [gA[10 + c]], scale=1.0 / 16)
                k.tt(g[:, 10 + c, cols], g[:, 10 + c, cols],
                     z.t[:, 2 + c, s * (Ls + 1) + 1:s * (Ls + 1) + 1 + L], ALU.mult, [gA[10 + c], zc[2 + c]], [gA[10 + c]])
                k.tr(ps, ps[0:L, c * 128:(c + 1) * 128], g[:, 10 + c, cols], ident, [gA[10 + c]] + RC)
            k.copy(kt[0:L, :, :], ps[0:L, 0:256].rearrange("p (c v) -> p c v", v=128), [ps], [kt])
            for h in range(4):
                c, two = h // 2, h % 2
                rows = slice(two * 64, two * 64 + 64)
                ps = k.psum()
                k.mm(ps, ps[0:L, 0:L], g[rows, 8 + c, cols], g[rows, 6 + c, cols], True, [gA[8 + c], gA[6 + c]])
                at = k.rot("gat", [128, 128], 2)
                k.tt(at[0:L, 0:L], ps[0:L, 0:L], cm(C_IU, L, L), ALU.mult, [ps] + RC, [at])
                po = k.psum()
                k.mm(po, po[:, 0:L], Mg[rows, c, :], g[rows, 6 + c, cols], False, [Mg, gA[6 + c]])
                k.mm(po, po[:, 0:L], vt[0:L, h, :], at[0:L, 0:L], True, [vt, at])
                k.copy(oT[:, h, cols], po[:, 0:L], [po], [oA[h]], eng="act")
                pu = k.psum()
                k.mm(pu, pu[:, 0:128], kt[0:L, c, :], vt[0:L, h, :], True, [kt, vt])
                k.stt(Mg[rows, c, :], Mg[rows, c, :], g[rows, 2 + c, last:last + 1], pu[rows, 0:128], ALU.mult, ALU.add,
                      [Mg, gA[2 + c], pu], [Mg])

            vtr = vpad
            ps = k.psum()
            for c in range(4):
                k.tr(ps, ps[0:L, c * 128:(c + 1) * 128], zm[:, 8 + c, cols], ident, [zmA[8 + c]] + RC)
            for c in range(4):
                for two in range(2):
                    h = 2 * c + two
                    k.copy(vpad[0:L, h, two * 64:two * 64 + 64], ps[0:L, c * 128 + two * 64:c * 128 + two * 64 + 64], [ps], [vpad],
                           eng="act" if two else "dve")
            ket = k.rot("rket", [128, 4, 2, 128], 1)
            for c in range(4):
                k.ts(r[:, 38 + c, cols], r[:, 34 + c, cols], r[:, 34 + c, last:last + 1], None, ALU.subtract, None, [rA[34 + c]], [rA[38 + c]])
                k.act(r[:, 38 + c, cols], r[:, 38 + c, cols], AF.Exp, [rA[38 + c]], [rA[38 + c]], scale=C0)
                tmp = k.rot("rtmp", [128, 2, 128], 1)
                k.tt(tmp[:, 0, 0:L], r[:, 30 + c, cols], r[:, 38 + c, cols], ALU.mult, [rA[30 + c], rA[38 + c]], [tmp])
                k.tt(tmp[:, 1, 0:L], r[:, 42 + c, cols], r[:, 38 + c, cols], ALU.mult, [rA[42 + c], rA[38 + c]], [tmp])
                ps = k.psum()
                k.tr(ps, ps[0:L, 0:128], tmp[:, 0, 0:L], ident, [tmp] + RC)
                k.tr(ps, ps[0:L, 128:256], tmp[:, 1, 0:L], ident, [tmp] + RC)
                k.copy(ket[0:L, c, :, :], ps[0:L, 0:256].rearrange("p (a v) -> p a v", v=128), [ps], [ket], eng="act")
            tbs = k.rot("tb", [128, 8, NMAX], 1)
            SC = slots3(ARENA, ARENA_T, range(0, 12)) + slots3(tbs, tbs, range(0, 8))
            sl_ = slice(0, L)
            for cp in (0, 2):
                NKS, jobs = {}, []
                for c in (cp, cp + 1):
                    NK = []
                    for two in range(2):
                        rows = slice(two * 64, two * 64 + 64)
                        pb = k.psum()
                        k.mm(pb, pb[0:L, 0:L], bk[rows, c, 0, cols], ar[rows, c, 0, cols], False, [bkA[c], arA[c]])
                        k.mm(pb, pb[0:L, L:2 * L], bk[rows, c, 0, cols], ar[rows, c, 1, cols], True, [bkA[c], arA[c]])
                        pk = k.psum()
                        k.mm(pk, pk[0:L, 0:L], bk[rows, c, 1, cols], ar[rows, c, 0, cols], False, [bkA[c], arA[c]])
                        k.mm(pk, pk[0:L, L:2 * L], bk[rows, c, 1, cols], ar[rows, c, 1, cols], True, [bkA[c], arA[c]])
                        pn = k.psum()
                        k.mm(pn, pn[0:L, 0:L], ar[rows, c, 0, cols], bk[rows, c, 0, cols], True, [bkA[c], arA[c]])
                        nb = k.rot("rnb", [128, 5, 128], 4)
                        k.tt(nb[0:L, 0, 0:L], pb[0:L, 0:L], cm(C_SU, L, L), ALU.mult, [pb] + RC, [nb])
                        k.tt(nb[0:L, 1, 0:L], pb[0:L, L:2 * L], cm(C_IU, L, L), ALU.mult, [pb] + RC, [nb])
                        k.tt(nb[0:L, 2, 0:L], pk[0:L, 0:L], cm(C_SU, L, L), ALU.mult, [pk] + RC, [nb])
                        k.tt(nb[0:L, 3, 0:L], pk[0:L, L:2 * L], cm(C_IU, L, L), ALU.mult, [pk] + RC, [nb])
                        k.tt(nb[0:L, 4, 0:L], pn[0:L, 0:L], cm(C_SL, L, L), ALU.mult, [pn] + RC, [nb])
                        NK.append(nb)
                    NKS[c] = NK
                    pr = k.psum()
                    k.mm(pr, pr[0:L, 0:128], ar[:, c, 0, cols], Mr[:, c, :], False, [arA[c], Mr])
                    for two in range(2):
                        k.mm(pr, pr[0:L, 0:128], NK[two][0:L, 2, 0:L], vpad[0:L, 2 * c + two, :], two == 1, [NK[two], vpad])
                    r0 = k.rot("rr0", [128, 128], 2)
                    k.copy(r0[0:L, :], pr[0:L, 0:128], [pr], [r0], eng="act")
                    for two in range(2):
                        h = 2 * c + two
                        ji = len(jobs)
                        sc = SC[5 * ji:5 * ji + 5]
                        rsl = sc[4]
                        RA_ = Slot(rsl.fn, rsl.tok)
                        RB_ = Slot((lambda r_, c_, f=rsl.fn: f(r_, slice(64 + c_.start, 64 + c_.stop))), rsl.tok)
                        jobs.append(dict(Pp=NK[two][0:L, 0, 0:L], Pm=NK[two][0:L, 4, 0:L], ptoks=[NK[two]],
                                         R0=r0[0:L, two * 64:two * 64 + 64], rtoks=[r0], L=L, w=64,
                                         out=upad[0:L, h, two * 64:two * 64 + 64], otoks=[upad],
                                         PA=sc[0:2], PB=sc[2:4], RA=RA_, RB=RB_))
                if SEQ_RW:
                    for jb_ in jobs:
                        tri_solve_multi([jb_], L)
                else:
                    tri_solve_multi(jobs, L)
                for c in (cp, cp + 1):
                    NK = NKS[c]
                    py = k.psum()
                    k.mm(py, py[:, 0:L], Mr[:, c, :], ar[:, c, 1, cols], False, [Mr, arA[c]])
                    for two in range(2):
                        h = 2 * c + two
                        k.mm(py, py[:, 0:L], upad[0:L, h, :], NK[two][0:L, 1, 0:L], False, [upad, NK[two]])
                        k.mm(py, py[:, 0:L], vpad[0:L, h, :], NK[two][0:L, 3, 0:L], two == 1, [vpad, NK[two]])
                    k.copy(oT[:, 4 + c, cols], py[:, 0:L], [py], [oA[4 + c]], eng="act")
                    pu = k.psum()
                    k.mm(pu, pu[:, 0:128], ket[0:L, c, 0, :], vpad[0:L, 2 * c, :], False, [ket, vpad])
                    k.mm(pu, pu[:, 0:128], ket[0:L, c, 0, :], vpad[0:L, 2 * c + 1, :], False, [ket, vpad])
                    k.mm(pu, pu[:, 0:128], ket[0:L, c, 1, :], upad[0:L, 2 * c, :], False, [ket, upad])
                    k.mm(pu, pu[:, 0:128], ket[0:L, c, 1, :], upad[0:L, 2 * c + 1, :], True, [ket, upad])
                    ut = k.rot("rut", [128, 128], 1)
                    k.tt(ut[:, :], pu[:, 0:128], bo64, ALU.mult, [pu] + RC, [ut])
                    de = k.rot("rde", [128, 1], 2)
                    k.act(de[:, :], r[:, 34 + c, last:last + 1], AF.Exp, [rA[34 + c]], [de], scale=-C0)
                    k.stt(Mr[:, c, :], Mr[:, c, :], de[:, 0:1], ut[:, :], ALU.mult, ALU.add, [Mr, de, ut], [Mr])

            if kind == "s" or tile["last"]:
                gi = 0 if kind == "p" else 1
                si = 0 if kind == "p" else s
                k.dma("pool", o_gla[gi][i, si].rearrange("(c two) kk v -> (two kk) c v", two=2), Mg[:, :, :], [Mg], [])
                ps = k.psum()
                for c in range(4):
                    k.tr(ps, ps[:, c * 128:(c + 1) * 128], Mr[:, c, :], ident, [Mr] + RC)
                so = k.rot("srw_out", [128, 4, 128], 1)
                k.copy(so[:, :, :], ps[:, 0:512].rearrange("p (c v) -> p c v", v=128), [ps], [so])
                for two in range(2):
                    k.dma("pool", o_rwkv[gi][i, si].rearrange("(c two) v kk -> two v c kk", two=2)[two],
                          so[two * 64:(two + 1) * 64, :, two * 64:(two + 1) * 64], [so], [])

        for h in range(4):
            sq = k.rot("pc_sq", [128, NMAX], 2)
            k.act(sq[:, 0:N], oT[:, h, 0:N], AF.Square, [oA[h]], [sq])
            ps = k.psum()
            k.mm(ps, ps[:, 0:N], ones, sq[:, 0:N], True, [sq] + RC)
            k.act(sq[:, 0:N], ps[:, 0:N], AF.Sqrt, [ps], [sq], bias=NORM_EPS, scale=1.0 / 128)
            k.recip(sq[:, 0:N], sq[:, 0:N], [sq], [sq])
            k.tt(sq[:, 0:N], sq[:, 0:N], oT[:, h, 0:N], ALU.mult, [sq, oA[h]], [sq])
            k.stt(mixT[:, h, 0:N].bitcast(F32R), sq[:, 0:N], P["gng"][:, h:h + 1], g[:, 12 + h, 0:N], ALU.mult, ALU.mult,
                  [sq, P["gng"], gA[12 + h]], [mixT])
        for c in range(4):
            sq = k.rot("pc_sq", [128, NMAX], 2)
            st = k.rot("pc_st", [128, 4, NMAX], 1)
            k.act(sq[:, 0:N], oT[:, 4 + c, 0:N], AF.Square, [oA[4 + c]], [sq])
            p1 = k.psum()
            k.mm(p1, p1[:, 0:N], bo64, oT[:, 4 + c, 0:N], True, [oA[4 + c]] + RC)
            p2 = k.psum()
            k.mm(p2, p2[:, 0:N], bo64, sq[:, 0:N], True, [sq] + RC)
            k.ts(st[:, 0, 0:N], p1[:, 0:N], 1.0 / 64, None, ALU.mult, None, [p1], [st])
            k.tt(st[:, 1, 0:N], st[:, 0, 0:N], st[:, 0, 0:N], ALU.mult, [st], [st])
            k.stt(st[:, 1, 0:N], p2[:, 0:N], 1.0 / 64, st[:, 1, 0:N], ALU.mult, ALU.subtract, [p2, st], [st])
            k.act(st[:, 1, 0:N], st[:, 1, 0:N], AF.Sqrt, [st], [st], bias=GN_EPS)
            k.recip(st[:, 1, 0:N], st[:, 1, 0:N], [st], [st])
            k.tt(st[:, 2, 0:N], oT[:, 4 + c, 0:N], st[:, 0, 0:N], ALU.subtract, [oA[4 + c], st], [st])
            k.tt(st[:, 2, 0:N], st[:, 2, 0:N], st[:, 1, 0:N], ALU.mult, [st], [st])
            k.ts(st[:, 2, 0:N], st[:, 2, 0:N], P["lg"][:, c:c + 1], P["lb"][:, c:c + 1], ALU.mult, ALU.add, [st, P["lg"], P["lb"]], [st])
            k.stt(sq[:, 0:N], zm[:, c, 0:N], P["rk"][:, c:c + 1], r[:, 30 + c, 0:N], ALU.mult, ALU.mult,
                  [zmA[c], P["rk"], rA[30 + c]], [sq])
            p3 = k.psum()
            k.mm(p3, p3[:, 0:N], bo64, sq[:, 0:N], True, [sq] + RC)
            k.tt(st[:, 3, 0:N], p3[:, 0:N], zm[:, 8 + c, 0:N], ALU.mult, [p3, zmA[8 + c]], [st])
            k.tt(st[:, 2, 0:N], st[:, 2, 0:N], st[:, 3, 0:N], ALU.add, [st], [st])
            k.tt(mixT[:, 4 + c, 0:N].bitcast(F32R), st[:, 2, 0:N], r[:, 22 + c, 0:N], ALU.mult, [st, rA[22 + c]], [mixT])

    def c_mixer(i, tile, xT, mixT):
        kind, nseq, Ls, N = tile["kind"], tile["nseq"], tile["Ls"], tile["N"]
        P = PC[i]
        Ne = nseq * (Ls + 3)
        z = k.rot("z", [128, 33, NMAX + 4], 1)
        zc = tile["zc"]
        ze = lambda ci: z.t[:, ci, 0:Ne].rearrange("p (s t) -> p s t", t=Ls + 3)
        if kind == "p":
            k.copy(z.t[:, 0:24, 0:3], C_gconv[i][:, :, :], [C_gconv[i]], zc[0:24], eng="pool")
        else:
            for c4 in range(0, 24, 4):
                if c4 % 8 == 0:
                    cb = k.rot("stg", [128, 1024], 1)
                    k.dma("pool", cb[0:nseq * 3, :], st_gconv[i].rearrange("s r c -> (s r) c")[:, c4 * 128:c4 * 128 + 1024], [], [cb])
                ps = k.psum()
                for j in range(4):
                    k.tr(ps, ps[:, j * 48:j * 48 + nseq * 3], cb[0:nseq * 3, (c4 % 8 + j) * 128:(c4 % 8 + j + 1) * 128],
                         ident[0:nseq * 3, 0:nseq * 3], [cb] + RC)
                for j in range(4):
                    k.copy(ze(c4 + j)[:, :, 0:3], ps[:, j * 48:j * 48 + nseq * 3].rearrange("p (s r) -> p s r", r=3), [ps], [zc[c4 + j]])

        def v3(ap):
            return ap.rearrange("p (s t) -> p s t", t=Ls)

        def cons(c_abs, cw, ps):
            ci = c_abs // 128
            if ci < 24:
                k.copy(ze(ci)[:, :, 3:Ls + 3], v3(ps[:, 0:N]), [ps], [zc[ci]], eng="act")
            else:
                k.act(z[:, ci, 0:N], ps[:, 0:N], AF.Silu, [ps], [zc[ci]])

        proj(w_in_c[i], 8, 0, 4096, lambda kk: xT[:, kk, 0:N], [xT], N, cons, 256)
        if kind == "p":
            k.copy(C_gconv[i][:, :, :], z.t[:, 0:24, Ls:Ls + 3], zc[0:24], [C_gconv[i]], eng="pool")
            if tile["last"]:
                store_rows(lambda c: C_gconv[i][:, c, :], [C_gconv[i]], 24, 3, o_gconv[0][i, 0])
        else:
            store_rows(lambda c: ze(c)[:, :, Ls:Ls + 3], zc[0:24], 24, nseq * 3, o_gconv[1][i].rearrange("s r c -> (s r) c"), view=(nseq, 3))
        q = ARENA
        qA = ARENA_T
        for ci in range(24):
            e = ze(ci)
            tmp = k.rot("ctmp", [128, NMAX], 2)
            k.ts(v3(tmp[:, 0:N]), e[:, :, 0:Ls], P["cw"][:, 0, ci:ci + 1], None, ALU.mult, None, [zc[ci], P["cw"]], [tmp])
            for j in (1, 2, 3):
                k.stt(v3(tmp[:, 0:N]), e[:, :, j:j + Ls], P["cw"][:, j, ci:ci + 1], v3(tmp[:, 0:N]), ALU.mult, ALU.add,
                      [zc[ci], P["cw"], tmp], [tmp])
            k.act(q[:, ci, 0:N], tmp[:, 0:N], AF.Silu, [tmp], [qA[ci]])
        for ci in range(16):
            sq = k.rot("pc_sq", [128, NMAX], 2)
            k.act(sq[:, 0:N], q[:, ci, 0:N], AF.Square, [qA[ci]], [sq])
            ps = k.psum()
            k.mm(ps, ps[:, 0:N], ones, sq[:, 0:N], True, [sq] + RC)
            k.act(sq[:, 0:N], ps[:, 0:N], AF.Sqrt, [ps], [sq], bias=NORM_EPS)
            k.recip(sq[:, 0:N], sq[:, 0:N], [sq], [sq])
            if ci < 8:
                k.stt(q[:, ci, 0:N], q[:, ci, 0:N], 128.0 ** -0.5, sq[:, 0:N], ALU.mult, ALU.mult, [qA[ci], sq], [qA[ci]])
            else:
                k.tt(q[:, ci, 0:N], q[:, ci, 0:N], sq[:, 0:N], ALU.mult, [qA[ci], sq], [qA[ci]])
        GBT = GL_T[10:16]
        pb_ = k.psum()
        for kk in range(8):
            k.mm(pb_, pb_[0:8, 0:N], P["wba"][:, kk, 0:8], xT[:, kk, 0:N], kk == 7, [P["wba"], xT])
        k.act(GL[0:8, 11, 0:N], pb_[0:8, 0:N], AF.Sigmoid, [pb_], GBT)
        pa_ = k.psum()
        for kk in range(8):
            k.mm(pa_, pa_[0:8, 0:N], P["wba"][:, kk, 8:16], xT[:, kk, 0:N], kk == 7, [P["wba"], xT])
        k.act(GL[0:8, 15, 0:N], pa_[0:8, 0:N], AF.Exp, [pa_, P["dtb"]], GBT, bias=P["dtb"][:, 0:1])
        k.act(GL[0:8, 15, 0:N], GL[0:8, 15, 0:N], AF.Ln, GBT, GBT, bias=1.0)
        k.ts(GL[0:8, 14, 0:N], GL[0:8, 15, 0:N], P["nexpA"][:, 0:1], None, ALU.mult, None, GBT + [P["nexpA"]], GBT)
        rst = ones[0:8, 0:N] if kind == "p" else cst[0:8, C_RST:C_RST + N]
        k.scan(GL[0:8, 10, 0:N], rst, GL[0:8, 14, 0:N], GBT + RC, GBT)
        k.act(GL[0:8, 13, 0:N], GL[0:8, 10, 0:N], AF.Exp, GBT, GBT)
        k.ts(GL[0:8, 13, 0:N], GL[0:8, 13, 0:N], -1.0, None, ALU.mult, None, GBT, GBT)

        oT = k.rot("oT", [128, 8, NMAX], 1)
        oA = k.alias(oT, 8)
        zm_ = k.rot("zm", [128, 14, NMAX], 1)
        ar_ = k.rot("ar", [128, 4, 2, NMAX], 1)
        bk_ = k.rot("bk", [128, 4, 2, NMAX], 1)
        rn0 = k.rot("rnb", [128, 5, 128], 4)
        rn1 = k.rot("rnb", [128, 5, 128], 4)
        rke = k.rot("rket", [128, 4, 2, 128], 1)
        zmT, arT, bkT = k.alias(zm_, 14), k.alias(ar_, 4), k.alias(bk_, 4)
        LANES = []
        for lane in range(4):
            if lane == 0:
                w6 = slots3(zm_, zmT, range(0, 6)); x5 = slots3(rn0, rn0, range(5))
            elif lane == 1:
                w6 = slots3(zm_, zmT, range(6, 12)); x5 = slots3(rn1, rn1, range(5))
            elif lane == 2:
                w6 = slots4(ar_, arT, range(0, 6)); x5 = slots4(rke, rke, range(5))
            else:
                w6 = slots4(bk_, bkT, range(0, 6))
                x5 = slots3(zm_, zmT, range(12, 14)) + slots4(ar_, arT, range(6, 8)) + slots4(bk_, bkT, range(6, 7))
            sv = slots3(ARENA, ARENA_T, range(24 + 6 * lane, 30 + 6 * lane))
            LANES.append(w6 + x5 + sv)
        for s in range(nseq):
            c0 = s * Ls
            L = Ls
            cols = slice(c0, c0 + L)
            last = c0 + L - 1
            if kind == "p":
                Ms = P_gdn[i]
                MsT = [Ms]
            else:
                Ms = GL
                MsT = GL_T[0:8]
                k.dma("sp", Ms[:, 0:8, :], st_gdn[i, s].rearrange("h kk v -> kk h v"), [], MsT)
            k.ts(GL[0:8, 12, cols], GL[0:8, 10, cols], GL[0:8, 10, last:last + 1], None, ALU.subtract, None, GBT, GBT)
            k.act(GL[0:8, 12, cols], GL[0:8, 12, cols], AF.Exp, GBT, GBT, scale=-1.0)
            ps = k.psum()
            for j in range(4):
                k.tr(ps, ps[0:L, j * 8:(j + 1) * 8], GL[0:8, 10 + j, cols], ident[0:8, 0:8], GBT + RC)
            tm = k.rot("gtm", [128, 4, 8], 2)
            k.copy(tm[0:L, :, :], ps[0:L, 0:32].rearrange("p (a h) -> p a h", h=8), [ps], [tm])
            sl_ = slice(0, L)
            for g4 in (0, 4):
                st1 = []
                for lane in range(4):
                    h = g4 + lane
                    LS = LANES[lane]
                    W = LS[0:6]
                    QG, VV, KE, RR, DL = LS[6:11]
                    gm = k.rot("ggm", [8, 2, NMAX], 2)
                    k.ts(gm[0:8, :, 0:L], GL[0:8, 10:12, cols], ident[0:8, h:h + 1], None, ALU.mult, None, GBT + RC, [gm])
                    pg = k.psum()
                    k.mm(pg, pg[:, 0:L], ones[0:8, :], gm[0:8, 0, 0:L], False, [gm] + RC)
                    k.mm(pg, pg[:, 128:128 + L], ones[0:8, :], gm[0:8, 1, 0:L], True, [gm] + RC)
                    pq = k.psum()
                    k.mm(pq, pq[0:L, 0:L], q[:, 8 + h, cols], q[:, h, cols], False, [qA[8 + h], qA[h]])
                    k.mm(pq, pq[0:L, 128:128 + L], q[:, 8 + h, cols], q[:, 8 + h, cols], True, [qA[8 + h]])
                    gam_p = tm[0:L, 0, h:h + 1]
                    k.ts(W[0](sl_, sl_), pg[0:L, 0:L], gam_p, 0.0, ALU.subtract, ALU.min, [pg, tm], [W[0].tok])
                    k.act(W[0](sl_, sl_), W[0](sl_, sl_), AF.Exp, [W[0].tok], [W[0].tok])
                    k.ts(W[1](sl_, sl_), pg[0:L, 0:L], gam_p, 0.0, ALU.subtract, ALU.max, [pg, tm], [W[1].tok])
                    k.act(W[1](sl_, sl_), W[1](sl_, sl_), AF.Exp, [W[1].tok], [W[1].tok], scale=-1.0)
                    k.tt(W[4](sl_, sl_), W[0](sl_, sl_), cm(C_IU, L, L), ALU.mult, [W[0].tok] + RC, [W[4].tok])
                    k.tt(W[4](sl_, sl_), W[4](sl_, sl_), pq[0:L, 0:L], ALU.mult, [W[4].tok, pq], [W[4].tok])
                    k.tt(W[2](sl_, sl_), W[0](sl_, sl_), pg[0:L, 128:128 + L], ALU.mult, [W[0].tok, pg], [W[2].tok])
                    k.tt(W[2](sl_, sl_), W[2](sl_, sl_), cm(C_NSU, L, L), ALU.mult, [W[2].tok] + RC, [W[2].tok])
                    k.tt(W[2](sl_, sl_), W[2](sl_, sl_), pq[0:L, 128:128 + L], ALU.mult, [W[2].tok, pq], [W[2].tok])
                    k.stt(W[3](sl_, sl_), W[1](sl_, sl_), tm[0:L, 1, h:h + 1], cm(C_NSL, L, L), ALU.mult, ALU.mult,
                          [W[1].tok, tm] + RC, [W[3].tok])
                    k.tt(W[3](sl_, sl_), W[3](sl_, sl_), pq[0:L, 128:128 + L], ALU.mult, [W[3].tok, pq], [W[3].tok])
                    fa = slice(0, 128)
                    k.act(W[5](fa, sl_), pg[:, 0:L], AF.Exp, [pg], [W[5].tok])
                    k.tt(QG(fa, sl_), q[:, h, cols], W[5](fa, sl_), ALU.mult, [qA[h], W[5].tok], [QG.tok])
                    pt = k.psum()
                    k.tr(pt, pt[0:L, 0:128], q[:, 16 + h, cols], ident, [qA[16 + h]] + RC)
                    k.tr(pt, pt[0:L, 128:256], q[:, 8 + h, cols], ident, [qA[8 + h]] + RC)
                    k.copy(VV(sl_, fa), pt[0:L, 0:128], [pt], [VV.tok], eng="act")
                    k.ts(KE(sl_, fa), pt[0:L, 128:256], tm[0:L, 2, h:h + 1], None, ALU.mult, None, [pt, tm], [KE.tok])
                    pk = k.psum()
                    k.mm(pk, pk[0:L, 0:128], q[:, 8 + h, cols], Ms[:, h, :], True, [qA[8 + h]] + MsT)
                    k.stt(RR(sl_, fa), pk[0:L, 0:128], tm[0:L, 3, h:h + 1], VV(sl_, fa), ALU.mult, ALU.add, [pk, tm, VV.tok], [RR.tok])
                    k.ts(RR(sl_, fa), RR(sl_, fa), tm[0:L, 1, h:h + 1], None, ALU.mult, None, [RR.tok, tm], [RR.tok])
                    st1.append(dict(Pp=W[2](sl_, sl_), Pm=W[3](sl_, sl_), ptoks=[W[2].tok, W[3].tok], R0=RR(sl_, fa), rtoks=[RR.tok],
                                    L=L, w=128, out=DL(sl_, fa), otoks=[DL.tok], PA=LS[11:13], PB=LS[13:15], RA=LS[15], RB=LS[16]))
                if SEQ_GDN:
                    for jb_ in st1:
                        tri_solve_multi([jb_], L)
                else:
                    tri_solve_multi(st1, L)
                for lane in range(4):
                    h = g4 + lane
                    LS = LANES[lane]
                    W = LS[0:6]
                    QG, VV, KE, RR, DL = LS[6:11]
                    fa = slice(0, 128)
                    po = k.psum()
                    k.mm(po, po[:, 0:L], Ms[:, h, :], QG(fa, sl_), False, MsT + [QG.tok])
                    k.mm(po, po[:, 0:L], DL(sl_, fa), W[4](sl_, sl_), True, [DL.tok, W[4].tok])
                    k.copy(oT[:, h, cols], po[:, 0:L], [po], [oA[h]], eng="act")
                    pu = k.psum()
                    k.mm(pu, pu[:, 0:128], KE(sl_, fa), DL(sl_, fa), True, [KE.tok, DL.tok])
                    k.stt(Ms[:, h, :], Ms[:, h, :], W[5](fa, slice(L - 1, L)), pu[:, 0:128], ALU.mult, ALU.add, MsT + [W[5].tok, pu], MsT)
            if kind == "s" or tile["last"]:
                gi = 0 if kind == "p" else 1
                si = 0 if kind == "p" else s
                k.dma("pool", o_gdn[gi][i, si].rearrange("h kk v -> kk h v"), Ms[:, 0:8, :], MsT, [])
        for h in range(8):
            sq = k.rot("pc_sq", [128, NMAX], 2)
            k.act(sq[:, 0:N], oT[:, h, 0:N], AF.Square, [oA[h]], [sq])
            ps = k.psum()
            k.mm(ps, ps[:, 0:N], ones, sq[:, 0:N], True, [sq] + RC)
            k.act(sq[:, 0:N], ps[:, 0:N], AF.Sqrt, [ps], [sq], bias=NORM_EPS, scale=1.0 / 128)
            k.recip(sq[:, 0:N], sq[:, 0:N], [sq], [sq])
            k.tt(sq[:, 0:N], sq[:, 0:N], oT[:, h, 0:N], ALU.mult, [sq, oA[h]], [sq])
            k.stt(mixT[:, h, 0:N].bitcast(F32R), sq[:, 0:N], P["ng"][:, 0:1], z[:, 24 + h, 0:N], ALU.mult, ALU.mult, [sq, P["ng"], zc[24 + h]], [mixT])

    def ffn(l, tile, xT, tb):
        kind, nseq, Ls, N = tile["kind"], tile["nseq"], tile["Ls"], tile["N"]
        P = PL[l]
        Ne = nseq * (Ls + 2)
        ge = k.rot("z", [128, 33, NMAX + 4], 1)
        zc = tile["zc"]
        gev = lambda ci: ge.t[:, ci, 0:Ne].rearrange("p (s t) -> p s t", t=Ls + 2)
        if kind == "p":
            k.copy(ge.t[:, 0:22, 0:2], C_fconv[l][:, :, :], [C_fconv[l]], zc[0:22], eng="pool")
        else:
            for c4 in range(0, 22, 4):
                n4 = min(4, 22 - c4)
                if c4 % 8 == 0:
                    wcb = min(1024, D_FF - c4 * 128)
                    cb = k.rot("stg", [128, 1024], 1)
                    k.dma("pool", cb[0:nseq * 2, 0:wcb], st_fconv[l].rearrange("s r c -> (s r) c")[:, c4 * 128:c4 * 128 + wcb], [], [cb])
                ps = k.psum()
                for j in range(n4):
                    k.tr(ps, ps[:, j * 32:j * 32 + nseq * 2], cb[0:nseq * 2, (c4 % 8 + j) * 128:(c4 % 8 + j + 1) * 128],
                         ident[0:nseq * 2, 0:nseq * 2], [cb] + RC)
                for j in range(n4):
                    k.copy(gev(c4 + j)[:, :, 0:2], ps[:, j * 32:j * 32 + nseq * 2].rearrange("p (s r) -> p s r", r=2), [ps], [zc[c4 + j]])
        aT = AT
        aA = AT_T

        def v3(ap):
            return ap.rearrange("p (s t) -> p s t", t=Ls)

        def cons(c_abs, cw, ps):
            ci = c_abs // 128
            if ci < 22:
                k.copy(gev(ci)[:, :, 2:Ls + 2], v3(ps[:, 0:N]), [ps], [zc[ci]], eng="act")
                e = gev(ci)
                tmp = k.rot("ctmp", [128, NMAX], 2)
                k.ts(v3(tmp[:, 0:N]), e[:, :, 0:Ls], P["fcw"][:, 0, ci:ci + 1], None, ALU.mult, None, [zc[ci], P["fcw"]], [tmp])
                for j in (1, 2):
                    k.stt(v3(tmp[:, 0:N]), e[:, :, j:j + Ls], P["fcw"][:, j, ci:ci + 1], v3(tmp[:, 0:N]), ALU.mult, ALU.add,
                          [zc[ci], P["fcw"], tmp], [tmp])
                k.act(AT[:, ci, 0:N].bitcast(F32R), tmp[:, 0:N], AF.Silu, [tmp, P["fcb"]], [aA[ci]], bias=P["fcb"][:, ci:ci + 1])
            else:
                j = ci - 22
                k.tt(AT[:, j, 0:N].bitcast(F32R), AT[:, j, 0:N], ps[:, 0:N], ALU.mult, [aA[j], ps], [aA[j]])

        proj(w_up[l], 8, 0, 2 * D_FF, lambda kk: xT[:, kk, 0:N], [xT], N, cons, 256)
        if kind == "p":
            k.copy(C_fconv[l][:, :, :], ge.t[:, 0:22, Ls:Ls + 2], zc[0:22], [C_fconv[l]], eng="pool")
            if tile["last"]:
                store_rows(lambda c: C_fconv[l][:, c, :], [C_fconv[l]], 22, 2, o_fconv[0][l, 0])
        else:
            store_rows(lambda c: gev(c)[:, :, Ls:Ls + 2], zc[0:22], 22, nseq * 2, o_fconv[1][l].rearrange("s r c -> (s r) c"), view=(nseq, 2))

        def cons2(c_abs, cw, ps):
            m = c_abs // 128
            k.stt(tb[:, m, 0:N], xT[:, m, 0:N], ALPHA, ps[:, 0:N], ALU.mult, ALU.add, [xT, ps], [tb])

        proj(w_down[l], 22, 0, D, lambda kk: AT[:, kk, 0:N], aA, N, cons2, 128)

    tiles = [dict(kind="p", nseq=1, Ls=N_META, N=N_META, r0=0)]
    r0 = N_META
    while r0 < TP:
        tiles.append(dict(kind="p", nseq=1, Ls=128, N=128, r0=r0))
        r0 += 128
    for t in tiles:
        t["last"] = False
    tiles[-1]["last"] = True
    tiles.append(dict(kind="s", nseq=NS, Ls=4, N=NS * 4, r0=0, last=True))
    zbuf0 = None
    for tile in tiles:
        N = tile["N"]
        tile["first_rw"] = False
        src = xp if tile["kind"] == "p" else xs
        xtm = k.rot("stg", [128, 1024], 1)
        k.dma("sp", xtm[0:N, 0:D], src[tile["r0"]:tile["r0"] + N, :], [], [xtm])
        xT = k.rot("xT", [128, 8, NMAX], 2)
        for c4 in (0, 4):
            ps = k.psum()
            for j in range(4):
                k.tr(ps, ps[:, j * 128:j * 128 + N], xtm[0:N, (c4 + j) * 128:(c4 + j + 1) * 128], ident[0:N, 0:N], [xtm] + RC)
            k.copy(xT[:, c4:c4 + 4, 0:N].bitcast(F32R), ps[:, 0:512].rearrange("p (c t) -> p c t", t=128)[:, :, 0:N], [ps], [xT])
        zb = k.rot("z", [128, 33, NMAX + 4], 1)
        if zbuf0 is None:
            zbuf0 = k.alias(zb, 33)
        tile["zc"] = zbuf0
        for l in range(DEPTH):
            i = l // 2
            mixT = k.rot("mixT", [128, 8, NMAX], 1)
            if l % 2 == 0:
                ab_mixer(i, tile, xT, mixT)
                Wo = w_out_ab[i]
            else:
                c_mixer(i, tile, xT, mixT)
                Wo = w_out_c[i]
            tb = k.rot("tb", [128, 8, NMAX], 1)

            def cons_o(c_abs, cw, ps, xT=xT, tb=tb):
                m = c_abs // 128
                k.stt(tb[:, m, 0:N], xT[:, m, 0:N], ALPHA, ps[:, 0:N], ALU.mult, ALU.add, [xT, ps], [tb])

            proj(Wo, 8, 0, D, lambda kk: mixT[:, kk, 0:N], [mixT], N, cons_o, 256)
            x2 = k.rot("xT", [128, 8, NMAX], 2)
            layer_norm(tb, N, PL[l]["lmg"], PL[l]["lmb"], x2)
            tb2 = k.rot("tb", [128, 8, NMAX], 1)
            ffn(l, tile, x2, tb2)
            x3 = k.rot("xT", [128, 8, NMAX], 2)
            layer_norm(tb2, N, PL[l]["lfg"], PL[l]["lfb"], x3)
            xT = x3
        if tile["kind"] == "s":
            store_rows(lambda c: xT[:, c, 0:N], [xT], 8, N, y_s[:, :])
        elif tile["r0"] >= N_META:
            store_rows(lambda c: xT[:, c, 0:N], [xT], 8, N, y_p[tile["r0"] - N_META:tile["r0"] - N_META + N, :])
    k.finish()
    stats = (k.n_inst, k.n_wait)
    es.close()
    return nc, stats


_W_NAMES = ["w_in_ab", "gla_gate_w2", "gla_gate_b", "gla_norm_g", "rwkv_mu", "rwkv_w0", "rwkv_w2", "rwkv_a0", "rwkv_a2",
            "rwkv_g2", "rwkv_k_k", "rwkv_k_a", "rwkv_r_k", "rwkv_ln_g", "rwkv_ln_b", "w_out_ab", "w_in_c", "gdn_conv_w",
            "gdn_A_log", "gdn_dt_bias", "gdn_norm_g", "w_out_c", "w_up", "ffn_conv_w", "ffn_conv_b", "w_down",
            "ln_mix_g", "ln_mix_b", "ln_ffn_g", "ln_ffn_b"]


def make_in_maps(inputs, n_cores, depth):
    f = lambda a: np.ascontiguousarray(np.asarray(a, dtype=np.float32))
    xprompt = f(inputs["x_prompt"])
    B, SEQ, _ = xprompt.shape
    xsample = f(inputs["x_sample"])
    NS = xsample.shape[0] // n_cores
    meta = f(inputs["meta_tokens"])
    consts = make_consts()
    shared = {"consts": consts}
    for nm in _W_NAMES:
        a = f(inputs[nm])
        if nm == "rwkv_r_k":
            a = a.reshape(a.shape[0], 512)
        shared[nm] = a
    maps = []
    for c in range(n_cores):
        m = dict(shared)
        m["xp"] = np.ascontiguousarray(np.concatenate([meta, xprompt[c]], axis=0))
        sl = slice(c * NS, (c + 1) * NS)
        m["xs"] = np.ascontiguousarray(xsample[sl].reshape(NS * 4, D))
        m["st_gla"] = f(inputs["state_gla"][:, sl])
        m["st_rwkv"] = f(inputs["state_rwkv"][:, sl])
        m["st_shift"] = f(inputs["state_rwkv_shift"][:, sl])
        m["st_gdn"] = f(inputs["state_gdn"][:, sl])
        m["st_gconv"] = f(inputs["state_gdn_conv"][:, sl])
        m["st_fconv"] = f(inputs["state_ffn_conv"][:, sl])
        maps.append(m)
    return maps, SEQ, NS


def gather(results, n_cores, NS):
    cat = lambda nm, ax: np.concatenate([np.asarray(r[nm]) for r in results], axis=ax)
    y_p = np.stack([np.asarray(r["y_p"]) for r in results], axis=0)
    y_s = cat("y_s", 0).reshape(n_cores * NS, 4, D)
    outs = [y_p, y_s]
    for g in ("p", "s"):
        for nm in ("gla", "rwkv", "shift", "gdn", "gconv", "fconv"):
            outs.append(cat("%s_%s" % (g, nm), 1))
    return tuple(np.ascontiguousarray(o, dtype=np.float32) for o in outs)


def kernel(**inputs):
    n_cores = 8
    maps, SEQ, NS = make_in_maps(inputs, n_cores, DEPTH_FULL)
    nc, _ = build(SEQ, NS, DEPTH_FULL)
    res = run_bass_kernel_spmd(nc, maps, core_ids=list(range(n_cores)))
    return gather(res.results, n_cores, NS)
```
